# Optimizing a Trainium2 kernel written in Bass

```python
import math
import jax, jax.numpy as jnp
from jax import lax
import numpy as np

D_MODEL = 1024
BATCH = 2
SEQ = 16384
DEPTH = 4

CHUNK = 128
N_SGU_GROUPS = 8
SGU_WIDTH = D_MODEL
SGU_GROUP = SGU_WIDTH // N_SGU_GROUPS
DIFF_HEAD_DIM = 64
N_DIFF_HEADS = D_MODEL // (2 * DIFF_HEAD_DIM)
DIFF_QK_WIDTH = N_DIFF_HEADS * 2 * DIFF_HEAD_DIM
DIFF_V_WIDTH = N_DIFF_HEADS * 2 * DIFF_HEAD_DIM
Q_BLOCK = 128
ROPE_THETA = 10000.0
N_BRANCH = 2
IN_WIDTH = 2 * SGU_WIDTH + 2 * DIFF_QK_WIDTH + DIFF_V_WIDTH + N_BRANCH * D_MODEL
SPLITS = [SGU_WIDTH, 2 * SGU_WIDTH, 2 * SGU_WIDTH + DIFF_QK_WIDTH,
          2 * SGU_WIDTH + 2 * DIFF_QK_WIDTH,
          2 * SGU_WIDTH + 2 * DIFF_QK_WIDTH + DIFF_V_WIDTH]
D_FF = 2816
ALPHA = (2.0 * DEPTH) ** 0.25
BETA = (8.0 * DEPTH) ** -0.25
LN_EPS = 1e-5

kernel_name = "hybrid_sgu_diffattn_macaron_deepnorm"


def _lambda_init(layer):
    return 0.8 - 0.6 * math.exp(-0.3 * layer)


def _layernorm(x, g, b):
    xf = x.astype(jnp.float32)
    mu = jnp.mean(xf, axis=-1, keepdims=True)
    var = jnp.mean(jnp.square(xf - mu), axis=-1, keepdims=True)
    y = (xf - mu) * lax.rsqrt(var + LN_EPS)
    return (y * g.astype(jnp.float32) + b.astype(jnp.float32)).astype(x.dtype)


def _rmsnorm(x, g):
    xf = x.astype(jnp.float32)
    y = xf * lax.rsqrt(jnp.mean(jnp.square(xf), axis=-1, keepdims=True) + LN_EPS)
    return (y * g.astype(jnp.float32)).astype(x.dtype)


def _swiglu(x, w1, w3, w2):
    return (jax.nn.silu(x @ w1) * (x @ w3)) @ w2


def _rope(x, cos, sin):
    c = cos[None, :, None, None, :]
    s = sin[None, :, None, None, :]
    xf = x.astype(jnp.float32)
    x1, x2 = jnp.split(xf, 2, axis=-1)
    out = jnp.concatenate([x1 * c - x2 * s, x2 * c + x1 * s], axis=-1)
    return out.astype(x.dtype)


def _chunked_sgu(u, v, g, b, w_s, b_s):
    bsz, seq, _ = v.shape
    v = _layernorm(v, g, b)
    n_chunks = seq // CHUNK
    vb = v.reshape(bsz, n_chunks, CHUNK, N_SGU_GROUPS, SGU_GROUP)
    causal = jnp.tril(jnp.ones((CHUNK, CHUNK), dtype=bool))
    w = jnp.where(causal[None], w_s, jnp.zeros((), w_s.dtype))
    s = jnp.einsum('gtr,bnrgc->bntgc', w, vb) + b_s.T[None, None, :, :, None]
    return u * s.reshape(bsz, seq, SGU_WIDTH).astype(u.dtype)


def _diff_attention(q, k, v, lam):
    bsz, seq = q.shape[:2]
    n_blocks = seq // Q_BLOCK
    qb = q.reshape(bsz, n_blocks, Q_BLOCK, N_DIFF_HEADS, 2, DIFF_HEAD_DIM).swapaxes(0, 1)
    k_pos = jnp.arange(seq)
    scale = DIFF_HEAD_DIM ** -0.5
    neg = jnp.finfo(jnp.float32).min

    def one_block(args):
        q_blk, i = args
        s = jnp.einsum('bqhmd,bkhmd->bhmqk', q_blk, k).astype(jnp.float32) * scale
        q_pos = i * Q_BLOCK + jnp.arange(Q_BLOCK)
        mask = k_pos[None, :] <= q_pos[:, None]
        s = jnp.where(mask, s, neg)
        p = jax.nn.softmax(s, axis=-1)
        a = p[:, :, 0] - lam * p[:, :, 1]
        return jnp.einsum('bhqk,bkhe->bqhe', a.astype(v.dtype), v)

    out = lax.map(one_block, (qb, jnp.arange(n_blocks)))
    return out.swapaxes(0, 1).reshape(bsz, seq, N_DIFF_HEADS, 2 * DIFF_HEAD_DIM)


def _mixer(x, layer, w_in, gate_b, sgu_ln_g, sgu_ln_b, sgu_w, sgu_b, lam, diff_ln_g,
           w_branch, w_out, cos, sin):
    bsz, seq, _ = x.shape
    z = x @ w_in
    u, v, q, k, val, gates = jnp.split(z, SPLITS, axis=-1)
    u = jax.nn.gelu(u, approximate=False)
    v = jax.nn.gelu(v, approximate=False)
    h_a = _chunked_sgu(u, v, sgu_ln_g, sgu_ln_b, sgu_w, sgu_b)
    q = _rope(q.reshape(bsz, seq, N_DIFF_HEADS, 2, DIFF_HEAD_DIM), cos, sin)
    k = _rope(k.reshape(bsz, seq, N_DIFF_HEADS, 2, DIFF_HEAD_DIM), cos, sin)
    val = val.reshape(bsz, seq, N_DIFF_HEADS, 2 * DIFF_HEAD_DIM)
    lam_init = _lambda_init(layer)
    lf = lam.astype(jnp.float32)
    lam_full = (jnp.exp(jnp.sum(lf[0] * lf[1])) - jnp.exp(jnp.sum(lf[2] * lf[3]))
                + lam_init)
    o = _diff_attention(q, k, val, lam_full)
    o = _rmsnorm(o, diff_ln_g) * (1.0 - lam_init)
    h_b = o.reshape(bsz, seq, DIFF_V_WIDTH)
    g = jax.nn.sigmoid(gates.reshape(bsz, seq, N_BRANCH, D_MODEL) + gate_b)
    m = g[:, :, 0] * (h_a @ w_branch[0]) + g[:, :, 1] * (h_b @ w_branch[1])
    return m @ w_out


def setup_inputs(seed: int = 0) -> dict:
    key = jax.random.key(seed)
    ks = jax.random.split(key, 20)
    f32 = jnp.float32
    nrm = lambda k, shape, s: jax.random.normal(k, shape, f32) * s
    x = jax.random.normal(ks[0], (BATCH, SEQ, D_MODEL), f32)
    w_in = nrm(ks[1], (DEPTH, D_MODEL, IN_WIDTH), D_MODEL ** -0.5)
    gate_b = nrm(ks[2], (DEPTH, N_BRANCH, D_MODEL), 0.02)
    sgu_ln_g = 1.0 + nrm(ks[3], (DEPTH, SGU_WIDTH), 0.02)
    sgu_ln_b = nrm(ks[4], (DEPTH, SGU_WIDTH), 0.02)
    sgu_w = nrm(ks[5], (DEPTH, N_SGU_GROUPS, CHUNK, CHUNK), CHUNK ** -0.5)
    sgu_b = 1.0 + nrm(ks[6], (DEPTH, N_SGU_GROUPS, CHUNK), 0.02)
    lam = nrm(ks[7], (DEPTH, 4, DIFF_HEAD_DIM), 0.1)
    diff_ln_g = 1.0 + nrm(ks[8], (DEPTH, 2 * DIFF_HEAD_DIM), 0.02)
    w_branch = jnp.concatenate([
        nrm(ks[9], (DEPTH, 1, SGU_WIDTH, D_MODEL), BETA * SGU_WIDTH ** -0.5),
        nrm(ks[10], (DEPTH, 1, DIFF_V_WIDTH, D_MODEL), BETA * DIFF_V_WIDTH ** -0.5)], axis=1)
    w_out = nrm(ks[11], (DEPTH, D_MODEL, D_MODEL), BETA * D_MODEL ** -0.5)
    ffn_w1 = nrm(ks[12], (DEPTH, 2, D_MODEL, D_FF), D_MODEL ** -0.5)
    ffn_w3 = nrm(ks[13], (DEPTH, 2, D_MODEL, D_FF), D_MODEL ** -0.5)
    ffn_w2 = nrm(ks[14], (DEPTH, 2, D_FF, D_MODEL), BETA * D_FF ** -0.5)
    ln_g = 1.0 + nrm(ks[15], (DEPTH, 3, D_MODEL), 0.02)
    ln_b = nrm(ks[16], (DEPTH, 3, D_MODEL), 0.02)
    return {"x": x, "w_in": w_in, "gate_b": gate_b, "sgu_ln_g": sgu_ln_g,
            "sgu_ln_b": sgu_ln_b, "sgu_w": sgu_w, "sgu_b": sgu_b, "lam": lam,
            "diff_ln_g": diff_ln_g, "w_branch": w_branch, "w_out": w_out,
            "ffn_w1": ffn_w1, "ffn_w3": ffn_w3, "ffn_w2": ffn_w2,
            "ln_g": ln_g, "ln_b": ln_b}


def reference(x, w_in, gate_b, sgu_ln_g, sgu_ln_b, sgu_w, sgu_b, lam, diff_ln_g,
              w_branch, w_out, ffn_w1, ffn_w3, ffn_w2, ln_g, ln_b):
    seq = x.shape[1]
    pos = jnp.arange(seq, dtype=jnp.float32)
    inv_freq = ROPE_THETA ** (-jnp.arange(0, DIFF_HEAD_DIM, 2, dtype=jnp.float32) / DIFF_HEAD_DIM)
    ang = pos[:, None] * inv_freq[None, :]
    cos, sin = jnp.cos(ang), jnp.sin(ang)
    for l in range(DEPTH):
        h = _swiglu(x, ffn_w1[l, 0], ffn_w3[l, 0], ffn_w2[l, 0])
        x = _layernorm(ALPHA * x + 0.5 * h, ln_g[l, 0], ln_b[l, 0])
        h = _mixer(x, l, w_in[l], gate_b[l], sgu_ln_g[l], sgu_ln_b[l], sgu_w[l], sgu_b[l],
                   lam[l], diff_ln_g[l], w_branch[l], w_out[l], cos, sin)
        x = _layernorm(ALPHA * x + h, ln_g[l, 1], ln_b[l, 1])
        h = _swiglu(x, ffn_w1[l, 1], ffn_w3[l, 1], ffn_w2[l, 1])
        x = _layernorm(ALPHA * x + 0.5 * h, ln_g[l, 2], ln_b[l, 2])
    return x
```

```python
import numpy as np
from contextlib import ExitStack
import concourse.bass as bass
import concourse.mybir as mybir
from concourse.bass_utils import run_bass_kernel_spmd

F32 = mybir.dt.float32
BF16 = mybir.dt.bfloat16
AF = mybir.ActivationFunctionType
ALU = mybir.AluOpType

D = 1024
DFF = 2816
NFC = 22
INW = 7168
DEPTH = 4
SEQ = 16384
BATCH = 2
ALPHA = (2.0 * DEPTH) ** 0.25
EPS = 1e-5
ENGS = ("sync", "scalar", "vector", "gpsimd", "tensor")
SAME_ENGINE_SYNC = True


class Sem:
    LIMIT = 30000

    def __init__(self, P, name):
        self.P, self.name, self.n = P, name, 0
        self.new()

    def new(self):
        self.h = self.P.stack.enter_context(self.P.nc.semaphore(f"{self.name}_{self.n}"))
        self.n += 1
        self.val = 0

    def inc(self, amt, eng):
        if self.val + amt > self.LIMIT:
            self.new()
        self.val += amt
        return (self.h, self.val, eng)


class Arena:
    def __init__(self):
        self.bufs = []


class Buf:
    def __init__(self, P, name, ap=None, arena=None, phase=0):
        self.P, self.name, self.ap = P, name, ap
        self.wset = {}
        self.readers = {}
        self.dsem = None
        self.arena, self.phase = arena, phase
        if arena is not None:
            arena.bufs.append(self)

    def sem(self):
        if self.dsem is None:
            self.dsem = Sem(self.P, "d_" + self.name)
        return self.dsem

    def add_reader(self, tk):
        k = id(tk[0])
        if k not in self.readers or self.readers[k][1] < tk[1]:
            self.readers[k] = tk

    def set_writer(self, tk):
        self.wset = {id(tk[0]): tk}
        self.readers = {}

    def add_writer(self, tk):
        k = id(tk[0])
        if k not in self.wset or self.wset[k][1] < tk[1]:
            self.wset[k] = tk

    def r_waits(self):
        return list(self.wset.values())

    def w_waits(self):
        w = list(self.readers.values()) + list(self.wset.values())
        if self.arena is not None:
            for o in self.arena.bufs:
                if o.phase != self.phase:
                    w += list(o.readers.values()) + list(o.wset.values())
        return w


class Prog:
    def __init__(self, nc):
        self.nc = nc
        self.stack = ExitStack()
        self.ops = {e: [] for e in ENGS}
        self.waited = {e: {} for e in ENGS}
        self.prog = {e: Sem(self, "p_" + e) for e in ENGS}
        self.nbuf = 0

    def _waits(self, eng, tks):
        for tk in tks:
            if tk is None:
                continue
            h, v, teng = tk
            if teng == eng and (eng == "tensor" or not SAME_ENGINE_SYNC):
                continue
            k = id(h)
            if self.waited[eng].get(k, 0) >= v:
                continue
            self.waited[eng][k] = v
            self.ops[eng].append(("w", h, v))

    def op(self, eng, fns, reads=(), writes=(), accum=()):
        if not isinstance(fns, (list, tuple)):
            fns = [fns]
        tks = []
        for b in reads:
            tks += b.r_waits()
        for b in accum:
            tks += b.r_waits()
        for b in writes:
            tks += b.w_waits()
        self._waits(eng, tks)
        tk = self.prog[eng].inc(1, eng)
        for f in fns[:-1]:
            self.ops[eng].append(("o", f, None, 0))
        self.ops[eng].append(("o", fns[-1], tk[0], 1))
        for b in reads:
            b.add_reader(tk)
        for b in list(writes) + list(accum):
            b.set_writer(tk)
        return tk

    def dma(self, eng, out, in_, reads=(), writes=(), nowait_dst=False, sem_of=None, **kw):
        tks = []
        for b in reads:
            tks += b.r_waits()
        for b in writes:
            if not nowait_dst:
                tks += b.w_waits()
        self._waits(eng, tks)
        wb = sem_of if sem_of is not None else writes[0]
        tk = wb.sem().inc(16, "dma")
        self.ops[eng].append(("o", lambda e: e.dma_start(out=out, in_=in_, **kw), tk[0], 16))
        for b in reads:
            b.add_reader(tk)
        for b in writes:
            if nowait_dst:
                b.add_writer(tk)
            else:
                b.set_writer(tk)
        return tk

    def wait(self, eng, tks):
        self._waits(eng, tks)

    def replay(self, eng, e):
        for o in self.ops[eng]:
            if o[0] == "w":
                e.wait_ge(o[1], o[2])
            else:
                ins = o[1](e)
                if o[2] is not None:
                    ins.then_inc(o[2], o[3])


class _Stop(Exception):
    pass


DEBUG_STAGE = None
DEBUG_DUMP = False
LAST_RES = [None]


def build_program(L, NT):
    T = NT * 512
    NB = NT * 4
    nc = bass.Bass("TRN2", target_bir_lowering=False)
    P = Prog(nc)
    st = P.stack

    def dram(name, shape, dtype, kind=None):
        if kind is None:
            return nc.dram_tensor(name, shape, dtype)
        return nc.dram_tensor(name, shape, dtype, kind=kind)

    def dap(h, off, pat):
        return bass.AP(h, off, [list(p) for p in pat])

    EI = "ExternalInput"
    x_in = dram("x", [T, D], F32, EI)
    out_d = dram("out", [T, D], F32, "ExternalOutput")
    w_in_d = dram("w_in", [L, D, INW], F32, EI)
    gate_b_d = dram("gate_b", [L, 2, D], F32, EI)
    sgu_ln_g_d = dram("sgu_ln_g", [L, D], F32, EI)
    sgu_ln_b_d = dram("sgu_ln_b", [L, D], F32, EI)
    sgu_w_d = dram("sgu_w", [L, 8, 128, 128], F32, EI)
    sgu_b_d = dram("sgu_b", [L, 8, 128], F32, EI)
    lam_d = dram("lam", [L, 4, 64], F32, EI)
    diff_g_d = dram("diff_ln_g", [L, 128], F32, EI)
    w_br_d = dram("w_branch", [L, 2, D, D], F32, EI)
    w_out_d = dram("w_out", [L, D, D], F32, EI)
    w1_d = dram("ffn_w1", [L, 2, D, DFF], F32, EI)
    w3_d = dram("ffn_w3", [L, 2, D, DFF], F32, EI)
    w2_d = dram("ffn_w2", [L, 2, DFF, D], F32, EI)
    ln_g_d = dram("ln_g", [L, 3, D], F32, EI)
    ln_b_d = dram("ln_b", [L, 3, D], F32, EI)
    cos_d = dram("cosT", [128, T], F32, EI)
    sin_d = dram("sinT", [128, T], F32, EI)
    mask_d = dram("masks", [128, 4, 128], F32, EI)
    const_d = dram("consts", [128, 3, 128], F32, EI)
    linit_d = dram("linit", [L, 2], F32, EI)

    wib = dram("wib", [L, D, INW], BF16)
    w1b = dram("w1b", [L, 2, D, DFF], BF16)
    w3b = dram("w3b", [L, 2, D, DFF], BF16)
    w2b = dram("w2b", [L, 2, DFF, D], BF16)
    wbrb = dram("wbrb", [L, 2, D, D], BF16)
    wob = dram("wob", [L, D, D], BF16)
    xres = dram("xres", [T, D], F32)
    DK = "ExternalOutput" if DEBUG_DUMP else None
    q_s = dram("q_s", [8, 128, T], BF16, DK)
    kT_l = [dram(f"kT_l{h}", [128, T], BF16) for h in range(8)]
    v_l = [dram(f"v_l{h}", [NT * 128, 512], BF16) for h in range(8)]
    kT_g = [dram(f"kT_g{h}", [4 * 128, T], BF16) for h in range(8)]
    v_g = [dram(f"v_g{h}", [4 * NT * 128, 512], BF16) for h in range(8)]
    mag_s = dram("mag_s", [8, 128, T], F32, DK)
    gb_s = dram("gb_s", [8, 128, T], F32, DK)
    if DEBUG_DUMP:
        dump_hb = dram("dump_hb", [128, 8 * 512], BF16, DK)

    def sb(name, shape, dtype):
        return st.enter_context(nc.sbuf_tensor("s_" + name, shape, dtype)).ap()

    def B(name, ap=None, arena=None, phase=0):
        return Buf(P, name, ap, arena, phase)

    consts_t = sb("consts", [128, 3, 128], F32)
    consts = B("consts", consts_t)
    ident = consts_t[:, 0, :]
    perm = consts_t[:, 1, :]
    ones32 = consts_t[:, 2, :]
    mask32_t = sb("mask32", [128, 4, 128], F32)
    maskb_t = sb("maskb", [128, 4, 128], BF16)
    mask32 = B("mask32", mask32_t)
    maskb = B("maskb", maskb_t)
    lnp_t = sb("lnp", [128, 4, 1024], F32)
    lnp = B("lnp", lnp_t)
    wmT_t = sb("wmT", [128, 8, 128], BF16)
    wmT = B("wmT", wmT_t)
    bS_t = sb("bS", [128, 8, 128], F32)
    bS = B("bS", bS_t)
    small_t = sb("small", [128, 64], F32)
    small = B("small", small_t)
    gateb_t = sb("gateb", [128, 16], F32)
    gateb = B("gateb", gateb_t)
    lam_t = sb("lamt", [128, 256], F32)
    lamb = B("lamb", lam_t)
    xa_t = sb("xa", [128, 4, 1024], F32)
    xb_t = sb("xb", [128, 4, 1024], F32)
    xa = B("xa", xa_t)
    xb = B("xb", xb_t)
    xT_t = [sb(f"xT{i}", [128, 8, 512], BF16) for i in range(2)]
    xT = [B(f"xT{i}", xT_t[i]) for i in range(2)]
    NW = 4
    wr_t = [sb(f"wr{i}", [128, 4096], BF16) for i in range(NW)]
    wr = [B(f"wr{i}", wr_t[i]) for i in range(NW)]
    gT_t = sb("gT", [128, 22, 512], BF16)
    gT = B("gT", gT_t)
    bA_t = sb("bA", [128, 8, 512], BF16)
    bA = B("bA", bA_t)
    bB_t = sb("bB", [128, 4096], BF16)
    bB = B("bB", bB_t)
    vn_t = bB_t[:, :].rearrange("p (a b) -> p a b", b=1024)
    QT_t = bB_t[:, :].rearrange("p (a b) -> p a b", b=512)
    bC_t = sb("bC", [128, 8, 512], BF16)
    bC = B("bC", bC_t)
    s2_t = [sb(f"s2_{i}", [128, 512], F32) for i in range(6)]
    s2 = [B(f"s2_{i}", s2_t[i]) for i in range(6)]
    r2_t = [sb(f"r2_{i}", [128, 1024], F32) for i in range(2)]
    r2 = [B(f"r2_{i}", r2_t[i]) for i in range(2)]
    vtile_t = sb("vtile", [128, 1024], F32)
    vtile = B("vtile", vtile_t)
    sgw_t = vtile_t[:, :].rearrange("p (a b) -> p a b", b=128)
    sgw = vtile
    cosb_t = sb("cosb", [128, 512], F32)
    sinb_t = sb("sinb", [128, 512], F32)
    cosb = B("cosb", cosb_t)
    sinb = B("sinb", sinb_t)
    qks_t = [sb(f"qks{i}", [128, 512], BF16) for i in range(2)]
    qks = [B(f"qks{i}", qks_t[i]) for i in range(2)]
    vsl_t = [sb(f"vsl{i}", [128, 1024], BF16) for i in range(2)]
    vsl = [B(f"vsl{i}", vsl_t[i]) for i in range(2)]
    NKV = 4
    kv_t = [sb(f"kv{i}", [128, 1024], BF16) for i in range(NKV)]
    kv = [B(f"kv{i}", kv_t[i]) for i in range(NKV)]
    NPT = 3
    pt_t = [sb(f"pt{i}", [128, 2, 512], BF16) for i in range(NPT)]
    pt = [B(f"pt{i}", pt_t[i]) for i in range(NPT)]
    stat_t = sb("stat", [128, 4, 16], F32)
    stats = [B(f"stat{i}", stat_t[:, i, :]) for i in range(4)]
    cb_t = sb("cb", [128, 2, 128], BF16)
    cb = B("cb", cb_t)
    permb = cb_t[:, 0, :]
    onesb = cb_t[:, 1, :]
    hl_t = [sb(f"hl{i}", [128, 512], BF16) for i in range(4)]
    hl = [B(f"hl{i}", hl_t[i]) for i in range(4)]
    lamp_t = sb("lamp", [128, 128], F32)
    lamp = B("lamp", lamp_t)
    linit_t = sb("linit", [128, 2 * L], F32)
    linit = B("linit", linit_t)
    lnp = [B(f"lnp{i}", lnp_t[:, i, :]) for i in range(4)]

    ps_t = [st.enter_context(nc.psum_tensor(f"ps{i}", [128, 512], F32)).ap() for i in range(8)]
    ps = [B(f"ps{i}", ps_t[i]) for i in range(8)]
    psrr = [0]

    def next_ps():
        i = psrr[0] % 8
        psrr[0] += 1
        return ps[i]

    d_w = [B(f"d_w{l}") for l in range(L)]
    d_xres = [B(f"d_xres{n}") for n in range(NT)]
    d_q = [B(f"d_q{n}") for n in range(NT)]
    d_mag = [B(f"d_mag{n}") for n in range(NT)]
    d_gb = [B(f"d_gb{n}") for n in range(NT)]
    d_kl = [B(f"d_kl{h}") for h in range(8)]
    d_vl = [B(f"d_vl{h}") for h in range(8)]
    d_kg = [B(f"d_kg{h}") for h in range(8)]
    d_vg = [B(f"d_vg{h}") for h in range(8)]
    d_out = B("d_out")
    ccsem = Sem(P, "cc")
    d_dump = B("d_dump")

    def cast_weights(l):
        def cp(src, dst, off, n):
            rows = n // 1024
            r0 = 0
            while r0 < rows:
                r = min(8192, rows - r0)
                P.dma("gpsimd", dap(dst, off + r0 * 1024, [(1024, r), (1, 1024)]),
                      dap(src, off + r0 * 1024, [(1024, r), (1, 1024)]), writes=[d_w[l]], nowait_dst=True)
                r0 += r
        cp(w1_d, w1b, l * 2 * D * DFF, 2 * D * DFF)
        cp(w3_d, w3b, l * 2 * D * DFF, 2 * D * DFF)
        cp(w2_d, w2b, l * 2 * D * DFF, 2 * D * DFF)
        cp(w_in_d, wib, l * D * INW, D * INW)
        cp(w_br_d, wbrb, l * 2 * D * D, 2 * D * D)
        cp(w_out_d, wob, l * D * D, D * D)

    wcount = [0]

    def wload(l, handle, off, rowstride, nk, ncols):
        i = wcount[0] % NW
        wcount[0] += 1
        view = wr_t[i][:, 0:nk * ncols].rearrange("p (a b) -> p a b", b=ncols)
        P.dma("sync", view, dap(handle, off, [(rowstride, 128), (128 * rowstride, nk), (1, ncols)]),
              reads=[d_w[l]], writes=[wr[i]])
        return wr[i], view

    def mm(out_ap, lhsT, rhs, start, stop):
        return lambda e: e.matmul(out_ap, lhsT, rhs, start=start, stop=stop)

    def act(out, in_, func, **kw):
        return lambda e: e.activation(out=out, in_=in_, func=func, **kw)

    def tt(out, in0, in1, op):
        return lambda e: e.tensor_tensor(out=out, in0=in0, in1=in1, op=op)

    def stt(out, in0, scalar, in1, op0, op1):
        return lambda e: e.scalar_tensor_tensor(out=out, in0=in0, scalar=scalar, in1=in1, op0=op0, op1=op1)

    def transposes_to_xT(src_t, src_b, di):
        for dc in range(8):
            pb = next_ps()
            fns = [lambda e, s=s, dc=dc, pb=pb: e.transpose(pb.ap[:, s * 128:(s + 1) * 128],
                                                           src_t[:, s, dc * 128:(dc + 1) * 128], ident)
                   for s in range(4)]
            P.op("tensor", fns, reads=[src_b, consts], writes=[pb])
            kw = dict(writes=[xT[di]]) if dc == 0 else dict(accum=[xT[di]])
            P.op("scalar", act(xT_t[di][:, dc, :], pb.ap, AF.Copy), reads=[pb], **kw)

    lncnt = [0]

    def layernorm(src_b, src_ap, dst_b, dst_ap, gi, bi, eps_ap, dst_first):
        sb_ = stats[lncnt[0] % 4]
        lncnt[0] += 1
        s_ap = sb_.ap
        P.op("vector", [lambda e: e.bn_stats(out=s_ap[:, 0:6], in_=src_ap[:, 0:512]),
                        lambda e: e.bn_stats(out=s_ap[:, 6:12], in_=src_ap[:, 512:1024])],
             reads=[src_b], writes=[sb_])
        P.op("vector", lambda e: e.bn_aggr(out=s_ap[:, 12:14], in_=s_ap[:, 0:12]), reads=[sb_], writes=[sb_])
        P.op("scalar", act(s_ap[:, 14:15], s_ap[:, 13:14], AF.Sqrt, bias=eps_ap, scale=1.0),
             reads=[small, sb_], writes=[sb_])
        P.op("vector", lambda e: e.reciprocal(out=s_ap[:, 14:15], in_=s_ap[:, 14:15]), reads=[sb_], writes=[sb_])
        P.op("vector", stt(s_ap[:, 15:16], s_ap[:, 12:13], -1.0, s_ap[:, 14:15], ALU.mult, ALU.mult),
             reads=[sb_], writes=[sb_])
        P.op("scalar", act(src_ap, src_ap, AF.Identity, scale=s_ap[:, 14:15], bias=s_ap[:, 15:16]),
             reads=[src_b, sb_], writes=[src_b])
        P.op("gpsimd", tt(src_ap, src_ap, lnp_t[:, gi, :], ALU.mult), reads=[src_b, lnp[gi]], writes=[src_b])
        kw = dict(writes=[dst_b]) if dst_first else dict(accum=[dst_b])
        P.op("gpsimd", tt(dst_ap, src_ap, lnp_t[:, bi, :], ALU.add), reads=[src_b, lnp[bi]], **kw)

    def ffn(l, slot, xi, res_b, res_t, dst_b, dst_t, gi, bi):
        base1 = (l * 2 + slot) * D * DFF
        first = True
        for c in range(6):
            nf = 4 if c < 5 else 2
            w1buf, w1v = wload(l, w1b, base1 + c * 512, DFF, 8, nf * 128)
            w3buf, w3v = wload(l, w3b, base1 + c * 512, DFF, 8, nf * 128)
            for j in range(nf):
                fc = c * 4 + j
                p1 = next_ps()
                p3 = next_ps()
                P.op("tensor", [mm(p1.ap, w1v[:, k, j * 128:(j + 1) * 128], xT_t[xi][:, k, :], k == 0, k == 7)
                                for k in range(8)], reads=[w1buf, xT[xi]], writes=[p1])
                P.op("tensor", [mm(p3.ap, w3v[:, k, j * 128:(j + 1) * 128], xT_t[xi][:, k, :], k == 0, k == 7)
                                for k in range(8)], reads=[w3buf, xT[xi]], writes=[p3])
                sbf = s2[fc % 4]
                P.op("scalar", act(sbf.ap, p1.ap, AF.Silu), reads=[p1], writes=[sbf])
                kw = dict(writes=[gT]) if first else dict(accum=[gT])
                first = False
                P.op("vector", tt(gT_t[:, fc, :], sbf.ap, p3.ap, ALU.mult), reads=[sbf, p3], **kw)
        for c in range(6):
            nf = 4 if c < 5 else 2
            w2buf, w2v = wload(l, w2b, (l * 2 + slot) * DFF * D + c * 512 * D, D, nf, 1024)
            for s in range(4):
                for hh in range(2):
                    pb = ps[s * 2 + hh]
                    fns = [mm(pb.ap, gT_t[:, c * 4 + j, s * 128:(s + 1) * 128], w2v[:, j, hh * 512:(hh + 1) * 512],
                              c == 0 and j == 0, c == 5 and j == nf - 1) for j in range(nf)]
                    if c == 0:
                        P.op("tensor", fns, reads=[w2buf, gT], writes=[pb])
                    else:
                        P.op("tensor", fns, reads=[w2buf, gT], accum=[pb])
        for s in range(4):
            rb = r2[s % 2]
            for hh in range(2):
                pb = ps[s * 2 + hh]
                kw = dict(writes=[rb]) if hh == 0 else dict(accum=[rb])
                P.op("vector", stt(rb.ap[:, hh * 512:(hh + 1) * 512], res_t[:, s, hh * 512:(hh + 1) * 512],
                                   2.0 * ALPHA, pb.ap, ALU.mult, ALU.add), reads=[res_b, pb], **kw)
            layernorm(rb, rb.ap, dst_b, dst_t[:, s, :], gi, bi, small_t[:, 1:2], dst_first=(s == 0))

    def load_lnp(parts):
        for i, (h, off) in enumerate(parts):
            P.dma("sync", lnp_t[:, i, :], dap(h, off, [(0, 128), (1, 1024)]), writes=[lnp[i]])

    P.dma("sync", consts_t, const_d.ap(), writes=[consts])
    P.dma("sync", mask32_t, mask_d.ap(), writes=[mask32])
    P.op("vector", lambda e: e.tensor_copy(out=maskb_t, in_=mask32_t), reads=[mask32], writes=[maskb])
    P.op("gpsimd", [lambda e: e.memset(small_t[:, 0:1], EPS), lambda e: e.memset(small_t[:, 1:2], 4 * EPS)],
         writes=[small])
    P.dma("sync", linit_t, dap(linit_d, 0, [(0, 128), (1, 2 * L)]), writes=[linit])
    P.op("vector", lambda e: e.tensor_copy(out=cb_t, in_=consts_t[:, 1:3, :]), reads=[consts], writes=[cb])

    def split_sum(out_b, lhsT_b16, src_ap, src_bs, i0):
        hi, lo = hl[i0], hl[i0 + 1]
        P.op("scalar", act(hi.ap, src_ap, AF.Copy), reads=src_bs, writes=[hi])
        P.op("vector", tt(lo.ap, src_ap, hi.ap, ALU.subtract), reads=list(src_bs) + [hi], writes=[lo])
        P.op("tensor", [mm(out_b.ap, lhsT_b16, hi.ap, True, False), mm(out_b.ap, lhsT_b16, lo.ap, False, True)],
             reads=[cb, hi, lo], writes=[out_b])
    for l in range(L):
        cast_weights(l)

    cur_n = [0]

    def stage(name, src_t=None, src_b=None):
        if DEBUG_STAGE == name or DEBUG_STAGE == f"{name}@{cur_n[0]}":
            if src_t is not None:
                P.dma("gpsimd", dap(out_d, 0, [(D, 128), (128 * D, 4), (1, D)]), src_t, reads=[src_b], sem_of=src_b, writes=[d_out], nowait_dst=True)
            raise _Stop()

    def emit_layer(l):
        src_h = x_in if l == 0 else xres
        load_lnp([(ln_g_d, (l * 3 + 0) * D), (ln_b_d, (l * 3 + 0) * D), (sgu_ln_g_d, l * D), (sgu_ln_b_d, l * D)])
        with nc.allow_non_contiguous_dma(reason="small param loads"):
            P.dma("sync", sgw_t, dap(sgu_w_d, l * 8 * 128 * 128, [(128, 128), (128 * 128, 8), (1, 128)]), writes=[sgw])
            P.dma("sync", bS_t, dap(sgu_b_d, l * 8 * 128, [(0, 128), (128, 8), (1, 128)]), writes=[bS])
            P.dma("sync", gateb_t, dap(gate_b_d, l * 2 * D, [(1, 128), (128, 16)]), writes=[gateb], allow_slow_non_contiguous=True)
            P.dma("sync", lam_t, dap(lam_d, l * 256, [(0, 128), (1, 256)]), writes=[lamb])
            P.dma("sync", small_t[:, 2:3], dap(diff_g_d, l * 128, [(1, 128), (1, 1)]), writes=[small])
        for g in range(8):
            pb = next_ps()
            P.op("tensor", lambda e, g=g, pb=pb: e.transpose(pb.ap[:, 0:128], sgw_t[:, g, :], ident),
                 reads=[sgw, consts], writes=[pb])
            sbf = s2[g % 4]
            P.op("scalar", act(sbf.ap[:, 0:128], pb.ap[:, 0:128], AF.Copy), reads=[pb], writes=[sbf])
            kw = dict(writes=[wmT]) if g == 0 else dict(accum=[wmT])
            P.op("gpsimd", lambda e, g=g, sbf=sbf: e.affine_select(
                out=wmT_t[:, g, :], in_=sbf.ap[:, 0:128], pattern=[[1, 128]], base=0, channel_multiplier=-1,
                compare_op=ALU.is_ge, fill=0.0), reads=[sbf], **kw)
        P.op("vector", [tt(lamp_t[:, 0:64], lam_t[:, 0:64], lam_t[:, 64:128], ALU.mult),
                        tt(lamp_t[:, 64:128], lam_t[:, 128:192], lam_t[:, 192:256], ALU.mult)],
             reads=[lamb], writes=[lamp])
        P.op("scalar", [act(lamp_t[:, 0:64], lamp_t[:, 0:64], AF.Identity, accum_out=small_t[:, 5:6]),
                        act(lamp_t[:, 64:128], lamp_t[:, 64:128], AF.Identity, accum_out=small_t[:, 6:7])],
             reads=[lamp], writes=[small, lamp])
        P.op("scalar", act(small_t[:, 7:9], small_t[:, 5:7], AF.Exp), reads=[small], writes=[small])
        P.op("vector", tt(small_t[:, 9:10], small_t[:, 7:8], small_t[:, 8:9], ALU.subtract), reads=[small], writes=[small])
        P.op("vector", stt(small_t[:, 4:5], small_t[:, 9:10], -1.0, linit_t[:, 2 * l:2 * l + 1], ALU.mult, ALU.subtract),
             reads=[small, linit], writes=[small])
        P.op("vector", tt(small_t[:, 3:4], small_t[:, 2:3], linit_t[:, 2 * l + 1:2 * l + 2], ALU.mult),
             reads=[small, linit], writes=[small])
        neglam = small_t[:, 4:5]
        gcoef = small_t[:, 3:4]

        for n in range(NT):
            cur_n[0] = n
            tsl = slice(n * 512, (n + 1) * 512)
            P.dma("sync", xa_t, dap(src_h, n * 512 * D, [(D, 128), (128 * D, 4), (1, D)]),
                  reads=[d_xres[n]] if l > 0 else [], writes=[xa])
            P.dma("sync", cosb_t, cos_d.ap()[:, tsl], writes=[cosb])
            P.dma("sync", sinb_t, sin_d.ap()[:, tsl], writes=[sinb])
            stage('A', xa_t, xa)
            transposes_to_xT(xa_t, xa, 0)
            ffn(l, 0, 0, xa, xa_t, xb, xb_t, 0, 1)
            stage('B', xb_t, xb)
            P.dma("gpsimd", dap(xres, n * 512 * D, [(D, 128), (128 * D, 4), (1, D)]), xb_t, reads=[xb], sem_of=xb, writes=[d_xres[n]])
            transposes_to_xT(xb_t, xb, 1)
            X1 = xT_t[1]
            X1b = xT[1]
            wbase = l * D * INW
            for c2 in range(2):
                wb_, wv = wload(l, wib, wbase + 0 + c2 * 512, INW, 8, 512)
                for j in range(4):
                    c = c2 * 4 + j
                    pb = next_ps()
                    P.op("tensor", [mm(pb.ap, wv[:, k, j * 128:(j + 1) * 128], X1[:, k, :], k == 0, k == 7) for k in range(8)],
                         reads=[wb_, X1b], writes=[pb])
                    kw = dict(writes=[bA]) if c == 0 else dict(accum=[bA])
                    P.op("scalar", act(bA_t[:, c, :], pb.ap, AF.Gelu), reads=[pb], **kw)
            stage('C1', xb_t, xb)
            wv0b, wv0 = wload(l, wib, wbase + 1024, INW, 8, 512)
            wv1b, wv1 = wload(l, wib, wbase + 1536, INW, 8, 512)
            for s in range(4):
                for hh, (wb_, wv) in enumerate([(wv0b, wv0), (wv1b, wv1)]):
                    pb = next_ps()
                    P.op("tensor", [mm(pb.ap, X1[:, k, s * 128:(s + 1) * 128], wv[:, k, :], k == 0, k == 7) for k in range(8)],
                         reads=[wb_, X1b], writes=[pb])
                    kw = dict(writes=[vtile]) if hh == 0 else dict(accum=[vtile])
                    P.op("scalar", act(vtile_t[:, hh * 512:(hh + 1) * 512], pb.ap, AF.Gelu), reads=[pb], **kw)
                layernorm(vtile, vtile_t, bB, vn_t[:, s, :], 2, 3, small_t[:, 0:1], dst_first=(s == 0))
            stage('C2', xb_t, xb)
            for g in range(8):
                pb = next_ps()
                P.op("tensor", [mm(pb.ap[:, s * 128:(s + 1) * 128], vn_t[:, s, g * 128:(g + 1) * 128], wmT_t[:, g, :], True, True)
                                for s in range(4)], reads=[bB, wmT], writes=[pb])
                sbf = s2[g % 4]
                bsb = bass.AP(bS_t.tensor, g * 128, [[1024, 128], [0, 4], [1, 128]])
                P.op("vector", tt(sbf.ap[:, :].rearrange("p (a b) -> p a b", b=128),
                                  pb.ap[:, :].rearrange("p (a b) -> p a b", b=128), bsb, ALU.add),
                     reads=[pb, bS], writes=[sbf])
                kw = dict(writes=[bC]) if g == 0 else dict(accum=[bC])
                P.op("gpsimd", tt(bC_t[:, g, :], sbf.ap, bA_t[:, g, :], ALU.mult), reads=[sbf, bA], **kw)
            stage('C3', xb_t, xb)
            for hf in range(2):
                wgb, wg = wload(l, wib, wbase + 5120 + hf * 512, INW, 8, 512)
                wab, wa = wload(l, wbrb, (l * 2 + 0) * D * D + hf * 512, D, 8, 512)
                for j in range(4):
                    c = hf * 4 + j
                    pg = next_ps()
                    P.op("tensor", [mm(pg.ap, wg[:, k, j * 128:(j + 1) * 128], X1[:, k, :], k == 0, k == 7) for k in range(8)],
                         reads=[wgb, X1b], writes=[pg])
                    gs = s2[(2 * c) % 6]
                    P.op("scalar", act(gs.ap, pg.ap, AF.Sigmoid, bias=gateb_t[:, c:c + 1], scale=1.0),
                         reads=[pg, gateb], writes=[gs])
                    pm = next_ps()
                    P.op("tensor", [mm(pm.ap, wa[:, k, j * 128:(j + 1) * 128], bC_t[:, k, :], k == 0, k == 7) for k in range(8)],
                         reads=[wab, bC], writes=[pm])
                    P.op("vector", tt(gs.ap, gs.ap, pm.ap, ALU.mult), reads=[gs, pm], writes=[gs])
                    P.dma("gpsimd", dap(mag_s, c * 128 * T + n * 512, [(T, 128), (1, 512)]), gs.ap, reads=[gs], sem_of=gs,
                          writes=[d_mag[n]], nowait_dst=(c > 0))
            stage('C4', xb_t, xb)
            for hf in range(2):
                wgb, wg = wload(l, wib, wbase + 6144 + hf * 512, INW, 8, 512)
                for j in range(4):
                    c = hf * 4 + j
                    pg = next_ps()
                    P.op("tensor", [mm(pg.ap, wg[:, k, j * 128:(j + 1) * 128], X1[:, k, :], k == 0, k == 7) for k in range(8)],
                         reads=[wgb, X1b], writes=[pg])
                    gs = s2[(2 * c + 1) % 6]
                    P.op("scalar", act(gs.ap, pg.ap, AF.Sigmoid, bias=gateb_t[:, 8 + c:9 + c], scale=1.0),
                         reads=[pg, gateb], writes=[gs])
                    P.dma("gpsimd", dap(gb_s, c * 128 * T + n * 512, [(T, 128), (1, 512)]), gs.ap, reads=[gs], sem_of=gs,
                          writes=[d_gb[n]], nowait_dst=(c > 0))
            stage('C5', xb_t, xb)
            for which in range(2):
                for hf in range(2):
                    wqb, wq = wload(l, wib, wbase + 2048 + which * 1024 + hf * 512, INW, 8, 512)
                    for j in range(4):
                        h = hf * 4 + j
                        pq = next_ps()
                        P.op("tensor", [mm(pq.ap, wq[:, k, j * 128:(j + 1) * 128], X1[:, k, :], k == 0, k == 7) for k in range(8)],
                             reads=[wqb, X1b], writes=[pq])
                        psw = next_ps()
                        split_sum(psw, permb, pq.ap, [pq], 0)
                        t1 = s2[(2 * h) % 6]
                        t2 = s2[(2 * h + 1) % 6]
                        P.op("vector", tt(t1.ap, pq.ap, cosb_t, ALU.mult), reads=[pq, cosb], writes=[t1])
                        P.op("vector", tt(t2.ap, psw.ap, sinb_t, ALU.mult), reads=[psw, sinb], writes=[t2])
                        qo = qks[h % 2]
                        P.op("gpsimd", tt(qo.ap, t1.ap, t2.ap, ALU.add), reads=[t1, t2], writes=[qo])
                        if which == 0:
                            P.dma("sync", dap(q_s, h * 128 * T + n * 512, [(T, 128), (1, 512)]), qo.ap, reads=[qo], sem_of=qo,
                                  writes=[d_q[n]], nowait_dst=(h > 0))
                        else:
                            P.dma("sync", dap(kT_l[h], n * 512, [(T, 128), (1, 512)]), qo.ap, reads=[qo], sem_of=qo,
                                  writes=[d_kl[h]], nowait_dst=(n > 0))
            stage('C6', xb_t, xb)
            wv0b, wv0 = wload(l, wib, wbase + 4096, INW, 8, 512)
            wv1b, wv1 = wload(l, wib, wbase + 4608, INW, 8, 512)
            for s in range(4):
                vo = vsl[s % 2]
                for hh, (wb_, wv) in enumerate([(wv0b, wv0), (wv1b, wv1)]):
                    pb = next_ps()
                    P.op("tensor", [mm(pb.ap, X1[:, k, s * 128:(s + 1) * 128], wv[:, k, :], k == 0, k == 7) for k in range(8)],
                         reads=[wb_, X1b], writes=[pb])
                    kw = dict(writes=[vo]) if hh == 0 else dict(accum=[vo])
                    P.op("scalar", act(vo.ap[:, hh * 512:(hh + 1) * 512], pb.ap, AF.Copy), reads=[pb], **kw)
                for h in range(8):
                    P.dma("sync", dap(v_l[h], n * 128 * 512 + s * 128, [(512, 128), (1, 128)]),
                          vo.ap[:, h * 128:(h + 1) * 128], reads=[vo], sem_of=vo, writes=[d_vl[h]],
                          nowait_dst=not (n == 0 and s == 0))

        cur_n[0] = -1
        stage('C', xb_t, xb)
        cc_list = []
        for h in range(8):
            cc_list.append((kT_l[h], kT_g[h], d_kl[h], d_kg[h]))
            cc_list.append((v_l[h], v_g[h], d_vl[h], d_vg[h]))
        tk = None
        for (src, dst, sbuf_, dbuf_) in cc_list:
            P.wait("gpsimd", sbuf_.r_waits() + dbuf_.w_waits())
            tk = ccsem.inc(1, "cc")
            P.ops["gpsimd"].append(("o", (lambda e, src=src, dst=dst: e.collective_compute(
                "AllGather", ALU.bypass, replica_groups=[[0, 1, 2, 3], [4, 5, 6, 7]],
                ins=[src.ap()], outs=[dst.ap()])), tk[0], 1))
            sbuf_.add_reader(tk)
        for (src, dst, sbuf_, dbuf_) in cc_list:
            sbuf_.add_reader(tk)
            dbuf_.set_writer(tk)

        stage('D', xb_t, xb)
        load_lnp([(ln_g_d, (l * 3 + 1) * D), (ln_b_d, (l * 3 + 1) * D), (ln_g_d, (l * 3 + 2) * D), (ln_b_d, (l * 3 + 2) * D)])
        O1, O2, L1, L2 = ps[0], ps[1], ps[2], ps[3]
        for n in range(NT):
            cur_n[0] = n
            P.dma("sync", xa_t, dap(xres, n * 512 * D, [(D, 128), (128 * D, 4), (1, D)]), reads=[d_xres[n]], writes=[xa])
            P.dma("sync", QT_t, dap(q_s, n * 512, [(T, 128), (128 * T, 8), (1, 512)]), reads=[d_q[n]], writes=[bB])
            steps = []
            for h in range(8):
                hs = []
                for tl in range(n):
                    for r in range(4):
                        hs.append((r, tl, None))
                for r in range(4):
                    hs.append((r, n, r))
                steps.append(hs)
            kvc = [0]
            stepc = [0]
            sidx_first = [None]
            for h in range(8):
                blocks = []
                for (r, tl, mt) in steps[h]:
                    i = kvc[0] % NKV
                    kvc[0] += 1
                    P.dma("sync", kv_t[i][:, 0:512], dap(kT_g[h], (r * 128) * T + tl * 512, [(T, 128), (1, 512)]),
                          reads=[d_kg[h]], writes=[kv[i]])
                    P.dma("sync", kv_t[i][:, 512:1024], dap(v_g[h], ((r * NT + tl) * 128) * 512, [(512, 128), (1, 512)]),
                          reads=[d_vg[h]], writes=[kv[i]], nowait_dst=True)
                    for a in range(4):
                        blocks.append((i, a, 0 if mt is None else a * 128, mt))
                    pend = blocks
                    blocks = []
                    for bi_, (i, a, qlo, mt) in enumerate(pend):
                        sidx = stepc[0]
                        stepc[0] += 1
                        S1 = ps[4 + (sidx % 2) * 2]
                        S2 = ps[5 + (sidx % 2) * 2]
                        pti = sidx % NPT
                        ptb = pt[pti]
                        ptt = pt_t[pti]
                        first = (sidx_first[0] is None)
                        if first:
                            sidx_first[0] = sidx
                        P.op("tensor", [mm(S1.ap[:, qlo:512], kv_t[i][0:64, a * 128:(a + 1) * 128], QT_t[0:64, h, qlo:512], True, True),
                                        mm(S2.ap[:, qlo:512], kv_t[i][64:128, a * 128:(a + 1) * 128], QT_t[64:128, h, qlo:512], True, True)],
                             reads=[kv[i], bB], writes=[S1, S2])
                        P.op("scalar", [act(ptt[:, 0, qlo:512], S1.ap[:, qlo:512], AF.Exp, scale=0.125),
                                        act(ptt[:, 1, qlo:512], S2.ap[:, qlo:512], AF.Exp, scale=0.125)],
                             reads=[S1, S2], writes=[ptb])
                        if mt is not None:
                            P.op("gpsimd", [tt(ptt[:, m_, qlo:qlo + 128], ptt[:, m_, qlo:qlo + 128], maskb_t[:, mt, :], ALU.mult)
                                            for m_ in range(2)], reads=[ptb, maskb], writes=[ptb])
                        vsl_ = kv_t[i][:, 512 + a * 128:512 + (a + 1) * 128]
                        if first:
                            P.op("tensor", [mm(O1.ap[:, qlo:512], vsl_, ptt[:, 0, qlo:512], True, False),
                                            mm(O2.ap[:, qlo:512], vsl_, ptt[:, 1, qlo:512], True, False)],
                                 reads=[kv[i], ptb], writes=[O1, O2])
                            P.op("vector", [lambda e, ptt=ptt: e.tensor_copy(out=L1.ap, in_=ptt[:, 0, :]),
                                            lambda e, ptt=ptt: e.tensor_copy(out=L2.ap, in_=ptt[:, 1, :])],
                                 reads=[ptb], writes=[L1, L2])
                        else:
                            P.op("tensor", [mm(O1.ap[:, qlo:512], vsl_, ptt[:, 0, qlo:512], False, False),
                                            mm(O2.ap[:, qlo:512], vsl_, ptt[:, 1, qlo:512], False, False)],
                                 reads=[kv[i], ptb], accum=[O1, O2])
                            P.op("vector", [tt(L1.ap[:, qlo:512], L1.ap[:, qlo:512], ptt[:, 0, qlo:512], ALU.add),
                                            tt(L2.ap[:, qlo:512], L2.ap[:, qlo:512], ptt[:, 1, qlo:512], ALU.add)],
                                 reads=[ptb, L1, L2], writes=[L1, L2])
                sidx_first[0] = None
                e0, e1, e2, e3 = s2[0], s2[1], s2[2], s2[3]
                split_sum(L1, onesb, L1.ap, [L1], 0)
                split_sum(L2, onesb, L2.ap, [L2], 2)
                P.op("vector", [lambda e: e.reciprocal(out=e0.ap, in_=L1.ap), lambda e: e.reciprocal(out=e1.ap, in_=L2.ap)],
                     reads=[L1, L2], writes=[e0, e1])
                P.op("vector", [tt(e2.ap, O1.ap, e0.ap, ALU.mult), tt(e3.ap, O2.ap, e1.ap, ALU.mult)],
                     reads=[O1, O2, e0, e1], writes=[e2, e3])
                P.op("vector", stt(e2.ap, e3.ap, neglam, e2.ap, ALU.mult, ALU.add), reads=[e2, e3, small], writes=[e2])
                P.op("scalar", act(e3.ap, e2.ap, AF.Square), reads=[e2], writes=[e3])
                split_sum(L1, onesb, e3.ap, [e3], 0)
                P.op("scalar", act(e3.ap, L1.ap, AF.Sqrt, bias=small_t[:, 0:1], scale=1.0 / 128.0), reads=[L1, small], writes=[e3])
                P.op("vector", lambda e: e.reciprocal(out=e3.ap, in_=e3.ap), reads=[e3], writes=[e3])
                kw = dict(writes=[bA]) if h == 0 else dict(accum=[bA])
                P.op("vector", stt(bA_t[:, h, :], e2.ap, gcoef, e3.ap, ALU.mult, ALU.mult), reads=[e2, e3, small], **kw)
            if DEBUG_DUMP and n == 0:
                P.dma("sync", dump_hb.ap(), bA_t[:, :, :].rearrange("p a b -> p (a b)"), reads=[bA], sem_of=bA, writes=[d_dump], nowait_dst=True)
            stage('E', xb_t, xb)
            for hf in range(2):
                wbb, wb2 = wload(l, wbrb, (l * 2 + 1) * D * D + hf * 512, D, 8, 512)
                for j in range(4):
                    c = hf * 4 + j
                    pm = next_ps()
                    P.op("tensor", [mm(pm.ap, wb2[:, k, j * 128:(j + 1) * 128], bA_t[:, k, :], k == 0, k == 7) for k in range(8)],
                         reads=[wbb, bA], writes=[pm])
                    gsl = s2[(2 * c) % 6]
                    msl = s2[(2 * c + 1) % 6]
                    P.dma("sync", gsl.ap, dap(gb_s, c * 128 * T + n * 512, [(T, 128), (1, 512)]), reads=[d_gb[n]], writes=[gsl])
                    P.dma("sync", msl.ap, dap(mag_s, c * 128 * T + n * 512, [(T, 128), (1, 512)]), reads=[d_mag[n]], writes=[msl])
                    P.op("vector", tt(gsl.ap, gsl.ap, pm.ap, ALU.mult), reads=[gsl, pm], writes=[gsl])
                    kw = dict(writes=[bC]) if c == 0 else dict(accum=[bC])
                    P.op("gpsimd", tt(bC_t[:, c, :], gsl.ap, msl.ap, ALU.add), reads=[gsl, msl], **kw)
            wo0b, wo0 = wload(l, wob, l * D * D + 0, D, 8, 512)
            wo1b, wo1 = wload(l, wob, l * D * D + 512, D, 8, 512)
            for s in range(4):
                rb = r2[s % 2]
                for hh, (wb_, wv) in enumerate([(wo0b, wo0), (wo1b, wo1)]):
                    pb = next_ps()
                    P.op("tensor", [mm(pb.ap, bC_t[:, k, s * 128:(s + 1) * 128], wv[:, k, :], k == 0, k == 7) for k in range(8)],
                         reads=[wb_, bC], writes=[pb])
                    kw = dict(writes=[rb]) if hh == 0 else dict(accum=[rb])
                    P.op("vector", stt(rb.ap[:, hh * 512:(hh + 1) * 512], xa_t[:, s, hh * 512:(hh + 1) * 512], ALPHA, pb.ap,
                                       ALU.mult, ALU.add), reads=[xa, pb], **kw)
                layernorm(rb, rb.ap, xb, xb_t[:, s, :], 0, 1, small_t[:, 0:1], dst_first=(s == 0))
            transposes_to_xT(xb_t, xb, 0)
            ffn(l, 1, 0, xb, xb_t, xb, xb_t, 2, 3)
            if l == L - 1:
                P.dma("gpsimd", dap(out_d, n * 512 * D, [(D, 128), (128 * D, 4), (1, D)]), xb_t, reads=[xb],
                      sem_of=xb, writes=[d_out], nowait_dst=True)
            else:
                P.dma("gpsimd", dap(xres, n * 512 * D, [(D, 128), (128 * D, 4), (1, D)]), xb_t, reads=[xb], sem_of=xb, writes=[d_xres[n]])

    try:
        for l in range(L):
            emit_layer(l)
    except _Stop:
        pass
    P.wait("gpsimd", d_out.r_waits() + d_dump.r_waits())
    P.op("gpsimd", lambda e: e.memset(small_t[:, 10:11], 0.0), writes=[])

    with nc.Block() as block:
        @block.sync
        def _(e):
            P.replay("sync", e)

        @block.scalar
        def _(e):
            P.replay("scalar", e)

        @block.vector
        def _(e):
            P.replay("vector", e)

        @block.gpsimd
        def _(e):
            P.replay("gpsimd", e)

        @block.tensor
        def _(e):
            P.replay("tensor", e)
    return nc


def _lambda_init(layer):
    import math
    return 0.8 - 0.6 * math.exp(-0.3 * layer)


def _core_tables(j, NT, seq):
    T = NT * 512
    i = np.arange(T)
    pos = ((4 * (i // 128) + j) * 128 + i % 128).astype(np.float32)
    inv_freq = (np.float32(10000.0) ** (-np.arange(0, 64, 2, dtype=np.float32) / np.float32(64))).astype(np.float32)
    ang = pos[None, :] * inv_freq[:, None]
    cos = np.cos(ang).astype(np.float32)
    sin = np.sin(ang).astype(np.float32)
    idx = np.arange(128) % 32
    return np.ascontiguousarray(cos[idx]), np.ascontiguousarray(sin[idx])


def _consts():
    c = np.zeros((128, 3, 128), np.float32)
    c[:, 0, :] = np.eye(128, dtype=np.float32)
    for p in range(128):
        if p % 64 < 32:
            c[p + 32, 1, p] = -1.0
        else:
            c[p - 32, 1, p] = 1.0
    c[:, 2, :] = 1.0
    return c


def _masks(j):
    m = np.zeros((128, 4, 128), np.float32)
    for t in range(4):
        if t < j:
            m[:, t, :] = 1.0
        elif t == j:
            m[:, t, :] = np.triu(np.ones((128, 128), np.float32))
    return m


_PROG_CACHE = {}


def run_layers(x, params, layers, NT):
    L = len(layers)
    key = (L, NT)
    if key not in _PROG_CACHE:
        _PROG_CACHE[key] = build_program(L, NT)
    nc = _PROG_CACHE[key]
    S = 4 * NT * 512
    NBLK = S // 128
    consts = _consts()
    linit = np.array([[_lambda_init(l), 1.0 - _lambda_init(l)] for l in layers], np.float32)
    sl = {k: np.ascontiguousarray(v[list(layers)]) for k, v in params.items()}
    in_maps = []
    for c in range(8):
        b, j = c // 4, c % 4
        xc = np.ascontiguousarray(x[b].reshape(NBLK, 128, D)[j::4].reshape(NT * 512, D))
        cosT, sinT = _core_tables(j, NT, S)
        m = dict(sl)
        m.update({"x": xc, "cosT": cosT, "sinT": sinT, "masks": _masks(j), "consts": consts, "linit": linit})
        in_maps.append(m)
    res = run_bass_kernel_spmd(nc, in_maps, core_ids=list(range(8)))
    LAST_RES[0] = res
    out = np.empty((2, S, D), np.float32)
    for c in range(8):
        b, j = c // 4, c % 4
        out[b].reshape(NBLK, 128, D)[j::4] = res.results[c]["out"].reshape(NT * 4, 128, D)
    return out


FUSED = True


def kernel(x, w_in, gate_b, sgu_ln_g, sgu_ln_b, sgu_w, sgu_b, lam, diff_ln_g, w_branch, w_out,
           ffn_w1, ffn_w3, ffn_w2, ln_g, ln_b):
    params = dict(w_in=w_in, gate_b=gate_b, sgu_ln_g=sgu_ln_g, sgu_ln_b=sgu_ln_b, sgu_w=sgu_w, sgu_b=sgu_b,
                  lam=lam, diff_ln_g=diff_ln_g, w_branch=w_branch, w_out=w_out, ffn_w1=ffn_w1, ffn_w3=ffn_w3,
                  ffn_w2=ffn_w2, ln_g=ln_g, ln_b=ln_b)
    params = {k: np.asarray(v, np.float32) for k, v in params.items()}
    x = np.asarray(x, np.float32)
    NT = x.shape[1] // 2048
    depth = params["w_in"].shape[0]
    if FUSED:
        return run_layers(x, params, list(range(depth)), NT)
    for l in range(depth):
        x = run_layers(x, params, [l], NT)
    return x
```

```python
import numpy as np
from contextlib import ExitStack
import concourse.bass as bass
import concourse.mybir as mybir
from concourse.bass_utils import run_bass_kernel_spmd

F32 = mybir.dt.float32
BF16 = mybir.dt.bfloat16
AF = mybir.ActivationFunctionType
ALU = mybir.AluOpType

D = 1024
DFF = 2816
NFC = 22
INW = 7168
DEPTH = 4
SEQ = 16384
BATCH = 2
ALPHA = (2.0 * DEPTH) ** 0.25
EPS = 1e-5
ENGS = ("sync", "scalar", "vector", "gpsimd", "tensor")
SAME_ENGINE_SYNC = True


class Sem:
    LIMIT = 30000

    def __init__(self, P, name):
        self.P, self.name, self.n = P, name, 0
        self.new()

    def new(self):
        self.h = self.P.stack.enter_context(self.P.nc.semaphore(f"{self.name}_{self.n}"))
        self.n += 1
        self.val = 0

    def inc(self, amt, eng):
        if self.val + amt > self.LIMIT:
            self.new()
        self.val += amt
        return (self.h, self.val, eng)


class Arena:
    def __init__(self):
        self.bufs = []


class Buf:
    def __init__(self, P, name, ap=None, arena=None, phase=0):
        self.P, self.name, self.ap = P, name, ap
        self.wset = {}
        self.readers = {}
        self.dsem = None
        self.arena, self.phase = arena, phase
        if arena is not None:
            arena.bufs.append(self)

    def sem(self):
        if self.dsem is None:
            self.dsem = Sem(self.P, "d_" + self.name)
        return self.dsem

    def add_reader(self, tk):
        k = id(tk[0])
        if k not in self.readers or self.readers[k][1] < tk[1]:
            self.readers[k] = tk

    def set_writer(self, tk):
        self.wset = {id(tk[0]): tk}
        self.readers = {}

    def add_writer(self, tk):
        k = id(tk[0])
        if k not in self.wset or self.wset[k][1] < tk[1]:
            self.wset[k] = tk

    def r_waits(self):
        return list(self.wset.values())

    def w_waits(self):
        w = list(self.readers.values()) + list(self.wset.values())
        if self.arena is not None:
            for o in self.arena.bufs:
                if o.phase != self.phase:
                    w += list(o.readers.values()) + list(o.wset.values())
        return w


class Prog:
    def __init__(self, nc):
        self.nc = nc
        self.stack = ExitStack()
        self.ops = {e: [] for e in ENGS}
        self.waited = {e: {} for e in ENGS}
        self.prog = {e: Sem(self, "p_" + e) for e in ENGS}
        self.nbuf = 0

    def _waits(self, eng, tks):
        for tk in tks:
            if tk is None:
                continue
            h, v, teng = tk
            if teng == eng and (eng == "tensor" or not SAME_ENGINE_SYNC):
                continue
            k = id(h)
            if self.waited[eng].get(k, 0) >= v:
                continue
            self.waited[eng][k] = v
            self.ops[eng].append(("w", h, v))

    def op(self, eng, fns, reads=(), writes=(), accum=()):
        if not isinstance(fns, (list, tuple)):
            fns = [fns]
        tks = []
        for b in reads:
            tks += b.r_waits()
        for b in accum:
            tks += b.r_waits()
        for b in writes:
            tks += b.w_waits()
        self._waits(eng, tks)
        tk = self.prog[eng].inc(1, eng)
        for f in fns[:-1]:
            self.ops[eng].append(("o", f, None, 0))
        self.ops[eng].append(("o", fns[-1], tk[0], 1))
        for b in reads:
            b.add_reader(tk)
        for b in list(writes) + list(accum):
            b.set_writer(tk)
        return tk

    def dma(self, eng, out, in_, reads=(), writes=(), nowait_dst=False, sem_of=None, **kw):
        tks = []
        for b in reads:
            tks += b.r_waits()
        for b in writes:
            if not nowait_dst:
                tks += b.w_waits()
        self._waits(eng, tks)
        wb = sem_of if sem_of is not None else writes[0]
        tk = wb.sem().inc(16, "dma")
        self.ops[eng].append(("o", lambda e: e.dma_start(out=out, in_=in_, **kw), tk[0], 16))
        for b in reads:
            b.add_reader(tk)
        for b in writes:
            if nowait_dst:
                b.add_writer(tk)
            else:
                b.set_writer(tk)
        return tk

    def wait(self, eng, tks):
        self._waits(eng, tks)

    def replay(self, eng, e):
        for o in self.ops[eng]:
            if o[0] == "w":
                e.wait_ge(o[1], o[2])
            else:
                ins = o[1](e)
                if o[2] is not None:
                    ins.then_inc(o[2], o[3])


class _Stop(Exception):
    pass


DEBUG_STAGE = None
DEBUG_DUMP = False
LAST_RES = [None]


def build_program(L, NT):
    T = NT * 512
    NB = NT * 4
    nc = bass.Bass("TRN2", target_bir_lowering=False)
    P = Prog(nc)
    st = P.stack

    def dram(name, shape, dtype, kind=None):
        if kind is None:
            return nc.dram_tensor(name, shape, dtype)
        return nc.dram_tensor(name, shape, dtype, kind=kind)

    def dap(h, off, pat):
        return bass.AP(h, off, [list(p) for p in pat])

    EI = "ExternalInput"
    x_in = dram("x", [T, D], F32, EI)
    out_d = dram("out", [T, D], F32, "ExternalOutput")
    w_in_d = dram("w_in", [L, D, INW], F32, EI)
    gate_b_d = dram("gate_b", [L, 2, D], F32, EI)
    sgu_ln_g_d = dram("sgu_ln_g", [L, D], F32, EI)
    sgu_ln_b_d = dram("sgu_ln_b", [L, D], F32, EI)
    sgu_w_d = dram("sgu_w", [L, 8, 128, 128], F32, EI)
    sgu_b_d = dram("sgu_b", [L, 8, 128], F32, EI)
    lam_d = dram("lam", [L, 4, 64], F32, EI)
    diff_g_d = dram("diff_ln_g", [L, 128], F32, EI)
    w_br_d = dram("w_branch", [L, 2, D, D], F32, EI)
    w_out_d = dram("w_out", [L, D, D], F32, EI)
    w1_d = dram("ffn_w1", [L, 2, D, DFF], F32, EI)
    w3_d = dram("ffn_w3", [L, 2, D, DFF], F32, EI)
    w2_d = dram("ffn_w2", [L, 2, DFF, D], F32, EI)
    ln_g_d = dram("ln_g", [L, 3, D], F32, EI)
    ln_b_d = dram("ln_b", [L, 3, D], F32, EI)
    cos_d = dram("cosT", [128, T], F32, EI)
    sin_d = dram("sinT", [128, T], F32, EI)
    mask_d = dram("masks", [128, 4, 128], F32, EI)
    const_d = dram("consts", [128, 3, 128], F32, EI)
    linit_d = dram("linit", [L, 2], F32, EI)

    wib = dram("wib", [L, D, INW], BF16)
    w1b = dram("w1b", [L, 2, D, DFF], BF16)
    w3b = dram("w3b", [L, 2, D, DFF], BF16)
    w2b = dram("w2b", [L, 2, DFF, D], BF16)
    wbrb = dram("wbrb", [L, 2, D, D], BF16)
    wob = dram("wob", [L, D, D], BF16)
    xres = dram("xres", [T, D], F32)
    DK = "ExternalOutput" if DEBUG_DUMP else None
    q_s = dram("q_s", [8, 128, T], BF16, DK)
    kT_l = [dram(f"kT_l{h}", [128, T], BF16) for h in range(8)]
    v_l = [dram(f"v_l{h}", [NT * 128, 512], BF16) for h in range(8)]
    kT_g = [dram(f"kT_g{h}", [4 * 128, T], BF16) for h in range(8)]
    v_g = [dram(f"v_g{h}", [4 * NT * 128, 512], BF16) for h in range(8)]
    mag_s = dram("mag_s", [8, 128, T], F32, DK)
    gb_s = dram("gb_s", [8, 128, T], F32, DK)
    if DEBUG_DUMP:
        dump_hb = dram("dump_hb", [128, 8 * 512], BF16, DK)

    def sb(name, shape, dtype):
        return st.enter_context(nc.sbuf_tensor("s_" + name, shape, dtype)).ap()

    def B(name, ap=None, arena=None, phase=0):
        return Buf(P, name, ap, arena, phase)

    consts_t = sb("consts", [128, 3, 128], F32)
    consts = B("consts", consts_t)
    ident = consts_t[:, 0, :]
    perm = consts_t[:, 1, :]
    ones32 = consts_t[:, 2, :]
    mask32_t = sb("mask32", [128, 4, 128], F32)
    maskb_t = sb("maskb", [128, 4, 128], BF16)
    mask32 = B("mask32", mask32_t)
    maskb = B("maskb", maskb_t)
    lnp_t = sb("lnp", [128, 4, 1024], F32)
    lnp = B("lnp", lnp_t)
    wmT_t = sb("wmT", [128, 8, 128], BF16)
    wmT = B("wmT", wmT_t)
    bS_t = sb("bS", [128, 8, 128], F32)
    bS = B("bS", bS_t)
    small_t = sb("small", [128, 64], F32)
    small = B("small", small_t)
    gateb_t = sb("gateb", [128, 16], F32)
    gateb = B("gateb", gateb_t)
    lam_t = sb("lamt", [128, 256], F32)
    lamb = B("lamb", lam_t)
    xa_t = sb("xa", [128, 4, 1024], F32)
    xb_t = sb("xb", [128, 4, 1024], F32)
    xa = B("xa", xa_t)
    xb = B("xb", xb_t)
    xT_t = [sb(f"xT{i}", [128, 8, 512], BF16) for i in range(2)]
    xT = [B(f"xT{i}", xT_t[i]) for i in range(2)]
    NW = 4
    wr_t = [sb(f"wr{i}", [128, 4096], BF16) for i in range(NW)]
    wr = [B(f"wr{i}", wr_t[i]) for i in range(NW)]
    gT_t = sb("gT", [128, 22, 512], BF16)
    gT = B("gT", gT_t)
    bA_t = sb("bA", [128, 8, 512], BF16)
    bA = B("bA", bA_t)
    bB_t = sb("bB", [128, 4096], BF16)
    bB = B("bB", bB_t)
    vn_t = bB_t[:, :].rearrange("p (a b) -> p a b", b=1024)
    QT_t = bB_t[:, :].rearrange("p (a b) -> p a b", b=512)
    bC_t = sb("bC", [128, 8, 512], BF16)
    bC = B("bC", bC_t)
    s2_t = [sb(f"s2_{i}", [128, 512], F32) for i in range(6)]
    s2 = [B(f"s2_{i}", s2_t[i]) for i in range(6)]
    r2_t = [sb(f"r2_{i}", [128, 1024], F32) for i in range(2)]
    r2 = [B(f"r2_{i}", r2_t[i]) for i in range(2)]
    vtile_t = sb("vtile", [128, 1024], F32)
    vtile = B("vtile", vtile_t)
    sgw_t = vtile_t[:, :].rearrange("p (a b) -> p a b", b=128)
    sgw = vtile
    cosb_t = sb("cosb", [128, 512], F32)
    sinb_t = sb("sinb", [128, 512], F32)
    cosb = B("cosb", cosb_t)
    sinb = B("sinb", sinb_t)
    qks_t = [sb(f"qks{i}", [128, 512], BF16) for i in range(2)]
    qks = [B(f"qks{i}", qks_t[i]) for i in range(2)]
    vsl_t = [sb(f"vsl{i}", [128, 1024], BF16) for i in range(2)]
    vsl = [B(f"vsl{i}", vsl_t[i]) for i in range(2)]
    NKV = 4
    kv_t = [sb(f"kv{i}", [128, 1024], BF16) for i in range(NKV)]
    kv = [B(f"kv{i}", kv_t[i]) for i in range(NKV)]
    NPT = 3
    pt_t = [sb(f"pt{i}", [128, 2, 512], BF16) for i in range(NPT)]
    pt = [B(f"pt{i}", pt_t[i]) for i in range(NPT)]
    stat_t = sb("stat", [128, 4, 16], F32)
    stats = [B(f"stat{i}", stat_t[:, i, :]) for i in range(4)]
    cb_t = sb("cb", [128, 2, 128], BF16)
    cb = B("cb", cb_t)
    permb = cb_t[:, 0, :]
    onesb = cb_t[:, 1, :]
    hl_t = [sb(f"hl{i}", [128, 512], BF16) for i in range(4)]
    hl = [B(f"hl{i}", hl_t[i]) for i in range(4)]
    lamp_t = sb("lamp", [128, 128], F32)
    lamp = B("lamp", lamp_t)
    linit_t = sb("linit", [128, 2 * L], F32)
    linit = B("linit", linit_t)
    lnp = [B(f"lnp{i}", lnp_t[:, i, :]) for i in range(4)]

    pp_t = [st.enter_context(nc.psum_tensor(f"pp{i}", [128, 2, 512], F32)).ap() for i in range(4)]
    ps_t = [pp_t[i // 2][:, i % 2, :] for i in range(8)]
    ps = [B(f"ps{i}", ps_t[i]) for i in range(8)]
    psrr = [0]

    def next_ps():
        i = psrr[0] % 8
        psrr[0] += 1
        return ps[i]

    d_w = [B(f"d_w{l}") for l in range(L)]
    d_xres = [B(f"d_xres{n}") for n in range(NT)]
    d_q = [B(f"d_q{n}") for n in range(NT)]
    d_mag = [B(f"d_mag{n}") for n in range(NT)]
    d_gb = [B(f"d_gb{n}") for n in range(NT)]
    d_kl = [B(f"d_kl{h}") for h in range(8)]
    d_vl = [B(f"d_vl{h}") for h in range(8)]
    d_kg = [B(f"d_kg{h}") for h in range(8)]
    d_vg = [B(f"d_vg{h}") for h in range(8)]
    d_out = B("d_out")
    ccsem = Sem(P, "cc")
    d_dump = B("d_dump")

    def cast_weights(l):
        def cp(src, dst, off, n):
            rows = n // 1024
            r0 = 0
            while r0 < rows:
                r = min(8192, rows - r0)
                P.dma("gpsimd", dap(dst, off + r0 * 1024, [(1024, r), (1, 1024)]),
                      dap(src, off + r0 * 1024, [(1024, r), (1, 1024)]), writes=[d_w[l]], nowait_dst=True)
                r0 += r
        cp(w1_d, w1b, l * 2 * D * DFF, 2 * D * DFF)
        cp(w3_d, w3b, l * 2 * D * DFF, 2 * D * DFF)
        cp(w2_d, w2b, l * 2 * D * DFF, 2 * D * DFF)
        cp(w_in_d, wib, l * D * INW, D * INW)
        cp(w_br_d, wbrb, l * 2 * D * D, 2 * D * D)
        cp(w_out_d, wob, l * D * D, D * D)

    def ffn_plan(l, slot):
        base1 = (l * 2 + slot) * D * DFF
        for c in range(6):
            nf = 4 if c < 5 else 2
            yield (l, "w1b", base1 + c * 512, DFF, 8, nf * 128)
            yield (l, "w3b", base1 + c * 512, DFF, 8, nf * 128)
        for c in range(6):
            nf = 4 if c < 5 else 2
            yield (l, "w2b", (l * 2 + slot) * DFF * D + c * 512 * D, D, nf, 1024)

    def layer_plan(l):
        wbase = l * D * INW
        for n in range(NT):
            yield from ffn_plan(l, 0)
            for off in (0, 512, 1024, 1536):
                yield (l, "wib", wbase + off, INW, 8, 512)
            for hf in range(2):
                yield (l, "wib", wbase + 5120 + hf * 512, INW, 8, 512)
                yield (l, "wbrb", (l * 2 + 0) * D * D + hf * 512, D, 8, 512)
            for hf in range(2):
                yield (l, "wib", wbase + 6144 + hf * 512, INW, 8, 512)
            for which in range(2):
                for hf in range(2):
                    yield (l, "wib", wbase + 2048 + which * 1024 + hf * 512, INW, 8, 512)
            yield (l, "wib", wbase + 4096, INW, 8, 512)
            yield (l, "wib", wbase + 4608, INW, 8, 512)
        for n in range(NT):
            for hf in range(2):
                yield (l, "wbrb", (l * 2 + 1) * D * D + hf * 512, D, 8, 512)
            yield (l, "wob", l * D * D + 0, D, 8, 512)
            yield (l, "wob", l * D * D + 512, D, 8, 512)
            yield from ffn_plan(l, 1)

    wplan = [e for l in range(L) for e in layer_plan(l)]
    whandles = {"w1b": w1b, "w3b": w3b, "w2b": w2b, "wib": wib, "wbrb": wbrb, "wob": wob}
    wcount = [0]
    wissued = [0]
    WLOOK = NW - 2

    def wissue(k):
        l, hn, off, rowstride, nk, ncols = wplan[k]
        i = k % NW
        view = wr_t[i][:, 0:nk * ncols].rearrange("p (a b) -> p a b", b=ncols)
        P.dma("sync", view, dap(whandles[hn], off, [(rowstride, 128), (128 * rowstride, nk), (1, ncols)]),
              reads=[d_w[l]], writes=[wr[i]])

    def wload(l, handle, off, rowstride, nk, ncols):
        k = wcount[0]
        wcount[0] += 1
        if DEBUG_STAGE is None:
            e = wplan[k]
            assert e[0] == l and whandles[e[1]] is handle and e[2:] == (off, rowstride, nk, ncols), (k, e, l, off)
            while wissued[0] <= min(k + WLOOK, len(wplan) - 1):
                wissue(wissued[0])
                wissued[0] += 1
        else:
            wplan[k:k + 1] = [(l, [n_ for n_, h_ in whandles.items() if h_ is handle][0], off, rowstride, nk, ncols)]
            wissue(k)
        i = k % NW
        view = wr_t[i][:, 0:nk * ncols].rearrange("p (a b) -> p a b", b=ncols)
        return wr[i], view

    def mm(out_ap, lhsT, rhs, start, stop):
        return lambda e: e.matmul(out_ap, lhsT, rhs, start=start, stop=stop)

    def act(out, in_, func, **kw):
        return lambda e: e.activation(out=out, in_=in_, func=func, **kw)

    def tt(out, in0, in1, op):
        return lambda e: e.tensor_tensor(out=out, in0=in0, in1=in1, op=op)

    def stt(out, in0, scalar, in1, op0, op1):
        return lambda e: e.scalar_tensor_tensor(out=out, in0=in0, scalar=scalar, in1=in1, op0=op0, op1=op1)

    def transposes_to_xT(src_t, src_b, di):
        for dc in range(8):
            pb = next_ps()
            fns = [lambda e, s=s, dc=dc, pb=pb: e.transpose(pb.ap[:, s * 128:(s + 1) * 128],
                                                           src_t[:, s, dc * 128:(dc + 1) * 128], ident)
                   for s in range(4)]
            P.op("tensor", fns, reads=[src_b, consts], writes=[pb])
            kw = dict(writes=[xT[di]]) if dc == 0 else dict(accum=[xT[di]])
            P.op("scalar", act(xT_t[di][:, dc, :], pb.ap, AF.Copy), reads=[pb], **kw)

    lncnt = [0]

    def layernorm(src_b, src_ap, dst_b, dst_ap, gi, bi, eps_ap, dst_first):
        sb_ = stats[lncnt[0] % 4]
        lncnt[0] += 1
        s_ap = sb_.ap
        P.op("vector", [lambda e: e.bn_stats(out=s_ap[:, 0:6], in_=src_ap[:, 0:512]),
                        lambda e: e.bn_stats(out=s_ap[:, 6:12], in_=src_ap[:, 512:1024])],
             reads=[src_b], writes=[sb_])
        P.op("vector", lambda e: e.bn_aggr(out=s_ap[:, 12:14], in_=s_ap[:, 0:12]), reads=[sb_], writes=[sb_])
        P.op("scalar", act(s_ap[:, 14:15], s_ap[:, 13:14], AF.Sqrt, bias=eps_ap, scale=1.0),
             reads=[small, sb_], writes=[sb_])
        P.op("vector", lambda e: e.reciprocal(out=s_ap[:, 14:15], in_=s_ap[:, 14:15]), reads=[sb_], writes=[sb_])
        P.op("vector", stt(s_ap[:, 15:16], s_ap[:, 12:13], -1.0, s_ap[:, 14:15], ALU.mult, ALU.mult),
             reads=[sb_], writes=[sb_])
        P.op("scalar", act(src_ap, src_ap, AF.Identity, scale=s_ap[:, 14:15], bias=s_ap[:, 15:16]),
             reads=[src_b, sb_], writes=[src_b])
        P.op("gpsimd", tt(src_ap, src_ap, lnp_t[:, gi, :], ALU.mult), reads=[src_b, lnp[gi]], writes=[src_b])
        kw = dict(writes=[dst_b]) if dst_first else dict(accum=[dst_b])
        P.op("gpsimd", tt(dst_ap, src_ap, lnp_t[:, bi, :], ALU.add), reads=[src_b, lnp[bi]], **kw)

    def ffn(l, slot, xi, res_b, res_t, dst_b, dst_t, gi, bi):
        base1 = (l * 2 + slot) * D * DFF
        first = True
        for c in range(6):
            nf = 4 if c < 5 else 2
            w1buf, w1v = wload(l, w1b, base1 + c * 512, DFF, 8, nf * 128)
            w3buf, w3v = wload(l, w3b, base1 + c * 512, DFF, 8, nf * 128)
            for j in range(nf):
                fc = c * 4 + j
                p1 = next_ps()
                p3 = next_ps()
                P.op("tensor", [mm(p1.ap, w1v[:, k, j * 128:(j + 1) * 128], xT_t[xi][:, k, :], k == 0, k == 7)
                                for k in range(8)], reads=[w1buf, xT[xi]], writes=[p1])
                P.op("tensor", [mm(p3.ap, w3v[:, k, j * 128:(j + 1) * 128], xT_t[xi][:, k, :], k == 0, k == 7)
                                for k in range(8)], reads=[w3buf, xT[xi]], writes=[p3])
                sbf = s2[fc % 4]
                P.op("scalar", act(sbf.ap, p1.ap, AF.Silu), reads=[p1], writes=[sbf])
                kw = dict(writes=[gT]) if first else dict(accum=[gT])
                first = False
                P.op("vector", tt(gT_t[:, fc, :], sbf.ap, p3.ap, ALU.mult), reads=[sbf, p3], **kw)
        for c in range(6):
            nf = 4 if c < 5 else 2
            w2buf, w2v = wload(l, w2b, (l * 2 + slot) * DFF * D + c * 512 * D, D, nf, 1024)
            for s in range(4):
                for hh in range(2):
                    pb = ps[s * 2 + hh]
                    fns = [mm(pb.ap, gT_t[:, c * 4 + j, s * 128:(s + 1) * 128], w2v[:, j, hh * 512:(hh + 1) * 512],
                              c == 0 and j == 0, c == 5 and j == nf - 1) for j in range(nf)]
                    if c == 0:
                        P.op("tensor", fns, reads=[w2buf, gT], writes=[pb])
                    else:
                        P.op("tensor", fns, reads=[w2buf, gT], accum=[pb])
        for s in range(4):
            rb = r2[s % 2]
            for hh in range(2):
                pb = ps[s * 2 + hh]
                kw = dict(writes=[rb]) if hh == 0 else dict(accum=[rb])
                P.op("vector", stt(rb.ap[:, hh * 512:(hh + 1) * 512], res_t[:, s, hh * 512:(hh + 1) * 512],
                                   2.0 * ALPHA, pb.ap, ALU.mult, ALU.add), reads=[res_b, pb], **kw)
            layernorm(rb, rb.ap, dst_b, dst_t[:, s, :], gi, bi, small_t[:, 1:2], dst_first=(s == 0))

    def load_lnp(parts):
        for i, (h, off) in enumerate(parts):
            P.dma("sync", lnp_t[:, i, :], dap(h, off, [(0, 128), (1, 1024)]), writes=[lnp[i]])

    P.dma("sync", consts_t, const_d.ap(), writes=[consts])
    P.dma("sync", mask32_t, mask_d.ap(), writes=[mask32])
    P.op("vector", lambda e: e.tensor_copy(out=maskb_t, in_=mask32_t), reads=[mask32], writes=[maskb])
    P.op("gpsimd", [lambda e: e.memset(small_t[:, 0:1], EPS), lambda e: e.memset(small_t[:, 1:2], 4 * EPS)],
         writes=[small])
    P.dma("sync", linit_t, dap(linit_d, 0, [(0, 128), (1, 2 * L)]), writes=[linit])
    P.op("vector", lambda e: e.tensor_copy(out=cb_t, in_=consts_t[:, 1:3, :]), reads=[consts], writes=[cb])

    def split_sum(out_b, lhsT_b16, src_ap, src_bs, i0):
        hi, lo = hl[i0], hl[i0 + 1]
        P.op("scalar", act(hi.ap, src_ap, AF.Copy), reads=src_bs, writes=[hi])
        P.op("vector", tt(lo.ap, src_ap, hi.ap, ALU.subtract), reads=list(src_bs) + [hi], writes=[lo])
        P.op("tensor", [mm(out_b.ap, lhsT_b16, hi.ap, True, False), mm(out_b.ap, lhsT_b16, lo.ap, False, True)],
             reads=[cb, hi, lo], writes=[out_b])
    for l in range(L):
        cast_weights(l)

    cur_n = [0]

    def stage(name, src_t=None, src_b=None):
        if DEBUG_STAGE == name or DEBUG_STAGE == f"{name}@{cur_n[0]}":
            if src_t is not None:
                P.dma("gpsimd", dap(out_d, 0, [(D, 128), (128 * D, 4), (1, D)]), src_t, reads=[src_b], sem_of=src_b, writes=[d_out], nowait_dst=True)
            raise _Stop()

    def emit_layer(l):
        src_h = x_in if l == 0 else xres
        load_lnp([(ln_g_d, (l * 3 + 0) * D), (ln_b_d, (l * 3 + 0) * D), (sgu_ln_g_d, l * D), (sgu_ln_b_d, l * D)])
        with nc.allow_non_contiguous_dma(reason="small param loads"):
            P.dma("sync", sgw_t, dap(sgu_w_d, l * 8 * 128 * 128, [(128, 128), (128 * 128, 8), (1, 128)]), writes=[sgw])
            P.dma("sync", bS_t, dap(sgu_b_d, l * 8 * 128, [(0, 128), (128, 8), (1, 128)]), writes=[bS])
            P.dma("sync", gateb_t, dap(gate_b_d, l * 2 * D, [(1, 128), (128, 16)]), writes=[gateb], allow_slow_non_contiguous=True)
            P.dma("sync", lam_t, dap(lam_d, l * 256, [(0, 128), (1, 256)]), writes=[lamb])
            P.dma("sync", small_t[:, 2:3], dap(diff_g_d, l * 128, [(1, 128), (1, 1)]), writes=[small])
        for g in range(8):
            pb = next_ps()
            P.op("tensor", lambda e, g=g, pb=pb: e.transpose(pb.ap[:, 0:128], sgw_t[:, g, :], ident),
                 reads=[sgw, consts], writes=[pb])
            sbf = s2[g % 4]
            P.op("scalar", act(sbf.ap[:, 0:128], pb.ap[:, 0:128], AF.Copy), reads=[pb], writes=[sbf])
            kw = dict(writes=[wmT]) if g == 0 else dict(accum=[wmT])
            P.op("gpsimd", lambda e, g=g, sbf=sbf: e.affine_select(
                out=wmT_t[:, g, :], in_=sbf.ap[:, 0:128], pattern=[[1, 128]], base=0, channel_multiplier=-1,
                compare_op=ALU.is_ge, fill=0.0), reads=[sbf], **kw)
        P.op("vector", [tt(lamp_t[:, 0:64], lam_t[:, 0:64], lam_t[:, 64:128], ALU.mult),
                        tt(lamp_t[:, 64:128], lam_t[:, 128:192], lam_t[:, 192:256], ALU.mult)],
             reads=[lamb], writes=[lamp])
        P.op("scalar", [act(lamp_t[:, 0:64], lamp_t[:, 0:64], AF.Identity, accum_out=small_t[:, 5:6]),
                        act(lamp_t[:, 64:128], lamp_t[:, 64:128], AF.Identity, accum_out=small_t[:, 6:7])],
             reads=[lamp], writes=[small, lamp])
        P.op("scalar", act(small_t[:, 7:9], small_t[:, 5:7], AF.Exp), reads=[small], writes=[small])
        P.op("vector", tt(small_t[:, 9:10], small_t[:, 7:8], small_t[:, 8:9], ALU.subtract), reads=[small], writes=[small])
        P.op("vector", stt(small_t[:, 4:5], small_t[:, 9:10], -1.0, linit_t[:, 2 * l:2 * l + 1], ALU.mult, ALU.subtract),
             reads=[small, linit], writes=[small])
        P.op("vector", tt(small_t[:, 3:4], small_t[:, 2:3], linit_t[:, 2 * l + 1:2 * l + 2], ALU.mult),
             reads=[small, linit], writes=[small])
        neglam = small_t[:, 4:5]
        gcoef = small_t[:, 3:4]

        for n in range(NT):
            cur_n[0] = n
            tsl = slice(n * 512, (n + 1) * 512)
            P.dma("sync", xa_t, dap(src_h, n * 512 * D, [(D, 128), (128 * D, 4), (1, D)]),
                  reads=[d_xres[n]] if l > 0 else [], writes=[xa])
            P.dma("sync", cosb_t, cos_d.ap()[:, tsl], writes=[cosb])
            P.dma("sync", sinb_t, sin_d.ap()[:, tsl], writes=[sinb])
            stage('A', xa_t, xa)
            transposes_to_xT(xa_t, xa, 0)
            ffn(l, 0, 0, xa, xa_t, xb, xb_t, 0, 1)
            stage('B', xb_t, xb)
            P.dma("gpsimd", dap(xres, n * 512 * D, [(D, 128), (128 * D, 4), (1, D)]), xb_t, reads=[xb], sem_of=xb, writes=[d_xres[n]])
            transposes_to_xT(xb_t, xb, 1)
            X1 = xT_t[1]
            X1b = xT[1]
            wbase = l * D * INW
            for c2 in range(2):
                wb_, wv = wload(l, wib, wbase + 0 + c2 * 512, INW, 8, 512)
                for j in range(4):
                    c = c2 * 4 + j
                    pb = next_ps()
                    P.op("tensor", [mm(pb.ap, wv[:, k, j * 128:(j + 1) * 128], X1[:, k, :], k == 0, k == 7) for k in range(8)],
                         reads=[wb_, X1b], writes=[pb])
                    kw = dict(writes=[bA]) if c == 0 else dict(accum=[bA])
                    P.op("scalar", act(bA_t[:, c, :], pb.ap, AF.Gelu), reads=[pb], **kw)
            stage('C1', xb_t, xb)
            wv0b, wv0 = wload(l, wib, wbase + 1024, INW, 8, 512)
            wv1b, wv1 = wload(l, wib, wbase + 1536, INW, 8, 512)
            for s in range(4):
                for hh, (wb_, wv) in enumerate([(wv0b, wv0), (wv1b, wv1)]):
                    pb = next_ps()
                    P.op("tensor", [mm(pb.ap, X1[:, k, s * 128:(s + 1) * 128], wv[:, k, :], k == 0, k == 7) for k in range(8)],
                         reads=[wb_, X1b], writes=[pb])
                    kw = dict(writes=[vtile]) if hh == 0 else dict(accum=[vtile])
                    P.op("scalar", act(vtile_t[:, hh * 512:(hh + 1) * 512], pb.ap, AF.Gelu), reads=[pb], **kw)
                layernorm(vtile, vtile_t, bB, vn_t[:, s, :], 2, 3, small_t[:, 0:1], dst_first=(s == 0))
            stage('C2', xb_t, xb)
            for g in range(8):
                pb = next_ps()
                P.op("tensor", [mm(pb.ap[:, s * 128:(s + 1) * 128], vn_t[:, s, g * 128:(g + 1) * 128], wmT_t[:, g, :], True, True)
                                for s in range(4)], reads=[bB, wmT], writes=[pb])
                sbf = s2[g % 4]
                bsb = bass.AP(bS_t.tensor, g * 128, [[1024, 128], [0, 4], [1, 128]])
                P.op("vector", tt(sbf.ap[:, :].rearrange("p (a b) -> p a b", b=128),
                                  pb.ap[:, :].rearrange("p (a b) -> p a b", b=128), bsb, ALU.add),
                     reads=[pb, bS], writes=[sbf])
                kw = dict(writes=[bC]) if g == 0 else dict(accum=[bC])
                P.op("gpsimd", tt(bC_t[:, g, :], sbf.ap, bA_t[:, g, :], ALU.mult), reads=[sbf, bA], **kw)
            stage('C3', xb_t, xb)
            for hf in range(2):
                wgb, wg = wload(l, wib, wbase + 5120 + hf * 512, INW, 8, 512)
                wab, wa = wload(l, wbrb, (l * 2 + 0) * D * D + hf * 512, D, 8, 512)
                for j in range(4):
                    c = hf * 4 + j
                    pg = next_ps()
                    P.op("tensor", [mm(pg.ap, wg[:, k, j * 128:(j + 1) * 128], X1[:, k, :], k == 0, k == 7) for k in range(8)],
                         reads=[wgb, X1b], writes=[pg])
                    gs = s2[(2 * c) % 6]
                    P.op("scalar", act(gs.ap, pg.ap, AF.Sigmoid, bias=gateb_t[:, c:c + 1], scale=1.0),
                         reads=[pg, gateb], writes=[gs])
                    pm = next_ps()
                    P.op("tensor", [mm(pm.ap, wa[:, k, j * 128:(j + 1) * 128], bC_t[:, k, :], k == 0, k == 7) for k in range(8)],
                         reads=[wab, bC], writes=[pm])
                    P.op("vector", tt(gs.ap, gs.ap, pm.ap, ALU.mult), reads=[gs, pm], writes=[gs])
                    P.dma("gpsimd", dap(mag_s, c * 128 * T + n * 512, [(T, 128), (1, 512)]), gs.ap, reads=[gs], sem_of=gs,
                          writes=[d_mag[n]], nowait_dst=(c > 0))
            stage('C4', xb_t, xb)
            for hf in range(2):
                wgb, wg = wload(l, wib, wbase + 6144 + hf * 512, INW, 8, 512)
                for j in range(4):
                    c = hf * 4 + j
                    pg = next_ps()
                    P.op("tensor", [mm(pg.ap, wg[:, k, j * 128:(j + 1) * 128], X1[:, k, :], k == 0, k == 7) for k in range(8)],
                         reads=[wgb, X1b], writes=[pg])
                    gs = s2[(2 * c + 1) % 6]
                    P.op("scalar", act(gs.ap, pg.ap, AF.Sigmoid, bias=gateb_t[:, 8 + c:9 + c], scale=1.0),
                         reads=[pg, gateb], writes=[gs])
                    P.dma("gpsimd", dap(gb_s, c * 128 * T + n * 512, [(T, 128), (1, 512)]), gs.ap, reads=[gs], sem_of=gs,
                          writes=[d_gb[n]], nowait_dst=(c > 0))
            stage('C5', xb_t, xb)
            for which in range(2):
                for hf in range(2):
                    wqb, wq = wload(l, wib, wbase + 2048 + which * 1024 + hf * 512, INW, 8, 512)
                    for j in range(4):
                        h = hf * 4 + j
                        pq = next_ps()
                        P.op("tensor", [mm(pq.ap, wq[:, k, j * 128:(j + 1) * 128], X1[:, k, :], k == 0, k == 7) for k in range(8)],
                             reads=[wqb, X1b], writes=[pq])
                        psw = next_ps()
                        split_sum(psw, permb, pq.ap, [pq], 0)
                        t1 = s2[(2 * h) % 6]
                        t2 = s2[(2 * h + 1) % 6]
                        P.op("vector", tt(t1.ap, pq.ap, cosb_t, ALU.mult), reads=[pq, cosb], writes=[t1])
                        P.op("vector", tt(t2.ap, psw.ap, sinb_t, ALU.mult), reads=[psw, sinb], writes=[t2])
                        qo = qks[h % 2]
                        P.op("gpsimd", tt(qo.ap, t1.ap, t2.ap, ALU.add), reads=[t1, t2], writes=[qo])
                        if which == 0:
                            P.dma("sync", dap(q_s, h * 128 * T + n * 512, [(T, 128), (1, 512)]), qo.ap, reads=[qo], sem_of=qo,
                                  writes=[d_q[n]], nowait_dst=(h > 0))
                        else:
                            P.dma("sync", dap(kT_l[h], n * 512, [(T, 128), (1, 512)]), qo.ap, reads=[qo], sem_of=qo,
                                  writes=[d_kl[h]], nowait_dst=(n > 0))
            stage('C6', xb_t, xb)
            wv0b, wv0 = wload(l, wib, wbase + 4096, INW, 8, 512)
            wv1b, wv1 = wload(l, wib, wbase + 4608, INW, 8, 512)
            for s in range(4):
                vo = vsl[s % 2]
                for hh, (wb_, wv) in enumerate([(wv0b, wv0), (wv1b, wv1)]):
                    pb = next_ps()
                    P.op("tensor", [mm(pb.ap, X1[:, k, s * 128:(s + 1) * 128], wv[:, k, :], k == 0, k == 7) for k in range(8)],
                         reads=[wb_, X1b], writes=[pb])
                    kw = dict(writes=[vo]) if hh == 0 else dict(accum=[vo])
                    P.op("scalar", act(vo.ap[:, hh * 512:(hh + 1) * 512], pb.ap, AF.Copy), reads=[pb], **kw)
                for h in range(8):
                    P.dma("sync", dap(v_l[h], n * 128 * 512 + s * 128, [(512, 128), (1, 128)]),
                          vo.ap[:, h * 128:(h + 1) * 128], reads=[vo], sem_of=vo, writes=[d_vl[h]],
                          nowait_dst=not (n == 0 and s == 0))

        cur_n[0] = -1
        stage('C', xb_t, xb)
        cc_list = []
        for h in range(8):
            cc_list.append((kT_l[h], kT_g[h], d_kl[h], d_kg[h]))
            cc_list.append((v_l[h], v_g[h], d_vl[h], d_vg[h]))
        tk = None
        for (src, dst, sbuf_, dbuf_) in cc_list:
            P.wait("gpsimd", sbuf_.r_waits() + dbuf_.w_waits())
            tk = ccsem.inc(1, "cc")
            P.ops["gpsimd"].append(("o", (lambda e, src=src, dst=dst: e.collective_compute(
                "AllGather", ALU.bypass, replica_groups=[[0, 1, 2, 3], [4, 5, 6, 7]],
                ins=[src.ap()], outs=[dst.ap()])), tk[0], 1))
            sbuf_.add_reader(tk)
        for (src, dst, sbuf_, dbuf_) in cc_list:
            sbuf_.add_reader(tk)
            dbuf_.set_writer(tk)

        stage('D', xb_t, xb)
        load_lnp([(ln_g_d, (l * 3 + 1) * D), (ln_b_d, (l * 3 + 1) * D), (ln_g_d, (l * 3 + 2) * D), (ln_b_d, (l * 3 + 2) * D)])
        O1, O2, L1, L2 = ps[0], ps[1], ps[2], ps[3]
        kvglob = [0]
        ptglob = [0]
        for n in range(NT):
            cur_n[0] = n
            P.dma("sync", xa_t, dap(xres, n * 512 * D, [(D, 128), (128 * D, 4), (1, D)]), reads=[d_xres[n]], writes=[xa])
            P.dma("sync", QT_t, dap(q_s, n * 512, [(T, 128), (128 * T, 8), (1, 512)]), reads=[d_q[n]], writes=[bB])
            def head_epilogue(h):
                e0, e1, e2, e3 = s2[0], s2[1], s2[2], s2[3]
                split_sum(L1, onesb, L1.ap, [L1], 0)
                split_sum(L2, onesb, L2.ap, [L2], 2)
                P.op("vector", [lambda e: e.reciprocal(out=e0.ap, in_=L1.ap), lambda e: e.reciprocal(out=e1.ap, in_=L2.ap)],
                     reads=[L1, L2], writes=[e0, e1])
                P.op("vector", [tt(e2.ap, O1.ap, e0.ap, ALU.mult), tt(e3.ap, O2.ap, e1.ap, ALU.mult)],
                     reads=[O1, O2, e0, e1], writes=[e2, e3])
                P.op("vector", stt(e2.ap, e3.ap, neglam, e2.ap, ALU.mult, ALU.add), reads=[e2, e3, small], writes=[e2])
                P.op("scalar", act(e3.ap, e2.ap, AF.Square), reads=[e2], writes=[e3])
                split_sum(L1, onesb, e3.ap, [e3], 0)
                P.op("scalar", act(e3.ap, L1.ap, AF.Sqrt, bias=small_t[:, 0:1], scale=1.0 / 128.0), reads=[L1, small], writes=[e3])
                P.op("vector", lambda e: e.reciprocal(out=e3.ap, in_=e3.ap), reads=[e3], writes=[e3])
                kw = dict(writes=[bA]) if h == 0 else dict(accum=[bA])
                P.op("vector", stt(bA_t[:, h, :], e2.ap, gcoef, e3.ap, ALU.mult, ALU.mult), reads=[e2, e3, small], **kw)
            chunks = []
            steps = []
            for h in range(8):
                cl = [(r, tl, None) for tl in range(n) for r in range(4)] + [(r, n, r) for r in range(4)]
                for ci, (r, tl, mt) in enumerate(cl):
                    chunks.append((h, r, tl))
                    for a in range(4):
                        steps.append(dict(h=h, c=len(chunks) - 1, a=a, qlo=0 if mt is None else a * 128, mt=mt,
                                          first=(ci == 0 and a == 0), last=(ci == len(cl) - 1 and a == 3)))
            kvbase = kvglob[0]
            kvglob[0] += len(chunks)
            issued = [0]

            def ensure_loaded(upto):
                while issued[0] <= min(upto, len(chunks) - 1):
                    c = issued[0]
                    h, r, tl = chunks[c]
                    i = (kvbase + c) % NKV
                    P.dma("sync", kv_t[i][:, 0:512], dap(kT_g[h], (r * 128) * T + tl * 512, [(T, 128), (1, 512)]),
                          reads=[d_kg[h]], writes=[kv[i]])
                    P.dma("sync", kv_t[i][:, 512:1024], dap(v_g[h], ((r * NT + tl) * 128) * 512, [(512, 128), (1, 512)]),
                          reads=[d_vg[h]], writes=[kv[i]], nowait_dst=True)
                    issued[0] += 1

            def emit_qk(si):
                sp = steps[si]
                if sp["a"] == 0:
                    ensure_loaded(sp["c"] + 2)
                i = (kvbase + sp["c"]) % NKV
                a, qlo, h = sp["a"], sp["qlo"], sp["h"]
                SS = pp_t[2 + si % 2]
                P.op("tensor", [mm(SS[:, 0, qlo:512], kv_t[i][0:64, a * 128:(a + 1) * 128], QT_t[0:64, h, qlo:512], True, True),
                                mm(SS[:, 1, qlo:512], kv_t[i][64:128, a * 128:(a + 1) * 128], QT_t[64:128, h, qlo:512], True, True)],
                     reads=[kv[i], bB], writes=[ps[4 + (si % 2) * 2], ps[5 + (si % 2) * 2]])

            def emit_rest(si):
                sp = steps[si]
                i = (kvbase + sp["c"]) % NKV
                a, qlo, h, mt = sp["a"], sp["qlo"], sp["h"], sp["mt"]
                SS = pp_t[2 + si % 2]
                S1, S2 = ps[4 + (si % 2) * 2], ps[5 + (si % 2) * 2]
                pti = (ptglob[0] + si) % NPT
                ptb, ptt = pt[pti], pt_t[pti]
                P.op("scalar", act(ptt[:, :, qlo:512], SS[:, :, qlo:512], AF.Exp, scale=0.125), reads=[S1, S2], writes=[ptb])
                if mt is not None:
                    P.op("gpsimd", [tt(ptt[:, m_, qlo:qlo + 128], ptt[:, m_, qlo:qlo + 128], maskb_t[:, mt, :], ALU.mult)
                                    for m_ in range(2)], reads=[ptb, maskb], writes=[ptb])
                vsl_ = kv_t[i][:, 512 + a * 128:512 + (a + 1) * 128]
                LL = pp_t[1]
                if sp["first"]:
                    P.op("tensor", [mm(O1.ap[:, qlo:512], vsl_, ptt[:, 0, qlo:512], True, False),
                                    mm(O2.ap[:, qlo:512], vsl_, ptt[:, 1, qlo:512], True, False)],
                         reads=[kv[i], ptb], writes=[O1, O2])
                    P.op("vector", lambda e, ptt=ptt: e.tensor_copy(out=LL, in_=ptt), reads=[ptb], writes=[L1, L2])
                else:
                    P.op("tensor", [mm(O1.ap[:, qlo:512], vsl_, ptt[:, 0, qlo:512], False, False),
                                    mm(O2.ap[:, qlo:512], vsl_, ptt[:, 1, qlo:512], False, False)],
                         reads=[kv[i], ptb], accum=[O1, O2])
                    P.op("vector", tt(LL[:, :, qlo:512], LL[:, :, qlo:512], ptt[:, :, qlo:512], ALU.add),
                         reads=[ptb, L1, L2], writes=[L1, L2])
                if sp["last"]:
                    head_epilogue(h)

            emit_qk(0)
            for si in range(len(steps)):
                if si + 1 < len(steps):
                    emit_qk(si + 1)
                emit_rest(si)
            ptglob[0] += len(steps)
            if DEBUG_DUMP and n == 0:
                P.dma("sync", dump_hb.ap(), bA_t[:, :, :].rearrange("p a b -> p (a b)"), reads=[bA], sem_of=bA, writes=[d_dump], nowait_dst=True)
            stage('E', xb_t, xb)
            for hf in range(2):
                wbb, wb2 = wload(l, wbrb, (l * 2 + 1) * D * D + hf * 512, D, 8, 512)
                for j in range(4):
                    c = hf * 4 + j
                    pm = next_ps()
                    P.op("tensor", [mm(pm.ap, wb2[:, k, j * 128:(j + 1) * 128], bA_t[:, k, :], k == 0, k == 7) for k in range(8)],
                         reads=[wbb, bA], writes=[pm])
                    gsl = s2[(2 * c) % 6]
                    msl = s2[(2 * c + 1) % 6]
                    P.dma("sync", gsl.ap, dap(gb_s, c * 128 * T + n * 512, [(T, 128), (1, 512)]), reads=[d_gb[n]], writes=[gsl])
                    P.dma("sync", msl.ap, dap(mag_s, c * 128 * T + n * 512, [(T, 128), (1, 512)]), reads=[d_mag[n]], writes=[msl])
                    P.op("vector", tt(gsl.ap, gsl.ap, pm.ap, ALU.mult), reads=[gsl, pm], writes=[gsl])
                    kw = dict(writes=[bC]) if c == 0 else dict(accum=[bC])
                    P.op("gpsimd", tt(bC_t[:, c, :], gsl.ap, msl.ap, ALU.add), reads=[gsl, msl], **kw)
            wo0b, wo0 = wload(l, wob, l * D * D + 0, D, 8, 512)
            wo1b, wo1 = wload(l, wob, l * D * D + 512, D, 8, 512)
            for s in range(4):
                rb = r2[s % 2]
                for hh, (wb_, wv) in enumerate([(wo0b, wo0), (wo1b, wo1)]):
                    pb = next_ps()
                    P.op("tensor", [mm(pb.ap, bC_t[:, k, s * 128:(s + 1) * 128], wv[:, k, :], k == 0, k == 7) for k in range(8)],
                         reads=[wb_, bC], writes=[pb])
                    kw = dict(writes=[rb]) if hh == 0 else dict(accum=[rb])
                    P.op("vector", stt(rb.ap[:, hh * 512:(hh + 1) * 512], xa_t[:, s, hh * 512:(hh + 1) * 512], ALPHA, pb.ap,
                                       ALU.mult, ALU.add), reads=[xa, pb], **kw)
                layernorm(rb, rb.ap, xb, xb_t[:, s, :], 0, 1, small_t[:, 0:1], dst_first=(s == 0))
            transposes_to_xT(xb_t, xb, 0)
            ffn(l, 1, 0, xb, xb_t, xb, xb_t, 2, 3)
            if l == L - 1:
                P.dma("gpsimd", dap(out_d, n * 512 * D, [(D, 128), (128 * D, 4), (1, D)]), xb_t, reads=[xb],
                      sem_of=xb, writes=[d_out], nowait_dst=True)
            else:
                P.dma("gpsimd", dap(xres, n * 512 * D, [(D, 128), (128 * D, 4), (1, D)]), xb_t, reads=[xb], sem_of=xb, writes=[d_xres[n]])

    try:
        for l in range(L):
            emit_layer(l)
    except _Stop:
        pass
    P.wait("gpsimd", d_out.r_waits() + d_dump.r_waits())
    P.op("gpsimd", lambda e: e.memset(small_t[:, 10:11], 0.0), writes=[])

    with nc.Block() as block:
        @block.sync
        def _(e):
            P.replay("sync", e)

        @block.scalar
        def _(e):
            P.replay("scalar", e)

        @block.vector
        def _(e):
            P.replay("vector", e)

        @block.gpsimd
        def _(e):
            P.replay("gpsimd", e)

        @block.tensor
        def _(e):
            P.replay("tensor", e)
    return nc


def _lambda_init(layer):
    import math
    return 0.8 - 0.6 * math.exp(-0.3 * layer)


def _core_tables(j, NT, seq):
    T = NT * 512
    i = np.arange(T)
    pos = ((4 * (i // 128) + j) * 128 + i % 128).astype(np.float32)
    inv_freq = (np.float32(10000.0) ** (-np.arange(0, 64, 2, dtype=np.float32) / np.float32(64))).astype(np.float32)
    ang = pos[None, :] * inv_freq[:, None]
    cos = np.cos(ang).astype(np.float32)
    sin = np.sin(ang).astype(np.float32)
    idx = np.arange(128) % 32
    return np.ascontiguousarray(cos[idx]), np.ascontiguousarray(sin[idx])


def _consts():
    c = np.zeros((128, 3, 128), np.float32)
    c[:, 0, :] = np.eye(128, dtype=np.float32)
    for p in range(128):
        if p % 64 < 32:
            c[p + 32, 1, p] = -1.0
        else:
            c[p - 32, 1, p] = 1.0
    c[:, 2, :] = 1.0
    return c


def _masks(j):
    m = np.zeros((128, 4, 128), np.float32)
    for t in range(4):
        if t < j:
            m[:, t, :] = 1.0
        elif t == j:
            m[:, t, :] = np.triu(np.ones((128, 128), np.float32))
    return m


_PROG_CACHE = {}


def run_layers(x, params, layers, NT):
    L = len(layers)
    key = (L, NT)
    if key not in _PROG_CACHE:
        _PROG_CACHE[key] = build_program(L, NT)
    nc = _PROG_CACHE[key]
    S = 4 * NT * 512
    NBLK = S // 128
    consts = _consts()
    linit = np.array([[_lambda_init(l), 1.0 - _lambda_init(l)] for l in layers], np.float32)
    sl = {k: np.ascontiguousarray(v[list(layers)]) for k, v in params.items()}
    in_maps = []
    for c in range(8):
        b, j = c // 4, c % 4
        xc = np.ascontiguousarray(x[b].reshape(NBLK, 128, D)[j::4].reshape(NT * 512, D))
        cosT, sinT = _core_tables(j, NT, S)
        m = dict(sl)
        m.update({"x": xc, "cosT": cosT, "sinT": sinT, "masks": _masks(j), "consts": consts, "linit": linit})
        in_maps.append(m)
    res = run_bass_kernel_spmd(nc, in_maps, core_ids=list(range(8)))
    LAST_RES[0] = res
    out = np.empty((2, S, D), np.float32)
    for c in range(8):
        b, j = c // 4, c % 4
        out[b].reshape(NBLK, 128, D)[j::4] = res.results[c]["out"].reshape(NT * 4, 128, D)
    return out


FUSED = True


def kernel(x, w_in, gate_b, sgu_ln_g, sgu_ln_b, sgu_w, sgu_b, lam, diff_ln_g, w_branch, w_out,
           ffn_w1, ffn_w3, ffn_w2, ln_g, ln_b):
    params = dict(w_in=w_in, gate_b=gate_b, sgu_ln_g=sgu_ln_g, sgu_ln_b=sgu_ln_b, sgu_w=sgu_w, sgu_b=sgu_b,
                  lam=lam, diff_ln_g=diff_ln_g, w_branch=w_branch, w_out=w_out, ffn_w1=ffn_w1, ffn_w3=ffn_w3,
                  ffn_w2=ffn_w2, ln_g=ln_g, ln_b=ln_b)
    params = {k: np.asarray(v, np.float32) for k, v in params.items()}
    x = np.asarray(x, np.float32)
    NT = x.shape[1] // 2048
    depth = params["w_in"].shape[0]
    if FUSED:
        return run_layers(x, params, list(range(depth)), NT)
    for l in range(depth):
        x = run_layers(x, params, [l], NT)
    return x
```

```python
import numpy as np
from contextlib import ExitStack
import concourse.bass as bass
import concourse.mybir as mybir
from concourse.bass_utils import run_bass_kernel_spmd

F32 = mybir.dt.float32
BF16 = mybir.dt.bfloat16
AF = mybir.ActivationFunctionType
ALU = mybir.AluOpType

D = 1024
DFF = 2816
NFC = 22
INW = 7168
DEPTH = 4
SEQ = 16384
BATCH = 2
ALPHA = (2.0 * DEPTH) ** 0.25
EPS = 1e-5
ENGS = ("sync", "scalar", "vector", "gpsimd", "tensor")
SAME_ENGINE_SYNC = True


class Sem:
    LIMIT = 30000

    def __init__(self, P, name):
        self.P, self.name, self.n = P, name, 0
        self.new()

    def new(self):
        self.h = self.P.stack.enter_context(self.P.nc.semaphore(f"{self.name}_{self.n}"))
        self.n += 1
        self.val = 0

    def inc(self, amt, eng):
        if self.val + amt > self.LIMIT:
            self.new()
        self.val += amt
        return (self.h, self.val, eng)


class Arena:
    def __init__(self):
        self.bufs = []


class Buf:
    def __init__(self, P, name, ap=None, arena=None, phase=0):
        self.P, self.name, self.ap = P, name, ap
        self.wset = {}
        self.readers = {}
        self.dsem = None
        self.arena, self.phase = arena, phase
        if arena is not None:
            arena.bufs.append(self)

    def sem(self):
        if self.dsem is None:
            self.dsem = Sem(self.P, "d_" + self.name)
        return self.dsem

    def add_reader(self, tk):
        k = id(tk[0])
        if k not in self.readers or self.readers[k][1] < tk[1]:
            self.readers[k] = tk

    def set_writer(self, tk):
        self.wset = {id(tk[0]): tk}
        self.readers = {}

    def add_writer(self, tk):
        k = id(tk[0])
        if k not in self.wset or self.wset[k][1] < tk[1]:
            self.wset[k] = tk

    def r_waits(self):
        return list(self.wset.values())

    def w_waits(self):
        w = list(self.readers.values()) + list(self.wset.values())
        if self.arena is not None:
            for o in self.arena.bufs:
                if o.phase != self.phase:
                    w += list(o.readers.values()) + list(o.wset.values())
        return w


class Prog:
    def __init__(self, nc):
        self.nc = nc
        self.stack = ExitStack()
        self.ops = {e: [] for e in ENGS}
        self.waited = {e: {} for e in ENGS}
        self.prog = {e: Sem(self, "p_" + e) for e in ENGS}
        self.nbuf = 0

    def _waits(self, eng, tks):
        for tk in tks:
            if tk is None:
                continue
            h, v, teng = tk
            if teng == eng and (eng == "tensor" or not SAME_ENGINE_SYNC):
                continue
            k = id(h)
            if self.waited[eng].get(k, 0) >= v:
                continue
            self.waited[eng][k] = v
            self.ops[eng].append(("w", h, v))

    def op(self, eng, fns, reads=(), writes=(), accum=()):
        if not isinstance(fns, (list, tuple)):
            fns = [fns]
        tks = []
        for b in reads:
            tks += b.r_waits()
        for b in accum:
            tks += b.r_waits()
        for b in writes:
            tks += b.w_waits()
        self._waits(eng, tks)
        tk = self.prog[eng].inc(1, eng)
        for f in fns[:-1]:
            self.ops[eng].append(("o", f, None, 0))
        self.ops[eng].append(("o", fns[-1], tk[0], 1))
        for b in reads:
            b.add_reader(tk)
        for b in list(writes) + list(accum):
            b.set_writer(tk)
        return tk

    def dma(self, eng, out, in_, reads=(), writes=(), nowait_dst=False, sem_of=None, **kw):
        tks = []
        for b in reads:
            tks += b.r_waits()
        for b in writes:
            if not nowait_dst:
                tks += b.w_waits()
        self._waits(eng, tks)
        wb = sem_of if sem_of is not None else writes[0]
        tk = wb.sem().inc(16, "dma")
        self.ops[eng].append(("o", lambda e: e.dma_start(out=out, in_=in_, **kw), tk[0], 16))
        for b in reads:
            b.add_reader(tk)
        for b in writes:
            if nowait_dst:
                b.add_writer(tk)
            else:
                b.set_writer(tk)
        return tk

    def wait(self, eng, tks):
        self._waits(eng, tks)

    def replay(self, eng, e):
        for o in self.ops[eng]:
            if o[0] == "w":
                e.wait_ge(o[1], o[2])
            else:
                ins = o[1](e)
                if o[2] is not None:
                    ins.then_inc(o[2], o[3])


class _Stop(Exception):
    pass


DEBUG_STAGE = None
DEBUG_DUMP = False
LAST_RES = [None]


def build_program(L, NT):
    T = NT * 512
    NB = NT * 4
    nc = bass.Bass("TRN2", target_bir_lowering=False)
    P = Prog(nc)
    st = P.stack

    def dram(name, shape, dtype, kind=None):
        if kind is None:
            return nc.dram_tensor(name, shape, dtype)
        return nc.dram_tensor(name, shape, dtype, kind=kind)

    def dap(h, off, pat):
        return bass.AP(h, off, [list(p) for p in pat])

    EI = "ExternalInput"
    x_in = dram("x", [T, D], F32, EI)
    out_d = dram("out", [T, D], F32, "ExternalOutput")
    w_in_d = dram("w_in", [L, D, INW], F32, EI)
    gate_b_d = dram("gate_b", [L, 2, D], F32, EI)
    sgu_ln_g_d = dram("sgu_ln_g", [L, D], F32, EI)
    sgu_ln_b_d = dram("sgu_ln_b", [L, D], F32, EI)
    sgu_w_d = dram("sgu_w", [L, 8, 128, 128], F32, EI)
    sgu_b_d = dram("sgu_b", [L, 8, 128], F32, EI)
    lam_d = dram("lam", [L, 4, 64], F32, EI)
    diff_g_d = dram("diff_ln_g", [L, 128], F32, EI)
    w_br_d = dram("w_branch", [L, 2, D, D], F32, EI)
    w_out_d = dram("w_out", [L, D, D], F32, EI)
    w1_d = dram("ffn_w1", [L, 2, D, DFF], F32, EI)
    w3_d = dram("ffn_w3", [L, 2, D, DFF], F32, EI)
    w2_d = dram("ffn_w2", [L, 2, DFF, D], F32, EI)
    ln_g_d = dram("ln_g", [L, 3, D], F32, EI)
    ln_b_d = dram("ln_b", [L, 3, D], F32, EI)
    cos_d = dram("cosT", [128, T], F32, EI)
    sin_d = dram("sinT", [128, T], F32, EI)
    mask_d = dram("masks", [128, 4, 128], F32, EI)
    const_d = dram("consts", [128, 3, 128], F32, EI)
    linit_d = dram("linit", [L, 2], F32, EI)

    wib = dram("wib", [L, D, INW], BF16)
    w1b = dram("w1b", [L, 2, D, DFF], BF16)
    w3b = dram("w3b", [L, 2, D, DFF], BF16)
    w2b = dram("w2b", [L, 2, DFF, D], BF16)
    wbrb = dram("wbrb", [L, 2, D, D], BF16)
    wob = dram("wob", [L, D, D], BF16)
    xres = dram("xres", [T, D], F32)
    DK = "ExternalOutput" if DEBUG_DUMP else None
    q_s = dram("q_s", [8, 128, T], BF16, DK)
    kT_l = [dram(f"kT_l{n}", [8 * 128, 512], BF16) for n in range(NT)]
    v_l = [dram(f"v_l{n}", [8 * 128, 512], BF16) for n in range(NT)]
    kT_g = [dram(f"kT_g{n}", [4 * 8 * 128, 512], BF16) for n in range(NT)]
    v_g = [dram(f"v_g{n}", [4 * 8 * 128, 512], BF16) for n in range(NT)]
    mag_s = dram("mag_s", [8, 128, T], F32, DK)
    gb_s = dram("gb_s", [8, 128, T], F32, DK)
    if DEBUG_DUMP:
        dump_hb = dram("dump_hb", [128, 8 * 512], BF16, DK)

    def sb(name, shape, dtype):
        return st.enter_context(nc.sbuf_tensor("s_" + name, shape, dtype)).ap()

    def B(name, ap=None, arena=None, phase=0):
        return Buf(P, name, ap, arena, phase)

    consts_t = sb("consts", [128, 3, 128], F32)
    consts = B("consts", consts_t)
    ident = consts_t[:, 0, :]
    perm = consts_t[:, 1, :]
    ones32 = consts_t[:, 2, :]
    maskb_t = sb("maskb", [128, 4, 128], BF16)
    maskb = B("maskb", maskb_t)
    lnp_t = sb("lnp", [128, 4, 1024], F32)
    lnp = B("lnp", lnp_t)
    wmT_t = sb("wmT", [128, 8, 128], BF16)
    wmT = B("wmT", wmT_t)
    bS_t = sb("bS", [128, 8, 128], F32)
    bS = B("bS", bS_t)
    small_t = sb("small", [128, 64], F32)
    small = B("small", small_t)
    gateb_t = sb("gateb", [128, 16], F32)
    gateb = B("gateb", gateb_t)
    lam_t = sb("lamt", [128, 256], F32)
    lamb = B("lamb", lam_t)
    xa_t = sb("xa", [128, 4, 1024], F32)
    xb_t = sb("xb", [128, 4, 1024], F32)
    xa = B("xa", xa_t)
    xb = B("xb", xb_t)
    xT_t = [sb("xT0", [128, 8, 512], BF16)] * 2
    xT = [B("xT0", xT_t[0])] * 2
    NW = 4
    wr_t = [sb(f"wr{i}", [128, 4096], BF16) for i in range(NW)]
    wr = [B(f"wr{i}", wr_t[i]) for i in range(NW)]
    gT_t = sb("gT", [128, 22, 512], BF16)
    gT = B("gT", gT_t)
    bA_t = sb("bA", [128, 8, 512], BF16)
    bA = B("bA", bA_t)
    bB_t = sb("bB", [128, 4096], BF16)
    bB = B("bB", bB_t)
    vn_t = bB_t[:, :].rearrange("p (a b) -> p a b", b=1024)
    QT_t = bB_t[:, :].rearrange("p (a b) -> p a b", b=512)
    bC_t = sb("bC", [128, 8, 512], BF16)
    bC = B("bC", bC_t)
    s2_t = [sb(f"s2_{i}", [128, 512], F32) for i in range(6)]
    s2 = [B(f"s2_{i}", s2_t[i]) for i in range(6)]
    r2_t = [sb(f"r2_{i}", [128, 1024], F32) for i in range(2)]
    r2 = [B(f"r2_{i}", r2_t[i]) for i in range(2)]
    vtile_t = sb("vtile", [128, 1024], F32)
    vtile = B("vtile", vtile_t)
    sgw_t = vtile_t[:, :].rearrange("p (a b) -> p a b", b=128)
    sgw = vtile
    mask32_t = vtile_t[:, 0:512].rearrange("p (a b) -> p a b", b=128)
    mask32 = vtile
    L2s_t = sb("L2s", [128, 512], F32)
    L2s = B("L2s", L2s_t)
    cosb_t = sb("cosb", [128, 512], F32)
    sinb_t = sb("sinb", [128, 512], F32)
    cosb = B("cosb", cosb_t)
    sinb = B("sinb", sinb_t)
    qks_t = [sb(f"qks{i}", [128, 512], BF16) for i in range(2)]
    qks = [B(f"qks{i}", qks_t[i]) for i in range(2)]
    vsl_t = [sb(f"vsl{i}", [128, 1024], BF16) for i in range(2)]
    vsl = [B(f"vsl{i}", vsl_t[i]) for i in range(2)]
    NKV = 6
    kv_t = [sb(f"kv{i}", [128, 1024], BF16) for i in range(NKV)]
    kv = [B(f"kv{i}", kv_t[i]) for i in range(NKV)]
    NPT = 4
    LS = 384
    pt_t = [sb(f"pt{i}", [128, 2, 512], BF16) for i in range(NPT)]
    pt = [B(f"pt{i}", pt_t[i]) for i in range(NPT)]
    stat_t = sb("stat", [128, 4, 16], F32)
    stats = [B(f"stat{i}", stat_t[:, i, :]) for i in range(4)]
    cb_t = sb("cb", [128, 2, 128], BF16)
    cb = B("cb", cb_t)
    permb = cb_t[:, 0, :]
    onesb = cb_t[:, 1, :]
    hl_t = [sb(f"hl{i}", [128, 512], BF16) for i in range(4)]
    hl = [B(f"hl{i}", hl_t[i]) for i in range(4)]
    lamp_t = sb("lamp", [128, 128], F32)
    lamp = B("lamp", lamp_t)
    linit_t = sb("linit", [128, 2 * L], F32)
    linit = B("linit", linit_t)
    lnp = [B(f"lnp{i}", lnp_t[:, i, :]) for i in range(4)]

    pp_t = [st.enter_context(nc.psum_tensor(f"pp{i}", [128, 2, 512], F32)).ap() for i in range(4)]
    ps_t = [pp_t[i // 2][:, i % 2, :] for i in range(8)]
    ps = [B(f"ps{i}", ps_t[i]) for i in range(8)]
    psrr = [0]

    def next_ps():
        i = psrr[0] % 8
        psrr[0] += 1
        return ps[i]

    d_w = {}

    def dwb(hn, l, sub=0):
        k = (hn, l, sub)
        if k not in d_w:
            d_w[k] = B(f"d_w_{hn}_{l}_{sub}")
        return d_w[k]
    d_xres = [B(f"d_xres{n}") for n in range(NT)]
    d_q = [B(f"d_q{n}") for n in range(NT)]
    d_mag = [B(f"d_mag{n}") for n in range(NT)]
    d_gb = [B(f"d_gb{n}") for n in range(NT)]
    d_kl = [B(f"d_kl{n}") for n in range(NT)]
    d_vl = [B(f"d_vl{n}") for n in range(NT)]
    d_kg = [B(f"d_kg{n}") for n in range(NT)]
    d_vg = [B(f"d_vg{n}") for n in range(NT)]
    d_out = B("d_out")
    ccsem = Sem(P, "cc")
    d_dump = B("d_dump")

    def cast_weights(l):
        def cp(src, dst, off, n, buf):
            rows = n // 1024
            r0 = 0
            while r0 < rows:
                r = min(8192, rows - r0)
                P.dma("gpsimd", dap(dst, off + r0 * 1024, [(1024, r), (1, 1024)]),
                      dap(src, off + r0 * 1024, [(1024, r), (1, 1024)]), writes=[buf], nowait_dst=True)
                r0 += r
        for slot in range(2):
            o = (l * 2 + slot) * D * DFF
            cp(w1_d, w1b, o, D * DFF, dwb("w1b", l, slot))
            cp(w3_d, w3b, o, D * DFF, dwb("w3b", l, slot))
            cp(w2_d, w2b, o, D * DFF, dwb("w2b", l, slot))
            if slot == 0:
                cp(w_in_d, wib, l * D * INW, D * INW, dwb("wib", l))
                for br in range(2):
                    cp(w_br_d, wbrb, (l * 2 + br) * D * D, D * D, dwb("wbrb", l, br))
                cp(w_out_d, wob, l * D * D, D * D, dwb("wob", l))

    def wbuf_of(e):
        l, hn, off = e[0], e[1], e[2]
        if hn in ("w1b", "w3b", "w2b"):
            return dwb(hn, l, (off - l * 2 * D * DFF) // (D * DFF))
        if hn == "wbrb":
            return dwb(hn, l, (off - l * 2 * D * D) // (D * D))
        return dwb(hn, l)

    def ffn_plan(l, slot):
        base1 = (l * 2 + slot) * D * DFF
        for c in range(6):
            nf = 4 if c < 5 else 2
            yield (l, "w1b", base1 + c * 512, DFF, 8, nf * 128)
            yield (l, "w3b", base1 + c * 512, DFF, 8, nf * 128)
        for c in range(6):
            nf = 4 if c < 5 else 2
            yield (l, "w2b", (l * 2 + slot) * DFF * D + c * 512 * D, D, nf, 1024)

    def layer_plan(l):
        wbase = l * D * INW
        for n in range(NT):
            yield from ffn_plan(l, 0)
            for off in (0, 512, 1024, 1536):
                yield (l, "wib", wbase + off, INW, 8, 512)
            for hf in range(2):
                yield (l, "wib", wbase + 5120 + hf * 512, INW, 8, 512)
                yield (l, "wbrb", (l * 2 + 0) * D * D + hf * 512, D, 8, 512)
            for hf in range(2):
                yield (l, "wib", wbase + 6144 + hf * 512, INW, 8, 512)
            for which in range(2):
                for hf in range(2):
                    yield (l, "wib", wbase + 2048 + which * 1024 + hf * 512, INW, 8, 512)
            yield (l, "wib", wbase + 4096, INW, 8, 512)
            yield (l, "wib", wbase + 4608, INW, 8, 512)
        for n in range(NT):
            for hf in range(2):
                yield (l, "wbrb", (l * 2 + 1) * D * D + hf * 512, D, 8, 512)
            yield (l, "wob", l * D * D + 0, D, 8, 512)
            yield (l, "wob", l * D * D + 512, D, 8, 512)
            yield from ffn_plan(l, 1)

    wplan = [e for l in range(L) for e in layer_plan(l)]
    whandles = {"w1b": w1b, "w3b": w3b, "w2b": w2b, "wib": wib, "wbrb": wbrb, "wob": wob}
    wcount = [0]
    wissued = [0]
    WLOOK = NW - 2

    def wissue(k):
        l, hn, off, rowstride, nk, ncols = wplan[k]
        i = k % NW
        view = wr_t[i][:, 0:nk * ncols].rearrange("p (a b) -> p a b", b=ncols)
        P.dma("sync", view, dap(whandles[hn], off, [(rowstride, 128), (128 * rowstride, nk), (1, ncols)]),
              reads=[wbuf_of(wplan[k])], writes=[wr[i]])

    def wload(l, handle, off, rowstride, nk, ncols):
        k = wcount[0]
        wcount[0] += 1
        if DEBUG_STAGE is None:
            e = wplan[k]
            assert e[0] == l and whandles[e[1]] is handle and e[2:] == (off, rowstride, nk, ncols), (k, e, l, off)
            while wissued[0] <= min(k + WLOOK, len(wplan) - 1):
                wissue(wissued[0])
                wissued[0] += 1
        else:
            wplan[k:k + 1] = [(l, [n_ for n_, h_ in whandles.items() if h_ is handle][0], off, rowstride, nk, ncols)]
            wissue(k)
        i = k % NW
        view = wr_t[i][:, 0:nk * ncols].rearrange("p (a b) -> p a b", b=ncols)
        return wr[i], view

    def mm(out_ap, lhsT, rhs, start, stop):
        return lambda e: e.matmul(out_ap, lhsT, rhs, start=start, stop=stop)

    def act(out, in_, func, **kw):
        return lambda e: e.activation(out=out, in_=in_, func=func, **kw)

    def tt(out, in0, in1, op):
        return lambda e: e.tensor_tensor(out=out, in0=in0, in1=in1, op=op)

    def stt(out, in0, scalar, in1, op0, op1):
        return lambda e: e.scalar_tensor_tensor(out=out, in0=in0, scalar=scalar, in1=in1, op0=op0, op1=op1)

    def transposes_to_xT(src_t, src_b, di):
        for dc in range(8):
            pb = next_ps()
            fns = [lambda e, s=s, dc=dc, pb=pb: e.transpose(pb.ap[:, s * 128:(s + 1) * 128],
                                                           src_t[:, s, dc * 128:(dc + 1) * 128], ident)
                   for s in range(4)]
            P.op("tensor", fns, reads=[src_b, consts], writes=[pb])
            kw = dict(writes=[xT[di]]) if dc == 0 else dict(accum=[xT[di]])
            P.op("scalar", act(xT_t[di][:, dc, :], pb.ap, AF.Copy), reads=[pb], **kw)

    lncnt = [0]

    def layernorm(src_b, src_ap, dst_b, dst_ap, gi, bi, eps_ap, dst_first):
        sb_ = stats[lncnt[0] % 4]
        lncnt[0] += 1
        s_ap = sb_.ap
        P.op("vector", [lambda e: e.bn_stats(out=s_ap[:, 0:6], in_=src_ap[:, 0:512]),
                        lambda e: e.bn_stats(out=s_ap[:, 6:12], in_=src_ap[:, 512:1024])],
             reads=[src_b], writes=[sb_])
        P.op("vector", lambda e: e.bn_aggr(out=s_ap[:, 12:14], in_=s_ap[:, 0:12]), reads=[sb_], writes=[sb_])
        P.op("scalar", act(s_ap[:, 14:15], s_ap[:, 13:14], AF.Sqrt, bias=eps_ap, scale=1.0),
             reads=[small, sb_], writes=[sb_])
        P.op("vector", lambda e: e.reciprocal(out=s_ap[:, 14:15], in_=s_ap[:, 14:15]), reads=[sb_], writes=[sb_])
        P.op("vector", stt(s_ap[:, 15:16], s_ap[:, 12:13], -1.0, s_ap[:, 14:15], ALU.mult, ALU.mult),
             reads=[sb_], writes=[sb_])
        P.op("scalar", act(src_ap, src_ap, AF.Identity, scale=s_ap[:, 14:15], bias=s_ap[:, 15:16]),
             reads=[src_b, sb_], writes=[src_b])
        P.op("gpsimd", tt(src_ap, src_ap, lnp_t[:, gi, :], ALU.mult), reads=[src_b, lnp[gi]], writes=[src_b])
        kw = dict(writes=[dst_b]) if dst_first else dict(accum=[dst_b])
        P.op("gpsimd", tt(dst_ap, src_ap, lnp_t[:, bi, :], ALU.add), reads=[src_b, lnp[bi]], **kw)

    def ffn(l, slot, xi, res_b, res_t, dst_b, dst_t, gi, bi):
        base1 = (l * 2 + slot) * D * DFF
        first = True
        for c in range(6):
            nf = 4 if c < 5 else 2
            w1buf, w1v = wload(l, w1b, base1 + c * 512, DFF, 8, nf * 128)
            w3buf, w3v = wload(l, w3b, base1 + c * 512, DFF, 8, nf * 128)
            for j in range(nf):
                fc = c * 4 + j
                p1 = next_ps()
                p3 = next_ps()
                P.op("tensor", [mm(p1.ap, w1v[:, k, j * 128:(j + 1) * 128], xT_t[xi][:, k, :], k == 0, k == 7)
                                for k in range(8)], reads=[w1buf, xT[xi]], writes=[p1])
                P.op("tensor", [mm(p3.ap, w3v[:, k, j * 128:(j + 1) * 128], xT_t[xi][:, k, :], k == 0, k == 7)
                                for k in range(8)], reads=[w3buf, xT[xi]], writes=[p3])
                sbf = s2[fc % 4]
                P.op("scalar", act(sbf.ap, p1.ap, AF.Silu), reads=[p1], writes=[sbf])
                kw = dict(writes=[gT]) if first else dict(accum=[gT])
                first = False
                P.op("vector", tt(gT_t[:, fc, :], sbf.ap, p3.ap, ALU.mult), reads=[sbf, p3], **kw)
        for c in range(6):
            nf = 4 if c < 5 else 2
            w2buf, w2v = wload(l, w2b, (l * 2 + slot) * DFF * D + c * 512 * D, D, nf, 1024)
            for s in range(4):
                for hh in range(2):
                    pb = ps[s * 2 + hh]
                    fns = [mm(pb.ap, gT_t[:, c * 4 + j, s * 128:(s + 1) * 128], w2v[:, j, hh * 512:(hh + 1) * 512],
                              c == 0 and j == 0, c == 5 and j == nf - 1) for j in range(nf)]
                    if c == 0:
                        P.op("tensor", fns, reads=[w2buf, gT], writes=[pb])
                    else:
                        P.op("tensor", fns, reads=[w2buf, gT], accum=[pb])
        for s in range(4):
            rb = r2[s % 2]
            for hh in range(2):
                pb = ps[s * 2 + hh]
                kw = dict(writes=[rb]) if hh == 0 else dict(accum=[rb])
                P.op("vector", stt(rb.ap[:, hh * 512:(hh + 1) * 512], res_t[:, s, hh * 512:(hh + 1) * 512],
                                   2.0 * ALPHA, pb.ap, ALU.mult, ALU.add), reads=[res_b, pb], **kw)
            layernorm(rb, rb.ap, dst_b, dst_t[:, s, :], gi, bi, small_t[:, 1:2], dst_first=(s == 0))

    def load_lnp(parts):
        for i, (h, off) in enumerate(parts):
            P.dma("sync", lnp_t[:, i, :], dap(h, off, [(0, 128), (1, 1024)]), writes=[lnp[i]])

    P.dma("sync", consts_t, const_d.ap(), writes=[consts])
    P.dma("sync", mask32_t, mask_d.ap(), writes=[mask32])
    P.op("vector", lambda e: e.tensor_copy(out=maskb_t, in_=mask32_t), reads=[mask32], writes=[maskb])
    P.op("gpsimd", [lambda e: e.memset(small_t[:, 0:1], EPS), lambda e: e.memset(small_t[:, 1:2], 4 * EPS)],
         writes=[small])
    P.dma("sync", linit_t, dap(linit_d, 0, [(0, 128), (1, 2 * L)]), writes=[linit])
    P.op("vector", lambda e: e.tensor_copy(out=cb_t, in_=consts_t[:, 1:3, :]), reads=[consts], writes=[cb])

    def split_sum(out_b, lhsT_b16, src_ap, src_bs, i0):
        hi, lo = hl[i0], hl[i0 + 1]
        P.op("scalar", act(hi.ap, src_ap, AF.Copy), reads=src_bs, writes=[hi])
        P.op("vector", tt(lo.ap, src_ap, hi.ap, ALU.subtract), reads=list(src_bs) + [hi], writes=[lo])
        P.op("tensor", [mm(out_b.ap, lhsT_b16, hi.ap, True, False), mm(out_b.ap, lhsT_b16, lo.ap, False, True)],
             reads=[cb, hi, lo], writes=[out_b])
    for l in range(L):
        cast_weights(l)

    cur_n = [0]

    def stage(name, src_t=None, src_b=None):
        if DEBUG_STAGE == name or DEBUG_STAGE == f"{name}@{cur_n[0]}":
            if src_t is not None:
                P.dma("gpsimd", dap(out_d, 0, [(D, 128), (128 * D, 4), (1, D)]), src_t, reads=[src_b], sem_of=src_b, writes=[d_out], nowait_dst=True)
            raise _Stop()

    def emit_layer(l):
        src_h = x_in if l == 0 else xres
        load_lnp([(ln_g_d, (l * 3 + 0) * D), (ln_b_d, (l * 3 + 0) * D), (sgu_ln_g_d, l * D), (sgu_ln_b_d, l * D)])
        with nc.allow_non_contiguous_dma(reason="small param loads"):
            P.dma("sync", sgw_t, dap(sgu_w_d, l * 8 * 128 * 128, [(128, 128), (128 * 128, 8), (1, 128)]), writes=[sgw])
            P.dma("sync", bS_t, dap(sgu_b_d, l * 8 * 128, [(0, 128), (128, 8), (1, 128)]), writes=[bS])
            P.dma("sync", gateb_t, dap(gate_b_d, l * 2 * D, [(1, 128), (128, 16)]), writes=[gateb], allow_slow_non_contiguous=True)
            P.dma("sync", lam_t, dap(lam_d, l * 256, [(0, 128), (1, 256)]), writes=[lamb])
            P.dma("sync", small_t[:, 2:3], dap(diff_g_d, l * 128, [(1, 128), (1, 1)]), writes=[small])
        for g in range(8):
            pb = next_ps()
            P.op("tensor", lambda e, g=g, pb=pb: e.transpose(pb.ap[:, 0:128], sgw_t[:, g, :], ident),
                 reads=[sgw, consts], writes=[pb])
            sbf = s2[g % 4]
            P.op("scalar", act(sbf.ap[:, 0:128], pb.ap[:, 0:128], AF.Copy), reads=[pb], writes=[sbf])
            kw = dict(writes=[wmT]) if g == 0 else dict(accum=[wmT])
            P.op("gpsimd", lambda e, g=g, sbf=sbf: e.affine_select(
                out=wmT_t[:, g, :], in_=sbf.ap[:, 0:128], pattern=[[1, 128]], base=0, channel_multiplier=-1,
                compare_op=ALU.is_ge, fill=0.0), reads=[sbf], **kw)
        P.op("vector", [tt(lamp_t[:, 0:64], lam_t[:, 0:64], lam_t[:, 64:128], ALU.mult),
                        tt(lamp_t[:, 64:128], lam_t[:, 128:192], lam_t[:, 192:256], ALU.mult)],
             reads=[lamb], writes=[lamp])
        P.op("scalar", [act(lamp_t[:, 0:64], lamp_t[:, 0:64], AF.Identity, accum_out=small_t[:, 5:6]),
                        act(lamp_t[:, 64:128], lamp_t[:, 64:128], AF.Identity, accum_out=small_t[:, 6:7])],
             reads=[lamp], writes=[small, lamp])
        P.op("scalar", act(small_t[:, 7:9], small_t[:, 5:7], AF.Exp), reads=[small], writes=[small])
        P.op("vector", tt(small_t[:, 9:10], small_t[:, 7:8], small_t[:, 8:9], ALU.subtract), reads=[small], writes=[small])
        P.op("vector", stt(small_t[:, 4:5], small_t[:, 9:10], -1.0, linit_t[:, 2 * l:2 * l + 1], ALU.mult, ALU.subtract),
             reads=[small, linit], writes=[small])
        P.op("vector", tt(small_t[:, 3:4], small_t[:, 2:3], linit_t[:, 2 * l + 1:2 * l + 2], ALU.mult),
             reads=[small, linit], writes=[small])
        neglam = small_t[:, 4:5]
        gcoef = small_t[:, 3:4]

        cc_pending = []
        for n in range(NT):
            cur_n[0] = n
            tsl = slice(n * 512, (n + 1) * 512)
            P.dma("sync", xa_t, dap(src_h, n * 512 * D, [(D, 128), (128 * D, 4), (1, D)]),
                  reads=[d_xres[n]] if l > 0 else [], writes=[xa])
            P.dma("sync", cosb_t, cos_d.ap()[:, tsl], writes=[cosb])
            P.dma("sync", sinb_t, sin_d.ap()[:, tsl], writes=[sinb])
            stage('A', xa_t, xa)
            transposes_to_xT(xa_t, xa, 0)
            ffn(l, 0, 0, xa, xa_t, xb, xb_t, 0, 1)
            stage('B', xb_t, xb)
            P.dma("gpsimd", dap(xres, n * 512 * D, [(D, 128), (128 * D, 4), (1, D)]), xb_t, reads=[xb], sem_of=xb, writes=[d_xres[n]])
            transposes_to_xT(xb_t, xb, 1)
            X1 = xT_t[1]
            X1b = xT[1]
            wbase = l * D * INW
            for c2 in range(2):
                wb_, wv = wload(l, wib, wbase + 0 + c2 * 512, INW, 8, 512)
                for j in range(4):
                    c = c2 * 4 + j
                    pb = next_ps()
                    P.op("tensor", [mm(pb.ap, wv[:, k, j * 128:(j + 1) * 128], X1[:, k, :], k == 0, k == 7) for k in range(8)],
                         reads=[wb_, X1b], writes=[pb])
                    kw = dict(writes=[bA]) if c == 0 else dict(accum=[bA])
                    P.op("scalar", act(bA_t[:, c, :], pb.ap, AF.Gelu), reads=[pb], **kw)
            stage('C1', xb_t, xb)
            wv0b, wv0 = wload(l, wib, wbase + 1024, INW, 8, 512)
            wv1b, wv1 = wload(l, wib, wbase + 1536, INW, 8, 512)
            for s in range(4):
                for hh, (wb_, wv) in enumerate([(wv0b, wv0), (wv1b, wv1)]):
                    pb = next_ps()
                    P.op("tensor", [mm(pb.ap, X1[:, k, s * 128:(s + 1) * 128], wv[:, k, :], k == 0, k == 7) for k in range(8)],
                         reads=[wb_, X1b], writes=[pb])
                    kw = dict(writes=[vtile]) if hh == 0 else dict(accum=[vtile])
                    P.op("scalar", act(vtile_t[:, hh * 512:(hh + 1) * 512], pb.ap, AF.Gelu), reads=[pb], **kw)
                layernorm(vtile, vtile_t, bB, vn_t[:, s, :], 2, 3, small_t[:, 0:1], dst_first=(s == 0))
            stage('C2', xb_t, xb)
            for g in range(8):
                pb = next_ps()
                P.op("tensor", [mm(pb.ap[:, s * 128:(s + 1) * 128], vn_t[:, s, g * 128:(g + 1) * 128], wmT_t[:, g, :], True, True)
                                for s in range(4)], reads=[bB, wmT], writes=[pb])
                sbf = s2[g % 4]
                bsb = bass.AP(bS_t.tensor, g * 128, [[1024, 128], [0, 4], [1, 128]])
                P.op("vector", tt(sbf.ap[:, :].rearrange("p (a b) -> p a b", b=128),
                                  pb.ap[:, :].rearrange("p (a b) -> p a b", b=128), bsb, ALU.add),
                     reads=[pb, bS], writes=[sbf])
                kw = dict(writes=[bC]) if g == 0 else dict(accum=[bC])
                P.op("gpsimd", tt(bC_t[:, g, :], sbf.ap, bA_t[:, g, :], ALU.mult), reads=[sbf, bA], **kw)
            stage('C3', xb_t, xb)
            for hf in range(2):
                wgb, wg = wload(l, wib, wbase + 5120 + hf * 512, INW, 8, 512)
                wab, wa = wload(l, wbrb, (l * 2 + 0) * D * D + hf * 512, D, 8, 512)
                for j in range(4):
                    c = hf * 4 + j
                    pg = next_ps()
                    P.op("tensor", [mm(pg.ap, wg[:, k, j * 128:(j + 1) * 128], X1[:, k, :], k == 0, k == 7) for k in range(8)],
                         reads=[wgb, X1b], writes=[pg])
                    gs = s2[(2 * c) % 6]
                    P.op("scalar", act(gs.ap, pg.ap, AF.Sigmoid, bias=gateb_t[:, c:c + 1], scale=1.0),
                         reads=[pg, gateb], writes=[gs])
                    pm = next_ps()
                    P.op("tensor", [mm(pm.ap, wa[:, k, j * 128:(j + 1) * 128], bC_t[:, k, :], k == 0, k == 7) for k in range(8)],
                         reads=[wab, bC], writes=[pm])
                    P.op("vector", tt(gs.ap, gs.ap, pm.ap, ALU.mult), reads=[gs, pm], writes=[gs])
                    P.dma("gpsimd", dap(mag_s, c * 128 * T + n * 512, [(T, 128), (1, 512)]), gs.ap, reads=[gs], sem_of=gs,
                          writes=[d_mag[n]], nowait_dst=(c > 0))
            stage('C4', xb_t, xb)
            for hf in range(2):
                wgb, wg = wload(l, wib, wbase + 6144 + hf * 512, INW, 8, 512)
                for j in range(4):
                    c = hf * 4 + j
                    pg = next_ps()
                    P.op("tensor", [mm(pg.ap, wg[:, k, j * 128:(j + 1) * 128], X1[:, k, :], k == 0, k == 7) for k in range(8)],
                         reads=[wgb, X1b], writes=[pg])
                    gs = s2[(2 * c + 1) % 6]
                    P.op("scalar", act(gs.ap, pg.ap, AF.Sigmoid, bias=gateb_t[:, 8 + c:9 + c], scale=1.0),
                         reads=[pg, gateb], writes=[gs])
                    P.dma("gpsimd", dap(gb_s, c * 128 * T + n * 512, [(T, 128), (1, 512)]), gs.ap, reads=[gs], sem_of=gs,
                          writes=[d_gb[n]], nowait_dst=(c > 0))
            stage('C5', xb_t, xb)
            for which in range(2):
                for hf in range(2):
                    wqb, wq = wload(l, wib, wbase + 2048 + which * 1024 + hf * 512, INW, 8, 512)
                    for j in range(4):
                        h = hf * 4 + j
                        pq = next_ps()
                        P.op("tensor", [mm(pq.ap, wq[:, k, j * 128:(j + 1) * 128], X1[:, k, :], k == 0, k == 7) for k in range(8)],
                             reads=[wqb, X1b], writes=[pq])
                        psw = next_ps()
                        split_sum(psw, permb, pq.ap, [pq], 0)
                        t1 = s2[(2 * h) % 6]
                        t2 = s2[(2 * h + 1) % 6]
                        P.op("vector", tt(t1.ap, pq.ap, cosb_t, ALU.mult), reads=[pq, cosb], writes=[t1])
                        P.op("vector", tt(t2.ap, psw.ap, sinb_t, ALU.mult), reads=[psw, sinb], writes=[t2])
                        qo = qks[h % 2]
                        P.op("gpsimd", tt(qo.ap, t1.ap, t2.ap, ALU.add), reads=[t1, t2], writes=[qo])
                        if which == 0:
                            P.dma("sync", dap(q_s, h * 128 * T + n * 512, [(T, 128), (1, 512)]), qo.ap, reads=[qo], sem_of=qo,
                                  writes=[d_q[n]], nowait_dst=(h > 0))
                        else:
                            P.dma("sync", dap(kT_l[n], h * 128 * 512, [(512, 128), (1, 512)]), qo.ap, reads=[qo], sem_of=qo,
                                  writes=[d_kl[n]], nowait_dst=(h > 0))
            stage('C6', xb_t, xb)
            wv0b, wv0 = wload(l, wib, wbase + 4096, INW, 8, 512)
            wv1b, wv1 = wload(l, wib, wbase + 4608, INW, 8, 512)
            for s in range(4):
                vo = vsl[s % 2]
                for hh, (wb_, wv) in enumerate([(wv0b, wv0), (wv1b, wv1)]):
                    pb = next_ps()
                    P.op("tensor", [mm(pb.ap, X1[:, k, s * 128:(s + 1) * 128], wv[:, k, :], k == 0, k == 7) for k in range(8)],
                         reads=[wb_, X1b], writes=[pb])
                    kw = dict(writes=[vo]) if hh == 0 else dict(accum=[vo])
                    P.op("scalar", act(vo.ap[:, hh * 512:(hh + 1) * 512], pb.ap, AF.Copy), reads=[pb], **kw)
                P.dma("sync", dap(v_l[n], s * 128, [(512, 128), (128 * 512, 8), (1, 128)]),
                      vo.ap[:, :].rearrange("p (a b) -> p a b", b=128), reads=[vo], sem_of=vo, writes=[d_vl[n]],
                      nowait_dst=(s > 0))
            for (src, dst, sbuf_, dbuf_) in ((kT_l[n], kT_g[n], d_kl[n], d_kg[n]), (v_l[n], v_g[n], d_vl[n], d_vg[n])):
                P.wait("gpsimd", sbuf_.r_waits() + dbuf_.w_waits())
                tk = ccsem.inc(1, "cc")
                P.ops["gpsimd"].append(("o", (lambda e, src=src, dst=dst: e.collective_compute(
                    "AllGather", ALU.bypass, replica_groups=[[0, 1, 2, 3], [4, 5, 6, 7]],
                    ins=[src.ap()], outs=[dst.ap()])), tk[0], 1))
                sbuf_.add_reader(tk)
                cc_pending.append((dbuf_, tk))

        cur_n[0] = -1
        stage('C', xb_t, xb)
        last_tk = cc_pending[-1][1]
        for dbuf_, tk in cc_pending:
            dbuf_.set_writer(last_tk)
        del cc_pending[:]

        stage('D', xb_t, xb)
        load_lnp([(ln_g_d, (l * 3 + 1) * D), (ln_b_d, (l * 3 + 1) * D), (ln_g_d, (l * 3 + 2) * D), (ln_b_d, (l * 3 + 2) * D)])
        O1, O2, L1, L2 = ps[0], ps[1], ps[2], ps[3]
        kvglob = [0]
        ptglob = [0]
        for n in range(NT):
            cur_n[0] = n
            P.dma("sync", xa_t, dap(xres, n * 512 * D, [(D, 128), (128 * D, 4), (1, D)]), reads=[d_xres[n]], writes=[xa])
            P.dma("sync", QT_t, dap(q_s, n * 512, [(T, 128), (128 * T, 8), (1, 512)]), reads=[d_q[n]], writes=[bB])
            def head_epilogue(h):
                e0, e1, e2, e3 = s2[0], s2[1], s2[2], s2[3]
                split_sum(L1, onesb, L1.ap, [L1], 0)
                hi2, lo2 = hl[2], hl[3]
                P.op("scalar", [act(hi2.ap[:, 0:LS], L2s_t[:, 0:LS], AF.Copy), act(hi2.ap[:, LS:512], L2.ap[:, LS:512], AF.Copy)],
                     reads=[L2s, L2], writes=[hi2])
                P.op("vector", [tt(lo2.ap[:, 0:LS], L2s_t[:, 0:LS], hi2.ap[:, 0:LS], ALU.subtract),
                                tt(lo2.ap[:, LS:512], L2.ap[:, LS:512], hi2.ap[:, LS:512], ALU.subtract)],
                     reads=[L2s, L2, hi2], writes=[lo2])
                P.op("tensor", [mm(L2.ap, onesb, hi2.ap, True, False), mm(L2.ap, onesb, lo2.ap, False, True)],
                     reads=[cb, hi2, lo2], writes=[L2])
                P.op("vector", [lambda e: e.reciprocal(out=e0.ap, in_=L1.ap), lambda e: e.reciprocal(out=e1.ap, in_=L2.ap)],
                     reads=[L1, L2], writes=[e0, e1])
                P.op("vector", [tt(e2.ap, O1.ap, e0.ap, ALU.mult), tt(e3.ap, O2.ap, e1.ap, ALU.mult)],
                     reads=[O1, O2, e0, e1], writes=[e2, e3])
                P.op("vector", stt(e2.ap, e3.ap, neglam, e2.ap, ALU.mult, ALU.add), reads=[e2, e3, small], writes=[e2])
                P.op("scalar", act(e3.ap, e2.ap, AF.Square), reads=[e2], writes=[e3])
                split_sum(L1, onesb, e3.ap, [e3], 0)
                P.op("scalar", act(e3.ap, L1.ap, AF.Sqrt, bias=small_t[:, 0:1], scale=1.0 / 128.0), reads=[L1, small], writes=[e3])
                P.op("vector", lambda e: e.reciprocal(out=e3.ap, in_=e3.ap), reads=[e3], writes=[e3])
                kw = dict(writes=[bA]) if h == 0 else dict(accum=[bA])
                P.op("vector", stt(bA_t[:, h, :], e2.ap, gcoef, e3.ap, ALU.mult, ALU.mult), reads=[e2, e3, small], **kw)
            chunks = []
            steps = []
            for h in range(8):
                cl = [(r, tl, None) for tl in range(n) for r in range(4)] + [(r, n, r) for r in range(4)]
                for ci, (r, tl, mt) in enumerate(cl):
                    chunks.append((h, r, tl))
                    for a in range(4):
                        steps.append(dict(h=h, c=len(chunks) - 1, a=a, qlo=0 if mt is None else a * 128, mt=mt,
                                          first=(ci == 0 and a == 0), last=(ci == len(cl) - 1 and a == 3)))
            kvbase = kvglob[0]
            kvglob[0] += len(chunks)
            issued = [0]

            def ensure_loaded(upto):
                while issued[0] <= min(upto, len(chunks) - 1):
                    c = issued[0]
                    h, r, tl = chunks[c]
                    i = (kvbase + c) % NKV
                    P.dma("sync", kv_t[i][:, 0:512], dap(kT_g[tl], (r * 1024 + h * 128) * 512, [(512, 128), (1, 512)]),
                          reads=[d_kg[tl]], writes=[kv[i]])
                    P.dma("sync", kv_t[i][:, 512:1024], dap(v_g[tl], (r * 1024 + h * 128) * 512, [(512, 128), (1, 512)]),
                          reads=[d_vg[tl]], writes=[kv[i]], nowait_dst=True)
                    issued[0] += 1

            def emit_qk(si):
                sp = steps[si]
                if sp["a"] == 0:
                    ensure_loaded(sp["c"] + 2)
                i = (kvbase + sp["c"]) % NKV
                a, qlo, h = sp["a"], sp["qlo"], sp["h"]
                SS = pp_t[2 + si % 2]
                P.op("tensor", [mm(SS[:, 0, qlo:512], kv_t[i][0:64, a * 128:(a + 1) * 128], QT_t[0:64, h, qlo:512], True, True),
                                mm(SS[:, 1, qlo:512], kv_t[i][64:128, a * 128:(a + 1) * 128], QT_t[64:128, h, qlo:512], True, True)],
                     reads=[kv[i], bB], writes=[ps[4 + (si % 2) * 2], ps[5 + (si % 2) * 2]])

            def emit_rest(si):
                sp = steps[si]
                i = (kvbase + sp["c"]) % NKV
                a, qlo, h, mt = sp["a"], sp["qlo"], sp["h"], sp["mt"]
                SS = pp_t[2 + si % 2]
                S1, S2 = ps[4 + (si % 2) * 2], ps[5 + (si % 2) * 2]
                pti = (ptglob[0] + si) % NPT
                ptb, ptt = pt[pti], pt_t[pti]
                P.op("scalar", act(ptt[:, :, qlo:512], SS[:, :, qlo:512], AF.Exp, scale=0.125), reads=[S1, S2], writes=[ptb])
                if mt is not None:
                    P.op("gpsimd", [tt(ptt[:, m_, qlo:qlo + 128], ptt[:, m_, qlo:qlo + 128], maskb_t[:, mt, :], ALU.mult)
                                    for m_ in range(2)], reads=[ptb, maskb], writes=[ptb])
                vsl_ = kv_t[i][:, 512 + a * 128:512 + (a + 1) * 128]
                LL = pp_t[1]
                if sp["first"]:
                    P.op("tensor", [mm(O1.ap[:, qlo:512], vsl_, ptt[:, 0, qlo:512], True, False),
                                    mm(O2.ap[:, qlo:512], vsl_, ptt[:, 1, qlo:512], True, False)],
                         reads=[kv[i], ptb], writes=[O1, O2])
                    P.op("vector", [lambda e, ptt=ptt: e.tensor_copy(out=L1.ap, in_=ptt[:, 0, :]),
                                    lambda e, ptt=ptt: e.tensor_copy(out=L2.ap[:, LS:512], in_=ptt[:, 1, LS:512])],
                         reads=[ptb], writes=[L1, L2])
                    P.op("gpsimd", lambda e, ptt=ptt: e.tensor_copy(out=L2s_t[:, 0:LS], in_=ptt[:, 1, 0:LS]), reads=[ptb], writes=[L2s])
                else:
                    P.op("tensor", [mm(O1.ap[:, qlo:512], vsl_, ptt[:, 0, qlo:512], False, False),
                                    mm(O2.ap[:, qlo:512], vsl_, ptt[:, 1, qlo:512], False, False)],
                         reads=[kv[i], ptb], accum=[O1, O2])
                    dlo = max(qlo, LS)
                    P.op("vector", [tt(L1.ap[:, qlo:512], L1.ap[:, qlo:512], ptt[:, 0, qlo:512], ALU.add),
                                    tt(L2.ap[:, dlo:512], L2.ap[:, dlo:512], ptt[:, 1, dlo:512], ALU.add)],
                         reads=[ptb, L1, L2], writes=[L1, L2])
                    if qlo < LS:
                        P.op("gpsimd", tt(L2s_t[:, qlo:LS], L2s_t[:, qlo:LS], ptt[:, 1, qlo:LS], ALU.add),
                             reads=[ptb, L2s], writes=[L2s])
                if sp["last"]:
                    head_epilogue(h)

            emit_qk(0)
            for si in range(len(steps)):
                if si + 1 < len(steps):
                    emit_qk(si + 1)
                emit_rest(si)
            ptglob[0] += len(steps)
            if DEBUG_DUMP and n == 0:
                P.dma("sync", dump_hb.ap(), bA_t[:, :, :].rearrange("p a b -> p (a b)"), reads=[bA], sem_of=bA, writes=[d_dump], nowait_dst=True)
            stage('E', xb_t, xb)
            for hf in range(2):
                wbb, wb2 = wload(l, wbrb, (l * 2 + 1) * D * D + hf * 512, D, 8, 512)
                for j in range(4):
                    c = hf * 4 + j
                    pm = next_ps()
                    P.op("tensor", [mm(pm.ap, wb2[:, k, j * 128:(j + 1) * 128], bA_t[:, k, :], k == 0, k == 7) for k in range(8)],
                         reads=[wbb, bA], writes=[pm])
                    gsl = s2[(2 * c) % 6]
                    msl = s2[(2 * c + 1) % 6]
                    P.dma("sync", gsl.ap, dap(gb_s, c * 128 * T + n * 512, [(T, 128), (1, 512)]), reads=[d_gb[n]], writes=[gsl])
                    P.dma("sync", msl.ap, dap(mag_s, c * 128 * T + n * 512, [(T, 128), (1, 512)]), reads=[d_mag[n]], writes=[msl])
                    P.op("vector", tt(gsl.ap, gsl.ap, pm.ap, ALU.mult), reads=[gsl, pm], writes=[gsl])
                    kw = dict(writes=[bC]) if c == 0 else dict(accum=[bC])
                    P.op("gpsimd", tt(bC_t[:, c, :], gsl.ap, msl.ap, ALU.add), reads=[gsl, msl], **kw)
            wo0b, wo0 = wload(l, wob, l * D * D + 0, D, 8, 512)
            wo1b, wo1 = wload(l, wob, l * D * D + 512, D, 8, 512)
            for s in range(4):
                rb = r2[s % 2]
                for hh, (wb_, wv) in enumerate([(wo0b, wo0), (wo1b, wo1)]):
                    pb = next_ps()
                    P.op("tensor", [mm(pb.ap, bC_t[:, k, s * 128:(s + 1) * 128], wv[:, k, :], k == 0, k == 7) for k in range(8)],
                         reads=[wb_, bC], writes=[pb])
                    kw = dict(writes=[rb]) if hh == 0 else dict(accum=[rb])
                    P.op("vector", stt(rb.ap[:, hh * 512:(hh + 1) * 512], xa_t[:, s, hh * 512:(hh + 1) * 512], ALPHA, pb.ap,
                                       ALU.mult, ALU.add), reads=[xa, pb], **kw)
                layernorm(rb, rb.ap, xb, xb_t[:, s, :], 0, 1, small_t[:, 0:1], dst_first=(s == 0))
            transposes_to_xT(xb_t, xb, 0)
            ffn(l, 1, 0, xb, xb_t, xb, xb_t, 2, 3)
            if l == L - 1:
                P.dma("gpsimd", dap(out_d, n * 512 * D, [(D, 128), (128 * D, 4), (1, D)]), xb_t, reads=[xb],
                      sem_of=xb, writes=[d_out], nowait_dst=True)
            else:
                P.dma("gpsimd", dap(xres, n * 512 * D, [(D, 128), (128 * D, 4), (1, D)]), xb_t, reads=[xb], sem_of=xb, writes=[d_xres[n]])

    try:
        for l in range(L):
            emit_layer(l)
    except _Stop:
        pass
    P.wait("gpsimd", d_out.r_waits() + d_dump.r_waits())
    P.op("gpsimd", lambda e: e.memset(small_t[:, 10:11], 0.0), writes=[])

    with nc.Block() as block:
        @block.sync
        def _(e):
            P.replay("sync", e)

        @block.scalar
        def _(e):
            P.replay("scalar", e)

        @block.vector
        def _(e):
            P.replay("vector", e)

        @block.gpsimd
        def _(e):
            P.replay("gpsimd", e)

        @block.tensor
        def _(e):
            P.replay("tensor", e)
    return nc


def _lambda_init(layer):
    import math
    return 0.8 - 0.6 * math.exp(-0.3 * layer)


def _core_tables(j, NT, seq):
    T = NT * 512
    i = np.arange(T)
    pos = ((4 * (i // 128) + j) * 128 + i % 128).astype(np.float32)
    inv_freq = (np.float32(10000.0) ** (-np.arange(0, 64, 2, dtype=np.float32) / np.float32(64))).astype(np.float32)
    ang = pos[None, :] * inv_freq[:, None]
    cos = np.cos(ang).astype(np.float32)
    sin = np.sin(ang).astype(np.float32)
    idx = np.arange(128) % 32
    return np.ascontiguousarray(cos[idx]), np.ascontiguousarray(sin[idx])


def _consts():
    c = np.zeros((128, 3, 128), np.float32)
    c[:, 0, :] = np.eye(128, dtype=np.float32)
    for p in range(128):
        if p % 64 < 32:
            c[p + 32, 1, p] = -1.0
        else:
            c[p - 32, 1, p] = 1.0
    c[:, 2, :] = 1.0
    return c


def _masks(j):
    m = np.zeros((128, 4, 128), np.float32)
    for t in range(4):
        if t < j:
            m[:, t, :] = 1.0
        elif t == j:
            m[:, t, :] = np.triu(np.ones((128, 128), np.float32))
    return m


_PROG_CACHE = {}


def run_layers(x, params, layers, NT):
    L = len(layers)
    key = (L, NT)
    if key not in _PROG_CACHE:
        _PROG_CACHE[key] = build_program(L, NT)
    nc = _PROG_CACHE[key]
    S = 4 * NT * 512
    NBLK = S // 128
    consts = _consts()
    linit = np.array([[_lambda_init(l), 1.0 - _lambda_init(l)] for l in layers], np.float32)
    sl = {k: np.ascontiguousarray(v[list(layers)]) for k, v in params.items()}
    in_maps = []
    for c in range(8):
        b, j = c // 4, c % 4
        xc = np.ascontiguousarray(x[b].reshape(NBLK, 128, D)[j::4].reshape(NT * 512, D))
        cosT, sinT = _core_tables(j, NT, S)
        m = dict(sl)
        m.update({"x": xc, "cosT": cosT, "sinT": sinT, "masks": _masks(j), "consts": consts, "linit": linit})
        in_maps.append(m)
    res = run_bass_kernel_spmd(nc, in_maps, core_ids=list(range(8)))
    LAST_RES[0] = res
    out = np.empty((2, S, D), np.float32)
    for c in range(8):
        b, j = c // 4, c % 4
        out[b].reshape(NBLK, 128, D)[j::4] = res.results[c]["out"].reshape(NT * 4, 128, D)
    return out


FUSED = True


def kernel(x, w_in, gate_b, sgu_ln_g, sgu_ln_b, sgu_w, sgu_b, lam, diff_ln_g, w_branch, w_out,
           ffn_w1, ffn_w3, ffn_w2, ln_g, ln_b):
    params = dict(w_in=w_in, gate_b=gate_b, sgu_ln_g=sgu_ln_g, sgu_ln_b=sgu_ln_b, sgu_w=sgu_w, sgu_b=sgu_b,
                  lam=lam, diff_ln_g=diff_ln_g, w_branch=w_branch, w_out=w_out, ffn_w1=ffn_w1, ffn_w3=ffn_w3,
                  ffn_w2=ffn_w2, ln_g=ln_g, ln_b=ln_b)
    params = {k: np.asarray(v, np.float32) for k, v in params.items()}
    x = np.asarray(x, np.float32)
    NT = x.shape[1] // 2048
    depth = params["w_in"].shape[0]
    if FUSED:
        return run_layers(x, params, list(range(depth)), NT)
    for l in range(depth):
        x = run_layers(x, params, [l], NT)
    return x
```

```python
import numpy as np
from contextlib import ExitStack
import concourse.bass as bass
import concourse.mybir as mybir
from concourse.bass_utils import run_bass_kernel_spmd

F32 = mybir.dt.float32
BF16 = mybir.dt.bfloat16
AF = mybir.ActivationFunctionType
ALU = mybir.AluOpType

D = 1024
DFF = 2816
NFC = 22
INW = 7168
DEPTH = 4
SEQ = 16384
BATCH = 2
ALPHA = (2.0 * DEPTH) ** 0.25
EPS = 1e-5
ENGS = ("sync", "scalar", "vector", "gpsimd", "tensor")
SAME_ENGINE_SYNC = True


class Sem:
    LIMIT = 30000

    def __init__(self, P, name):
        self.P, self.name, self.n = P, name, 0
        self.new()

    def new(self):
        self.h = self.P.stack.enter_context(self.P.nc.semaphore(f"{self.name}_{self.n}"))
        self.n += 1
        self.val = 0

    def inc(self, amt, eng):
        if self.val + amt > self.LIMIT:
            self.new()
        self.val += amt
        return (self.h, self.val, eng)


class Arena:
    def __init__(self):
        self.bufs = []


class Buf:
    def __init__(self, P, name, ap=None, arena=None, phase=0):
        self.P, self.name, self.ap = P, name, ap
        self.wset = {}
        self.readers = {}
        self.dsem = None
        self.arena, self.phase = arena, phase
        if arena is not None:
            arena.bufs.append(self)

    def sem(self):
        if self.dsem is None:
            self.dsem = Sem(self.P, "d_" + self.name)
        return self.dsem

    def add_reader(self, tk):
        k = id(tk[0])
        if k not in self.readers or self.readers[k][1] < tk[1]:
            self.readers[k] = tk

    def set_writer(self, tk):
        self.wset = {id(tk[0]): tk}
        self.readers = {}

    def add_writer(self, tk):
        k = id(tk[0])
        if k not in self.wset or self.wset[k][1] < tk[1]:
            self.wset[k] = tk

    def r_waits(self):
        return list(self.wset.values())

    def w_waits(self):
        w = list(self.readers.values()) + list(self.wset.values())
        if self.arena is not None:
            for o in self.arena.bufs:
                if o.phase != self.phase:
                    w += list(o.readers.values()) + list(o.wset.values())
        return w


class Prog:
    def __init__(self, nc):
        self.nc = nc
        self.stack = ExitStack()
        self.ops = {e: [] for e in ENGS}
        self.waited = {e: {} for e in ENGS}
        self.prog = {e: Sem(self, "p_" + e) for e in ENGS}
        self.nbuf = 0

    def _waits(self, eng, tks):
        for tk in tks:
            if tk is None:
                continue
            h, v, teng = tk
            if teng == eng and (eng == "tensor" or not SAME_ENGINE_SYNC):
                continue
            k = id(h)
            if self.waited[eng].get(k, 0) >= v:
                continue
            self.waited[eng][k] = v
            self.ops[eng].append(("w", h, v))

    def op(self, eng, fns, reads=(), writes=(), accum=()):
        if not isinstance(fns, (list, tuple)):
            fns = [fns]
        tks = []
        for b in reads:
            tks += b.r_waits()
        for b in accum:
            tks += b.r_waits()
        for b in writes:
            tks += b.w_waits()
        self._waits(eng, tks)
        tk = self.prog[eng].inc(1, eng)
        for f in fns[:-1]:
            self.ops[eng].append(("o", f, None, 0))
        self.ops[eng].append(("o", fns[-1], tk[0], 1))
        for b in reads:
            b.add_reader(tk)
        for b in list(writes) + list(accum):
            b.set_writer(tk)
        return tk

    def dma(self, eng, out, in_, reads=(), writes=(), nowait_dst=False, sem_of=None, **kw):
        tks = []
        for b in reads:
            tks += b.r_waits()
        for b in writes:
            if not nowait_dst:
                tks += b.w_waits()
        self._waits(eng, tks)
        wb = sem_of if sem_of is not None else writes[0]
        tk = wb.sem().inc(16, "dma")
        self.ops[eng].append(("o", lambda e: e.dma_start(out=out, in_=in_, **kw), tk[0], 16))
        for b in reads:
            b.add_reader(tk)
        for b in writes:
            if nowait_dst:
                b.add_writer(tk)
            else:
                b.set_writer(tk)
        return tk

    def wait(self, eng, tks):
        self._waits(eng, tks)

    def replay(self, eng, e):
        for o in self.ops[eng]:
            if o[0] == "w":
                e.wait_ge(o[1], o[2])
            else:
                ins = o[1](e)
                if o[2] is not None:
                    ins.then_inc(o[2], o[3])


class _Stop(Exception):
    pass


DEBUG_STAGE = None
DEBUG_DUMP = False
LAST_RES = [None]


def build_program(L, NT):
    T = NT * 512
    NB = NT * 4
    nc = bass.Bass("TRN2", target_bir_lowering=False)
    P = Prog(nc)
    st = P.stack

    def dram(name, shape, dtype, kind=None):
        if kind is None:
            return nc.dram_tensor(name, shape, dtype)
        return nc.dram_tensor(name, shape, dtype, kind=kind)

    def dap(h, off, pat):
        return bass.AP(h, off, [list(p) for p in pat])

    EI = "ExternalInput"
    x_in = dram("x", [T, D], F32, EI)
    out_d = dram("out", [T, D], F32, "ExternalOutput")
    w_in_d = dram("w_in", [L, D, INW], F32, EI)
    gate_b_d = dram("gate_b", [L, 2, D], F32, EI)
    sgu_ln_g_d = dram("sgu_ln_g", [L, D], F32, EI)
    sgu_ln_b_d = dram("sgu_ln_b", [L, D], F32, EI)
    sgu_w_d = dram("sgu_w", [L, 8, 128, 128], F32, EI)
    sgu_b_d = dram("sgu_b", [L, 8, 128], F32, EI)
    lam_d = dram("lam", [L, 4, 64], F32, EI)
    diff_g_d = dram("diff_ln_g", [L, 128], F32, EI)
    w_br_d = dram("w_branch", [L, 2, D, D], F32, EI)
    w_out_d = dram("w_out", [L, D, D], F32, EI)
    w1_d = dram("ffn_w1", [L, 2, D, DFF], F32, EI)
    w3_d = dram("ffn_w3", [L, 2, D, DFF], F32, EI)
    w2_d = dram("ffn_w2", [L, 2, DFF, D], F32, EI)
    ln_g_d = dram("ln_g", [L, 3, D], F32, EI)
    ln_b_d = dram("ln_b", [L, 3, D], F32, EI)
    cos_d = dram("cosT", [128, T], F32, EI)
    sin_d = dram("sinT", [128, T], F32, EI)
    mask_d = dram("masks", [128, 4, 128], F32, EI)
    const_d = dram("consts", [128, 3, 128], F32, EI)
    linit_d = dram("linit", [L, 2], F32, EI)

    wib = dram("wib", [L, D, INW], BF16)
    w1b = dram("w1b", [L, 2, D, DFF], BF16)
    w3b = dram("w3b", [L, 2, D, DFF], BF16)
    w2b = dram("w2b", [L, 2, DFF, D], BF16)
    wbrb = dram("wbrb", [L, 2, D, D], BF16)
    wob = dram("wob", [L, D, D], BF16)
    xres = dram("xres", [T, D], F32)
    DK = "ExternalOutput" if DEBUG_DUMP else None
    q_s = dram("q_s", [8, 128, T], BF16, DK)
    kT_l = [dram(f"kT_l{n}", [8 * 128, 512], BF16) for n in range(NT)]
    v_l = [dram(f"v_l{n}", [8 * 128, 512], BF16) for n in range(NT)]
    kT_g = [dram(f"kT_g{n}", [4 * 8 * 128, 512], BF16) for n in range(NT)]
    v_g = [dram(f"v_g{n}", [4 * 8 * 128, 512], BF16) for n in range(NT)]
    mag_s = dram("mag_s", [8, 128, T], F32, DK)
    gb_s = dram("gb_s", [8, 128, T], F32, DK)
    if DEBUG_DUMP:
        dump_hb = dram("dump_hb", [128, 8 * 512], BF16, DK)

    def sb(name, shape, dtype):
        return st.enter_context(nc.sbuf_tensor("s_" + name, shape, dtype)).ap()

    def B(name, ap=None, arena=None, phase=0):
        return Buf(P, name, ap, arena, phase)

    consts_t = sb("consts", [128, 3, 128], F32)
    consts = B("consts", consts_t)
    ident = consts_t[:, 0, :]
    perm = consts_t[:, 1, :]
    ones32 = consts_t[:, 2, :]
    maskb_t = sb("maskb", [128, 4, 128], BF16)
    maskb = B("maskb", maskb_t)
    lnp_t = sb("lnp", [128, 4, 1024], F32)
    lnp = B("lnp", lnp_t)
    wmT_t = sb("wmT", [128, 8, 128], BF16)
    wmT = B("wmT", wmT_t)
    bS_t = sb("bS", [128, 8, 128], F32)
    bS = B("bS", bS_t)
    small_t = sb("small", [128, 64], F32)
    small = B("small", small_t)
    gateb_t = sb("gateb", [128, 16], F32)
    gateb = B("gateb", gateb_t)
    lam_t = sb("lamt", [128, 256], F32)
    lamb = B("lamb", lam_t)
    xa_t = sb("xa", [128, 4, 1024], F32)
    xb_t = sb("xb", [128, 4, 1024], F32)
    xa = B("xa", xa_t)
    xb = B("xb", xb_t)
    xT_t = [sb("xT0", [128, 8, 512], BF16)] * 2
    xT = [B("xT0", xT_t[0])] * 2
    NW = 4
    wr_t = [sb(f"wr{i}", [128, 4096], BF16) for i in range(NW)]
    wr = [B(f"wr{i}", wr_t[i]) for i in range(NW)]
    gT_t = sb("gT", [128, 22, 512], BF16)
    gT = B("gT", gT_t)
    bA_t = sb("bA", [128, 8, 512], BF16)
    bA = B("bA", bA_t)
    bB_t = sb("bB", [128, 4096], BF16)
    bB = B("bB", bB_t)
    vn_t = bB_t[:, :].rearrange("p (a b) -> p a b", b=1024)
    QT_t = bB_t[:, :].rearrange("p (a b) -> p a b", b=512)
    bC_t = sb("bC", [128, 8, 512], BF16)
    bC = B("bC", bC_t)
    s2_t = [sb(f"s2_{i}", [128, 512], F32) for i in range(6)]
    s2 = [B(f"s2_{i}", s2_t[i]) for i in range(6)]
    r2_t = [sb(f"r2_{i}", [128, 1024], F32) for i in range(2)]
    r2 = [B(f"r2_{i}", r2_t[i]) for i in range(2)]
    vtile_t = sb("vtile", [128, 1024], F32)
    vtile = B("vtile", vtile_t)
    sgw_t = vtile_t[:, :].rearrange("p (a b) -> p a b", b=128)
    sgw = vtile
    mask32_t = vtile_t[:, 0:512].rearrange("p (a b) -> p a b", b=128)
    mask32 = vtile
    L2s_t = sb("L2s", [128, 512], F32)
    L2s = B("L2s", L2s_t)
    cosb_t = sb("cosb", [128, 512], F32)
    sinb_t = sb("sinb", [128, 512], F32)
    cosb = B("cosb", cosb_t)
    sinb = B("sinb", sinb_t)
    qks_t = [sb(f"qks{i}", [128, 512], BF16) for i in range(2)]
    qks = [B(f"qks{i}", qks_t[i]) for i in range(2)]
    vsl_t = [sb(f"vsl{i}", [128, 1024], BF16) for i in range(2)]
    vsl = [B(f"vsl{i}", vsl_t[i]) for i in range(2)]
    NKV = 6
    kv_t = [sb(f"kv{i}", [128, 1024], BF16) for i in range(NKV)]
    kv = [B(f"kv{i}", kv_t[i]) for i in range(NKV)]
    NPT = 4
    LS = 384
    pt_t = [sb(f"pt{i}", [128, 2, 512], BF16) for i in range(NPT)]
    pt = [B(f"pt{i}", pt_t[i]) for i in range(NPT)]
    stat_t = sb("stat", [128, 4, 16], F32)
    stats = [B(f"stat{i}", stat_t[:, i, :]) for i in range(4)]
    cb_t = sb("cb", [128, 2, 128], BF16)
    cb = B("cb", cb_t)
    permb = cb_t[:, 0, :]
    onesb = cb_t[:, 1, :]
    hl_t = [sb(f"hl{i}", [128, 512], BF16) for i in range(4)]
    hl = [B(f"hl{i}", hl_t[i]) for i in range(4)]
    lamp_t = sb("lamp", [128, 128], F32)
    lamp = B("lamp", lamp_t)
    linit_t = sb("linit", [128, 2 * L], F32)
    linit = B("linit", linit_t)
    lnp = [B(f"lnp{i}", lnp_t[:, i, :]) for i in range(4)]

    pp_t = [st.enter_context(nc.psum_tensor(f"pp{i}", [128, 2, 512], F32)).ap() for i in range(4)]
    ps_t = [pp_t[i // 2][:, i % 2, :] for i in range(8)]
    ps = [B(f"ps{i}", ps_t[i]) for i in range(8)]
    psrr = [0]

    def next_ps():
        i = psrr[0] % 8
        psrr[0] += 1
        return ps[i]

    d_w = {}

    def dwb(hn, l, sub=0):
        k = (hn, l, sub)
        if k not in d_w:
            d_w[k] = B(f"d_w_{hn}_{l}_{sub}")
        return d_w[k]
    d_xres = [B(f"d_xres{n}") for n in range(NT)]
    d_q = [B(f"d_q{n}") for n in range(NT)]
    d_mag = [B(f"d_mag{n}") for n in range(NT)]
    d_gb = [B(f"d_gb{n}") for n in range(NT)]
    d_kl = [B(f"d_kl{n}") for n in range(NT)]
    d_vl = [B(f"d_vl{n}") for n in range(NT)]
    d_kg = [B(f"d_kg{n}") for n in range(NT)]
    d_vg = [B(f"d_vg{n}") for n in range(NT)]
    d_out = B("d_out")
    ccsem = Sem(P, "cc")
    d_dump = B("d_dump")

    def cast_weights(l):
        def cp(src, dst, off, n, buf):
            rows = n // 1024
            r0 = 0
            while r0 < rows:
                r = min(8192, rows - r0)
                P.dma("gpsimd", dap(dst, off + r0 * 1024, [(1024, r), (1, 1024)]),
                      dap(src, off + r0 * 1024, [(1024, r), (1, 1024)]), writes=[buf], nowait_dst=True)
                r0 += r
        for slot in range(2):
            o = (l * 2 + slot) * D * DFF
            cp(w1_d, w1b, o, D * DFF, dwb("w1b", l, slot))
            cp(w3_d, w3b, o, D * DFF, dwb("w3b", l, slot))
            cp(w2_d, w2b, o, D * DFF, dwb("w2b", l, slot))
            if slot == 0:
                cp(w_in_d, wib, l * D * INW, D * INW, dwb("wib", l))
                for br in range(2):
                    cp(w_br_d, wbrb, (l * 2 + br) * D * D, D * D, dwb("wbrb", l, br))
                cp(w_out_d, wob, l * D * D, D * D, dwb("wob", l))

    def wbuf_of(e):
        l, hn, off = e[0], e[1], e[2]
        if hn in ("w1b", "w3b", "w2b"):
            return dwb(hn, l, (off - l * 2 * D * DFF) // (D * DFF))
        if hn == "wbrb":
            return dwb(hn, l, (off - l * 2 * D * D) // (D * D))
        return dwb(hn, l)

    def ffn_plan(l, slot):
        base1 = (l * 2 + slot) * D * DFF
        for c in range(6):
            nf = 4 if c < 5 else 2
            yield (l, "w1b", base1 + c * 512, DFF, 8, nf * 128)
            yield (l, "w3b", base1 + c * 512, DFF, 8, nf * 128)
        for c in range(6):
            nf = 4 if c < 5 else 2
            yield (l, "w2b", (l * 2 + slot) * DFF * D + c * 512 * D, D, nf, 1024)

    def layer_plan(l):
        wbase = l * D * INW
        for n in range(NT):
            yield from ffn_plan(l, 0)
            for off in (0, 512, 1024, 1536):
                yield (l, "wib", wbase + off, INW, 8, 512)
            for hf in range(2):
                yield (l, "wib", wbase + 5120 + hf * 512, INW, 8, 512)
                yield (l, "wbrb", (l * 2 + 0) * D * D + hf * 512, D, 8, 512)
            for hf in range(2):
                yield (l, "wib", wbase + 6144 + hf * 512, INW, 8, 512)
            for which in range(2):
                for hf in range(2):
                    yield (l, "wib", wbase + 2048 + which * 1024 + hf * 512, INW, 8, 512)
            yield (l, "wib", wbase + 4096, INW, 8, 512)
            yield (l, "wib", wbase + 4608, INW, 8, 512)
        for n in range(NT):
            for hf in range(2):
                yield (l, "wbrb", (l * 2 + 1) * D * D + hf * 512, D, 8, 512)
            yield (l, "wob", l * D * D + 0, D, 8, 512)
            yield (l, "wob", l * D * D + 512, D, 8, 512)
            yield from ffn_plan(l, 1)

    wplan = [e for l in range(L) for e in layer_plan(l)]
    whandles = {"w1b": w1b, "w3b": w3b, "w2b": w2b, "wib": wib, "wbrb": wbrb, "wob": wob}
    wcount = [0]
    wissued = [0]
    WLOOK = NW - 2

    def wissue(k):
        l, hn, off, rowstride, nk, ncols = wplan[k]
        i = k % NW
        view = wr_t[i][:, 0:nk * ncols].rearrange("p (a b) -> p a b", b=ncols)
        P.dma("sync", view, dap(whandles[hn], off, [(rowstride, 128), (128 * rowstride, nk), (1, ncols)]),
              reads=[wbuf_of(wplan[k])], writes=[wr[i]])

    def wload(l, handle, off, rowstride, nk, ncols):
        k = wcount[0]
        wcount[0] += 1
        if DEBUG_STAGE is None:
            e = wplan[k]
            assert e[0] == l and whandles[e[1]] is handle and e[2:] == (off, rowstride, nk, ncols), (k, e, l, off)
            while wissued[0] <= min(k + WLOOK, len(wplan) - 1):
                wissue(wissued[0])
                wissued[0] += 1
        else:
            wplan[k:k + 1] = [(l, [n_ for n_, h_ in whandles.items() if h_ is handle][0], off, rowstride, nk, ncols)]
            wissue(k)
        i = k % NW
        view = wr_t[i][:, 0:nk * ncols].rearrange("p (a b) -> p a b", b=ncols)
        return wr[i], view

    def mm(out_ap, lhsT, rhs, start, stop):
        return lambda e: e.matmul(out_ap, lhsT, rhs, start=start, stop=stop)

    def act(out, in_, func, **kw):
        return lambda e: e.activation(out=out, in_=in_, func=func, **kw)

    def tt(out, in0, in1, op):
        return lambda e: e.tensor_tensor(out=out, in0=in0, in1=in1, op=op)

    def stt(out, in0, scalar, in1, op0, op1):
        return lambda e: e.scalar_tensor_tensor(out=out, in0=in0, scalar=scalar, in1=in1, op0=op0, op1=op1)

    def transposes_to_xT(src_t, src_b, di):
        for dc in range(8):
            pb = next_ps()
            fns = [lambda e, s=s, dc=dc, pb=pb: e.transpose(pb.ap[:, s * 128:(s + 1) * 128],
                                                           src_t[:, s, dc * 128:(dc + 1) * 128], ident)
                   for s in range(4)]
            P.op("tensor", fns, reads=[src_b, consts], writes=[pb])
            kw = dict(writes=[xT[di]]) if dc == 0 else dict(accum=[xT[di]])
            P.op("scalar", act(xT_t[di][:, dc, :], pb.ap, AF.Copy), reads=[pb], **kw)

    lncnt = [0]

    def layernorm(src_b, src_ap, dst_b, dst_ap, gi, bi, eps_ap, dst_first):
        sb_ = stats[lncnt[0] % 4]
        lncnt[0] += 1
        s_ap = sb_.ap
        P.op("vector", [lambda e: e.bn_stats(out=s_ap[:, 0:6], in_=src_ap[:, 0:512]),
                        lambda e: e.bn_stats(out=s_ap[:, 6:12], in_=src_ap[:, 512:1024])],
             reads=[src_b], writes=[sb_])
        P.op("vector", lambda e: e.bn_aggr(out=s_ap[:, 12:14], in_=s_ap[:, 0:12]), reads=[sb_], writes=[sb_])
        P.op("scalar", act(s_ap[:, 14:15], s_ap[:, 13:14], AF.Sqrt, bias=eps_ap, scale=1.0),
             reads=[small, sb_], writes=[sb_])
        P.op("vector", lambda e: e.reciprocal(out=s_ap[:, 14:15], in_=s_ap[:, 14:15]), reads=[sb_], writes=[sb_])
        P.op("vector", stt(s_ap[:, 15:16], s_ap[:, 12:13], -1.0, s_ap[:, 14:15], ALU.mult, ALU.mult),
             reads=[sb_], writes=[sb_])
        P.op("scalar", act(src_ap, src_ap, AF.Identity, scale=s_ap[:, 14:15], bias=s_ap[:, 15:16]),
             reads=[src_b, sb_], writes=[src_b])
        aff = "gpsimd" if lncnt[0] % 2 == 0 else "vector"
        P.op(aff, tt(src_ap, src_ap, lnp_t[:, gi, :], ALU.mult), reads=[src_b, lnp[gi]], writes=[src_b])
        kw = dict(writes=[dst_b]) if dst_first else dict(accum=[dst_b])
        P.op(aff, tt(dst_ap, src_ap, lnp_t[:, bi, :], ALU.add), reads=[src_b, lnp[bi]], **kw)

    def ffn(l, slot, xi, res_b, res_t, dst_b, dst_t, gi, bi):
        base1 = (l * 2 + slot) * D * DFF
        first = True
        for c in range(6):
            nf = 4 if c < 5 else 2
            w1buf, w1v = wload(l, w1b, base1 + c * 512, DFF, 8, nf * 128)
            w3buf, w3v = wload(l, w3b, base1 + c * 512, DFF, 8, nf * 128)
            for j in range(nf):
                fc = c * 4 + j
                p1 = next_ps()
                p3 = next_ps()
                P.op("tensor", [mm(p1.ap, w1v[:, k, j * 128:(j + 1) * 128], xT_t[xi][:, k, :], k == 0, k == 7)
                                for k in range(8)], reads=[w1buf, xT[xi]], writes=[p1])
                P.op("tensor", [mm(p3.ap, w3v[:, k, j * 128:(j + 1) * 128], xT_t[xi][:, k, :], k == 0, k == 7)
                                for k in range(8)], reads=[w3buf, xT[xi]], writes=[p3])
                sbf = s2[fc % 4]
                P.op("scalar", act(sbf.ap, p1.ap, AF.Silu), reads=[p1], writes=[sbf])
                kw = dict(writes=[gT]) if first else dict(accum=[gT])
                first = False
                P.op("vector", tt(gT_t[:, fc, :], sbf.ap, p3.ap, ALU.mult), reads=[sbf, p3], **kw)
        for c in range(6):
            nf = 4 if c < 5 else 2
            w2buf, w2v = wload(l, w2b, (l * 2 + slot) * DFF * D + c * 512 * D, D, nf, 1024)
            for s in range(4):
                for hh in range(2):
                    pb = ps[s * 2 + hh]
                    fns = [mm(pb.ap, gT_t[:, c * 4 + j, s * 128:(s + 1) * 128], w2v[:, j, hh * 512:(hh + 1) * 512],
                              c == 0 and j == 0, c == 5 and j == nf - 1) for j in range(nf)]
                    if c == 0:
                        P.op("tensor", fns, reads=[w2buf, gT], writes=[pb])
                    else:
                        P.op("tensor", fns, reads=[w2buf, gT], accum=[pb])
        for s in range(4):
            rb = r2[s % 2]
            for hh in range(2):
                pb = ps[s * 2 + hh]
                kw = dict(writes=[rb]) if hh == 0 else dict(accum=[rb])
                P.op("vector", stt(rb.ap[:, hh * 512:(hh + 1) * 512], res_t[:, s, hh * 512:(hh + 1) * 512],
                                   2.0 * ALPHA, pb.ap, ALU.mult, ALU.add), reads=[res_b, pb], **kw)
            layernorm(rb, rb.ap, dst_b, dst_t[:, s, :], gi, bi, small_t[:, 1:2], dst_first=(s == 0))

    def load_lnp(parts):
        for i, (h, off) in enumerate(parts):
            P.dma("sync", lnp_t[:, i, :], dap(h, off, [(0, 128), (1, 1024)]), writes=[lnp[i]])

    P.dma("sync", consts_t, const_d.ap(), writes=[consts])
    P.dma("sync", mask32_t, mask_d.ap(), writes=[mask32])
    P.op("vector", lambda e: e.tensor_copy(out=maskb_t, in_=mask32_t), reads=[mask32], writes=[maskb])
    P.op("gpsimd", [lambda e: e.memset(small_t[:, 0:1], EPS), lambda e: e.memset(small_t[:, 1:2], 4 * EPS)],
         writes=[small])
    P.dma("sync", linit_t, dap(linit_d, 0, [(0, 128), (1, 2 * L)]), writes=[linit])
    P.op("vector", lambda e: e.tensor_copy(out=cb_t, in_=consts_t[:, 1:3, :]), reads=[consts], writes=[cb])

    def split_sum(out_b, lhsT_b16, src_ap, src_bs, i0):
        hi, lo = hl[i0], hl[i0 + 1]
        P.op("scalar", act(hi.ap, src_ap, AF.Copy), reads=src_bs, writes=[hi])
        P.op("vector", tt(lo.ap, src_ap, hi.ap, ALU.subtract), reads=list(src_bs) + [hi], writes=[lo])
        P.op("tensor", [mm(out_b.ap, lhsT_b16, hi.ap, True, False), mm(out_b.ap, lhsT_b16, lo.ap, False, True)],
             reads=[cb, hi, lo], writes=[out_b])
    for l in range(L):
        cast_weights(l)

    cur_n = [0]

    def stage(name, src_t=None, src_b=None):
        if DEBUG_STAGE == name or DEBUG_STAGE == f"{name}@{cur_n[0]}":
            if src_t is not None:
                P.dma("gpsimd", dap(out_d, 0, [(D, 128), (128 * D, 4), (1, D)]), src_t, reads=[src_b], sem_of=src_b, writes=[d_out], nowait_dst=True)
            raise _Stop()

    def emit_layer(l):
        src_h = x_in if l == 0 else xres
        load_lnp([(ln_g_d, (l * 3 + 0) * D), (ln_b_d, (l * 3 + 0) * D), (sgu_ln_g_d, l * D), (sgu_ln_b_d, l * D)])
        with nc.allow_non_contiguous_dma(reason="small param loads"):
            P.dma("sync", sgw_t, dap(sgu_w_d, l * 8 * 128 * 128, [(128, 128), (128 * 128, 8), (1, 128)]), writes=[sgw])
            P.dma("sync", bS_t, dap(sgu_b_d, l * 8 * 128, [(0, 128), (128, 8), (1, 128)]), writes=[bS])
            P.dma("sync", gateb_t, dap(gate_b_d, l * 2 * D, [(1, 128), (128, 16)]), writes=[gateb], allow_slow_non_contiguous=True)
            P.dma("sync", lam_t, dap(lam_d, l * 256, [(0, 128), (1, 256)]), writes=[lamb])
            P.dma("sync", small_t[:, 2:3], dap(diff_g_d, l * 128, [(1, 128), (1, 1)]), writes=[small])
        for g in range(8):
            pb = next_ps()
            P.op("tensor", lambda e, g=g, pb=pb: e.transpose(pb.ap[:, 0:128], sgw_t[:, g, :], ident),
                 reads=[sgw, consts], writes=[pb])
            sbf = s2[g % 4]
            P.op("scalar", act(sbf.ap[:, 0:128], pb.ap[:, 0:128], AF.Copy), reads=[pb], writes=[sbf])
            kw = dict(writes=[wmT]) if g == 0 else dict(accum=[wmT])
            P.op("gpsimd", lambda e, g=g, sbf=sbf: e.affine_select(
                out=wmT_t[:, g, :], in_=sbf.ap[:, 0:128], pattern=[[1, 128]], base=0, channel_multiplier=-1,
                compare_op=ALU.is_ge, fill=0.0), reads=[sbf], **kw)
        P.op("vector", [tt(lamp_t[:, 0:64], lam_t[:, 0:64], lam_t[:, 64:128], ALU.mult),
                        tt(lamp_t[:, 64:128], lam_t[:, 128:192], lam_t[:, 192:256], ALU.mult)],
             reads=[lamb], writes=[lamp])
        P.op("scalar", [act(lamp_t[:, 0:64], lamp_t[:, 0:64], AF.Identity, accum_out=small_t[:, 5:6]),
                        act(lamp_t[:, 64:128], lamp_t[:, 64:128], AF.Identity, accum_out=small_t[:, 6:7])],
             reads=[lamp], writes=[small, lamp])
        P.op("scalar", act(small_t[:, 7:9], small_t[:, 5:7], AF.Exp), reads=[small], writes=[small])
        P.op("vector", tt(small_t[:, 9:10], small_t[:, 7:8], small_t[:, 8:9], ALU.subtract), reads=[small], writes=[small])
        P.op("vector", stt(small_t[:, 4:5], small_t[:, 9:10], -1.0, linit_t[:, 2 * l:2 * l + 1], ALU.mult, ALU.subtract),
             reads=[small, linit], writes=[small])
        P.op("vector", tt(small_t[:, 3:4], small_t[:, 2:3], linit_t[:, 2 * l + 1:2 * l + 2], ALU.mult),
             reads=[small, linit], writes=[small])
        neglam = small_t[:, 4:5]
        gcoef = small_t[:, 3:4]

        cc_pending = []
        for n in range(NT):
            cur_n[0] = n
            tsl = slice(n * 512, (n + 1) * 512)
            P.dma("sync", xa_t, dap(src_h, n * 512 * D, [(D, 128), (128 * D, 4), (1, D)]),
                  reads=[d_xres[n]] if l > 0 else [], writes=[xa])
            P.dma("sync", cosb_t, cos_d.ap()[:, tsl], writes=[cosb])
            P.dma("sync", sinb_t, sin_d.ap()[:, tsl], writes=[sinb])
            stage('A', xa_t, xa)
            transposes_to_xT(xa_t, xa, 0)
            ffn(l, 0, 0, xa, xa_t, xb, xb_t, 0, 1)
            stage('B', xb_t, xb)
            P.dma("gpsimd", dap(xres, n * 512 * D, [(D, 128), (128 * D, 4), (1, D)]), xb_t, reads=[xb], sem_of=xb, writes=[d_xres[n]])
            transposes_to_xT(xb_t, xb, 1)
            X1 = xT_t[1]
            X1b = xT[1]
            wbase = l * D * INW
            for c2 in range(2):
                wb_, wv = wload(l, wib, wbase + 0 + c2 * 512, INW, 8, 512)
                for j in range(4):
                    c = c2 * 4 + j
                    pb = next_ps()
                    P.op("tensor", [mm(pb.ap, wv[:, k, j * 128:(j + 1) * 128], X1[:, k, :], k == 0, k == 7) for k in range(8)],
                         reads=[wb_, X1b], writes=[pb])
                    kw = dict(writes=[bA]) if c == 0 else dict(accum=[bA])
                    P.op("scalar", act(bA_t[:, c, :], pb.ap, AF.Gelu), reads=[pb], **kw)
            stage('C1', xb_t, xb)
            wv0b, wv0 = wload(l, wib, wbase + 1024, INW, 8, 512)
            wv1b, wv1 = wload(l, wib, wbase + 1536, INW, 8, 512)
            for s in range(4):
                for hh, (wb_, wv) in enumerate([(wv0b, wv0), (wv1b, wv1)]):
                    pb = next_ps()
                    P.op("tensor", [mm(pb.ap, X1[:, k, s * 128:(s + 1) * 128], wv[:, k, :], k == 0, k == 7) for k in range(8)],
                         reads=[wb_, X1b], writes=[pb])
                    kw = dict(writes=[vtile]) if hh == 0 else dict(accum=[vtile])
                    P.op("scalar", act(vtile_t[:, hh * 512:(hh + 1) * 512], pb.ap, AF.Gelu), reads=[pb], **kw)
                layernorm(vtile, vtile_t, bB, vn_t[:, s, :], 2, 3, small_t[:, 0:1], dst_first=(s == 0))
            stage('C2', xb_t, xb)
            for g in range(8):
                pb = next_ps()
                P.op("tensor", [mm(pb.ap[:, s * 128:(s + 1) * 128], vn_t[:, s, g * 128:(g + 1) * 128], wmT_t[:, g, :], True, True)
                                for s in range(4)], reads=[bB, wmT], writes=[pb])
                sbf = s2[g % 4]
                bsb = bass.AP(bS_t.tensor, g * 128, [[1024, 128], [0, 4], [1, 128]])
                P.op("vector", tt(sbf.ap[:, :].rearrange("p (a b) -> p a b", b=128),
                                  pb.ap[:, :].rearrange("p (a b) -> p a b", b=128), bsb, ALU.add),
                     reads=[pb, bS], writes=[sbf])
                kw = dict(writes=[bC]) if g == 0 else dict(accum=[bC])
                P.op("gpsimd", tt(bC_t[:, g, :], sbf.ap, bA_t[:, g, :], ALU.mult), reads=[sbf, bA], **kw)
            stage('C3', xb_t, xb)
            for hf in range(2):
                wgb, wg = wload(l, wib, wbase + 5120 + hf * 512, INW, 8, 512)
                wab, wa = wload(l, wbrb, (l * 2 + 0) * D * D + hf * 512, D, 8, 512)
                for j in range(4):
                    c = hf * 4 + j
                    pg = next_ps()
                    P.op("tensor", [mm(pg.ap, wg[:, k, j * 128:(j + 1) * 128], X1[:, k, :], k == 0, k == 7) for k in range(8)],
                         reads=[wgb, X1b], writes=[pg])
                    gs = s2[(2 * c) % 6]
                    P.op("scalar", act(gs.ap, pg.ap, AF.Sigmoid, bias=gateb_t[:, c:c + 1], scale=1.0),
                         reads=[pg, gateb], writes=[gs])
                    pm = next_ps()
                    P.op("tensor", [mm(pm.ap, wa[:, k, j * 128:(j + 1) * 128], bC_t[:, k, :], k == 0, k == 7) for k in range(8)],
                         reads=[wab, bC], writes=[pm])
                    P.op("vector", tt(gs.ap, gs.ap, pm.ap, ALU.mult), reads=[gs, pm], writes=[gs])
                    P.dma("gpsimd", dap(mag_s, c * 128 * T + n * 512, [(T, 128), (1, 512)]), gs.ap, reads=[gs], sem_of=gs,
                          writes=[d_mag[n]], nowait_dst=(c > 0))
            stage('C4', xb_t, xb)
            for hf in range(2):
                wgb, wg = wload(l, wib, wbase + 6144 + hf * 512, INW, 8, 512)
                for j in range(4):
                    c = hf * 4 + j
                    pg = next_ps()
                    P.op("tensor", [mm(pg.ap, wg[:, k, j * 128:(j + 1) * 128], X1[:, k, :], k == 0, k == 7) for k in range(8)],
                         reads=[wgb, X1b], writes=[pg])
                    gs = s2[(2 * c + 1) % 6]
                    P.op("scalar", act(gs.ap, pg.ap, AF.Sigmoid, bias=gateb_t[:, 8 + c:9 + c], scale=1.0),
                         reads=[pg, gateb], writes=[gs])
                    P.dma("gpsimd", dap(gb_s, c * 128 * T + n * 512, [(T, 128), (1, 512)]), gs.ap, reads=[gs], sem_of=gs,
                          writes=[d_gb[n]], nowait_dst=(c > 0))
            stage('C5', xb_t, xb)
            for which in range(2):
                for hf in range(2):
                    wqb, wq = wload(l, wib, wbase + 2048 + which * 1024 + hf * 512, INW, 8, 512)
                    for j in range(4):
                        h = hf * 4 + j
                        pq = next_ps()
                        P.op("tensor", [mm(pq.ap, wq[:, k, j * 128:(j + 1) * 128], X1[:, k, :], k == 0, k == 7) for k in range(8)],
                             reads=[wqb, X1b], writes=[pq])
                        psw = next_ps()
                        split_sum(psw, permb, pq.ap, [pq], 0)
                        t1 = s2[(2 * h) % 6]
                        t2 = s2[(2 * h + 1) % 6]
                        P.op("vector", tt(t1.ap, pq.ap, cosb_t, ALU.mult), reads=[pq, cosb], writes=[t1])
                        P.op("vector", tt(t2.ap, psw.ap, sinb_t, ALU.mult), reads=[psw, sinb], writes=[t2])
                        qo = qks[h % 2]
                        P.op("gpsimd", tt(qo.ap, t1.ap, t2.ap, ALU.add), reads=[t1, t2], writes=[qo])
                        if which == 0:
                            P.dma("sync", dap(q_s, h * 128 * T + n * 512, [(T, 128), (1, 512)]), qo.ap, reads=[qo], sem_of=qo,
                                  writes=[d_q[n]], nowait_dst=(h > 0))
                        else:
                            P.dma("sync", dap(kT_l[n], h * 128 * 512, [(512, 128), (1, 512)]), qo.ap, reads=[qo], sem_of=qo,
                                  writes=[d_kl[n]], nowait_dst=(h > 0))
            stage('C6', xb_t, xb)
            wv0b, wv0 = wload(l, wib, wbase + 4096, INW, 8, 512)
            wv1b, wv1 = wload(l, wib, wbase + 4608, INW, 8, 512)
            for s in range(4):
                vo = vsl[s % 2]
                for hh, (wb_, wv) in enumerate([(wv0b, wv0), (wv1b, wv1)]):
                    pb = next_ps()
                    P.op("tensor", [mm(pb.ap, X1[:, k, s * 128:(s + 1) * 128], wv[:, k, :], k == 0, k == 7) for k in range(8)],
                         reads=[wb_, X1b], writes=[pb])
                    kw = dict(writes=[vo]) if hh == 0 else dict(accum=[vo])
                    P.op("scalar", act(vo.ap[:, hh * 512:(hh + 1) * 512], pb.ap, AF.Copy), reads=[pb], **kw)
                P.dma("sync", dap(v_l[n], s * 128, [(512, 128), (128 * 512, 8), (1, 128)]),
                      vo.ap[:, :].rearrange("p (a b) -> p a b", b=128), reads=[vo], sem_of=vo, writes=[d_vl[n]],
                      nowait_dst=(s > 0))
            for (src, dst, sbuf_, dbuf_) in ((kT_l[n], kT_g[n], d_kl[n], d_kg[n]), (v_l[n], v_g[n], d_vl[n], d_vg[n])):
                P.wait("gpsimd", sbuf_.r_waits() + dbuf_.w_waits())
                tk = ccsem.inc(1, "cc")
                P.ops["gpsimd"].append(("o", (lambda e, src=src, dst=dst: e.collective_compute(
                    "AllGather", ALU.bypass, replica_groups=[[0, 1, 2, 3], [4, 5, 6, 7]],
                    ins=[src.ap()], outs=[dst.ap()])), tk[0], 1))
                sbuf_.add_reader(tk)
                cc_pending.append((dbuf_, tk))

        cur_n[0] = -1
        stage('C', xb_t, xb)
        last_tk = cc_pending[-1][1]
        for dbuf_, tk in cc_pending:
            dbuf_.set_writer(last_tk)
        del cc_pending[:]

        stage('D', xb_t, xb)
        load_lnp([(ln_g_d, (l * 3 + 1) * D), (ln_b_d, (l * 3 + 1) * D), (ln_g_d, (l * 3 + 2) * D), (ln_b_d, (l * 3 + 2) * D)])
        O1, O2, L1, L2 = ps[0], ps[1], ps[2], ps[3]
        kvglob = [0]
        ptglob = [0]
        for n in range(NT):
            cur_n[0] = n
            P.dma("sync", xa_t, dap(xres, n * 512 * D, [(D, 128), (128 * D, 4), (1, D)]), reads=[d_xres[n]], writes=[xa])
            P.dma("sync", QT_t, dap(q_s, n * 512, [(T, 128), (128 * T, 8), (1, 512)]), reads=[d_q[n]], writes=[bB])
            def head_epilogue(h):
                e0, e1, e2, e3 = s2[0], s2[1], s2[2], s2[3]
                split_sum(L1, onesb, L1.ap, [L1], 0)
                hi2, lo2 = hl[2], hl[3]
                P.op("scalar", [act(hi2.ap[:, 0:LS], L2s_t[:, 0:LS], AF.Copy), act(hi2.ap[:, LS:512], L2.ap[:, LS:512], AF.Copy)],
                     reads=[L2s, L2], writes=[hi2])
                P.op("vector", [tt(lo2.ap[:, 0:LS], L2s_t[:, 0:LS], hi2.ap[:, 0:LS], ALU.subtract),
                                tt(lo2.ap[:, LS:512], L2.ap[:, LS:512], hi2.ap[:, LS:512], ALU.subtract)],
                     reads=[L2s, L2, hi2], writes=[lo2])
                P.op("tensor", [mm(L2.ap, onesb, hi2.ap, True, False), mm(L2.ap, onesb, lo2.ap, False, True)],
                     reads=[cb, hi2, lo2], writes=[L2])
                P.op("vector", [lambda e: e.reciprocal(out=e0.ap, in_=L1.ap), lambda e: e.reciprocal(out=e1.ap, in_=L2.ap)],
                     reads=[L1, L2], writes=[e0, e1])
                P.op("vector", [tt(e2.ap, O1.ap, e0.ap, ALU.mult), tt(e3.ap, O2.ap, e1.ap, ALU.mult)],
                     reads=[O1, O2, e0, e1], writes=[e2, e3])
                P.op("vector", stt(e2.ap, e3.ap, neglam, e2.ap, ALU.mult, ALU.add), reads=[e2, e3, small], writes=[e2])
                P.op("scalar", act(e3.ap, e2.ap, AF.Square), reads=[e2], writes=[e3])
                split_sum(L1, onesb, e3.ap, [e3], 0)
                P.op("scalar", act(e3.ap, L1.ap, AF.Sqrt, bias=small_t[:, 0:1], scale=1.0 / 128.0), reads=[L1, small], writes=[e3])
                P.op("vector", lambda e: e.reciprocal(out=e3.ap, in_=e3.ap), reads=[e3], writes=[e3])
                kw = dict(writes=[bA]) if h == 0 else dict(accum=[bA])
                P.op("vector", stt(bA_t[:, h, :], e2.ap, gcoef, e3.ap, ALU.mult, ALU.mult), reads=[e2, e3, small], **kw)
            chunks = []
            steps = []
            for h in range(8):
                cl = [(r, tl, None) for tl in range(n) for r in range(4)] + [(r, n, r) for r in range(4)]
                for ci, (r, tl, mt) in enumerate(cl):
                    chunks.append((h, r, tl))
                    for a in range(4):
                        steps.append(dict(h=h, c=len(chunks) - 1, a=a, qlo=0 if mt is None else a * 128, mt=mt,
                                          first=(ci == 0 and a == 0), last=(ci == len(cl) - 1 and a == 3)))
            kvbase = kvglob[0]
            kvglob[0] += len(chunks)
            issued = [0]

            def ensure_loaded(upto):
                while issued[0] <= min(upto, len(chunks) - 1):
                    c = issued[0]
                    h, r, tl = chunks[c]
                    i = (kvbase + c) % NKV
                    P.dma("sync", kv_t[i][:, 0:512], dap(kT_g[tl], (r * 1024 + h * 128) * 512, [(512, 128), (1, 512)]),
                          reads=[d_kg[tl]], writes=[kv[i]])
                    P.dma("sync", kv_t[i][:, 512:1024], dap(v_g[tl], (r * 1024 + h * 128) * 512, [(512, 128), (1, 512)]),
                          reads=[d_vg[tl]], writes=[kv[i]], nowait_dst=True)
                    issued[0] += 1

            def emit_qk(si):
                sp = steps[si]
                if sp["a"] == 0:
                    ensure_loaded(sp["c"] + 2)
                i = (kvbase + sp["c"]) % NKV
                a, qlo, h = sp["a"], sp["qlo"], sp["h"]
                SS = pp_t[2 + si % 2]
                P.op("tensor", [mm(SS[:, 0, qlo:512], kv_t[i][0:64, a * 128:(a + 1) * 128], QT_t[0:64, h, qlo:512], True, True),
                                mm(SS[:, 1, qlo:512], kv_t[i][64:128, a * 128:(a + 1) * 128], QT_t[64:128, h, qlo:512], True, True)],
                     reads=[kv[i], bB], writes=[ps[4 + (si % 2) * 2], ps[5 + (si % 2) * 2]])

            def emit_rest(si):
                sp = steps[si]
                i = (kvbase + sp["c"]) % NKV
                a, qlo, h, mt = sp["a"], sp["qlo"], sp["h"], sp["mt"]
                SS = pp_t[2 + si % 2]
                S1, S2 = ps[4 + (si % 2) * 2], ps[5 + (si % 2) * 2]
                pti = (ptglob[0] + si) % NPT
                ptb, ptt = pt[pti], pt_t[pti]
                P.op("scalar", act(ptt[:, :, qlo:512], SS[:, :, qlo:512], AF.Exp, scale=0.125), reads=[S1, S2], writes=[ptb])
                if mt is not None:
                    mk2 = bass.AP(maskb_t.tensor, mt * 128, [[512, 128], [0, 2], [1, 128]])
                    P.op("vector", tt(ptt[:, :, qlo:qlo + 128], ptt[:, :, qlo:qlo + 128], mk2, ALU.mult),
                         reads=[ptb, maskb], writes=[ptb])
                vsl_ = kv_t[i][:, 512 + a * 128:512 + (a + 1) * 128]
                LL = pp_t[1]
                if sp["first"]:
                    P.op("tensor", [mm(O1.ap[:, qlo:512], vsl_, ptt[:, 0, qlo:512], True, False),
                                    mm(O2.ap[:, qlo:512], vsl_, ptt[:, 1, qlo:512], True, False)],
                         reads=[kv[i], ptb], writes=[O1, O2])
                    P.op("vector", [lambda e, ptt=ptt: e.tensor_copy(out=L1.ap, in_=ptt[:, 0, :]),
                                    lambda e, ptt=ptt: e.tensor_copy(out=L2.ap[:, LS:512], in_=ptt[:, 1, LS:512])],
                         reads=[ptb], writes=[L1, L2])
                    P.op("gpsimd", lambda e, ptt=ptt: e.tensor_copy(out=L2s_t[:, 0:LS], in_=ptt[:, 1, 0:LS]), reads=[ptb], writes=[L2s])
                else:
                    P.op("tensor", [mm(O1.ap[:, qlo:512], vsl_, ptt[:, 0, qlo:512], False, False),
                                    mm(O2.ap[:, qlo:512], vsl_, ptt[:, 1, qlo:512], False, False)],
                         reads=[kv[i], ptb], accum=[O1, O2])
                    dlo = max(qlo, LS)
                    P.op("vector", [tt(L1.ap[:, qlo:512], L1.ap[:, qlo:512], ptt[:, 0, qlo:512], ALU.add),
                                    tt(L2.ap[:, dlo:512], L2.ap[:, dlo:512], ptt[:, 1, dlo:512], ALU.add)],
                         reads=[ptb, L1, L2], writes=[L1, L2])
                    if qlo < LS:
                        P.op("gpsimd", tt(L2s_t[:, qlo:LS], L2s_t[:, qlo:LS], ptt[:, 1, qlo:LS], ALU.add),
                             reads=[ptb, L2s], writes=[L2s])
                if sp["last"]:
                    head_epilogue(h)

            emit_qk(0)
            for si in range(len(steps)):
                if si + 1 < len(steps):
                    emit_qk(si + 1)
                emit_rest(si)
            ptglob[0] += len(steps)
            if DEBUG_DUMP and n == 0:
                P.dma("sync", dump_hb.ap(), bA_t[:, :, :].rearrange("p a b -> p (a b)"), reads=[bA], sem_of=bA, writes=[d_dump], nowait_dst=True)
            stage('E', xb_t, xb)
            for hf in range(2):
                wbb, wb2 = wload(l, wbrb, (l * 2 + 1) * D * D + hf * 512, D, 8, 512)
                for j in range(4):
                    c = hf * 4 + j
                    pm = next_ps()
                    P.op("tensor", [mm(pm.ap, wb2[:, k, j * 128:(j + 1) * 128], bA_t[:, k, :], k == 0, k == 7) for k in range(8)],
                         reads=[wbb, bA], writes=[pm])
                    gsl = s2[(2 * c) % 6]
                    msl = s2[(2 * c + 1) % 6]
                    P.dma("sync", gsl.ap, dap(gb_s, c * 128 * T + n * 512, [(T, 128), (1, 512)]), reads=[d_gb[n]], writes=[gsl])
                    P.dma("sync", msl.ap, dap(mag_s, c * 128 * T + n * 512, [(T, 128), (1, 512)]), reads=[d_mag[n]], writes=[msl])
                    P.op("vector", tt(gsl.ap, gsl.ap, pm.ap, ALU.mult), reads=[gsl, pm], writes=[gsl])
                    kw = dict(writes=[bC]) if c == 0 else dict(accum=[bC])
                    P.op("gpsimd", tt(bC_t[:, c, :], gsl.ap, msl.ap, ALU.add), reads=[gsl, msl], **kw)
            wo0b, wo0 = wload(l, wob, l * D * D + 0, D, 8, 512)
            wo1b, wo1 = wload(l, wob, l * D * D + 512, D, 8, 512)
            for s in range(4):
                rb = r2[s % 2]
                for hh, (wb_, wv) in enumerate([(wo0b, wo0), (wo1b, wo1)]):
                    pb = next_ps()
                    P.op("tensor", [mm(pb.ap, bC_t[:, k, s * 128:(s + 1) * 128], wv[:, k, :], k == 0, k == 7) for k in range(8)],
                         reads=[wb_, bC], writes=[pb])
                    kw = dict(writes=[rb]) if hh == 0 else dict(accum=[rb])
                    P.op("vector", stt(rb.ap[:, hh * 512:(hh + 1) * 512], xa_t[:, s, hh * 512:(hh + 1) * 512], ALPHA, pb.ap,
                                       ALU.mult, ALU.add), reads=[xa, pb], **kw)
                layernorm(rb, rb.ap, xb, xb_t[:, s, :], 0, 1, small_t[:, 0:1], dst_first=(s == 0))
            transposes_to_xT(xb_t, xb, 0)
            ffn(l, 1, 0, xb, xb_t, xb, xb_t, 2, 3)
            if l == L - 1:
                P.dma("gpsimd", dap(out_d, n * 512 * D, [(D, 128), (128 * D, 4), (1, D)]), xb_t, reads=[xb],
                      sem_of=xb, writes=[d_out], nowait_dst=True)
            else:
                P.dma("gpsimd", dap(xres, n * 512 * D, [(D, 128), (128 * D, 4), (1, D)]), xb_t, reads=[xb], sem_of=xb, writes=[d_xres[n]])

    try:
        for l in range(L):
            emit_layer(l)
    except _Stop:
        pass
    P.wait("gpsimd", d_out.r_waits() + d_dump.r_waits())
    P.op("gpsimd", lambda e: e.memset(small_t[:, 10:11], 0.0), writes=[])

    with nc.Block() as block:
        @block.sync
        def _(e):
            P.replay("sync", e)

        @block.scalar
        def _(e):
            P.replay("scalar", e)

        @block.vector
        def _(e):
            P.replay("vector", e)

        @block.gpsimd
        def _(e):
            P.replay("gpsimd", e)

        @block.tensor
        def _(e):
            P.replay("tensor", e)
    return nc


def _lambda_init(layer):
    import math
    return 0.8 - 0.6 * math.exp(-0.3 * layer)


def _core_tables(j, NT, seq):
    T = NT * 512
    i = np.arange(T)
    pos = ((4 * (i // 128) + j) * 128 + i % 128).astype(np.float32)
    inv_freq = (np.float32(10000.0) ** (-np.arange(0, 64, 2, dtype=np.float32) / np.float32(64))).astype(np.float32)
    ang = pos[None, :] * inv_freq[:, None]
    cos = np.cos(ang).astype(np.float32)
    sin = np.sin(ang).astype(np.float32)
    idx = np.arange(128) % 32
    return np.ascontiguousarray(cos[idx]), np.ascontiguousarray(sin[idx])


def _consts():
    c = np.zeros((128, 3, 128), np.float32)
    c[:, 0, :] = np.eye(128, dtype=np.float32)
    for p in range(128):
        if p % 64 < 32:
            c[p + 32, 1, p] = -1.0
        else:
            c[p - 32, 1, p] = 1.0
    c[:, 2, :] = 1.0
    return c


def _masks(j):
    m = np.zeros((128, 4, 128), np.float32)
    for t in range(4):
        if t < j:
            m[:, t, :] = 1.0
        elif t == j:
            m[:, t, :] = np.triu(np.ones((128, 128), np.float32))
    return m


_PROG_CACHE = {}


def run_layers(x, params, layers, NT):
    L = len(layers)
    key = (L, NT)
    if key not in _PROG_CACHE:
        _PROG_CACHE[key] = build_program(L, NT)
    nc = _PROG_CACHE[key]
    S = 4 * NT * 512
    NBLK = S // 128
    consts = _consts()
    linit = np.array([[_lambda_init(l), 1.0 - _lambda_init(l)] for l in layers], np.float32)
    sl = {k: np.ascontiguousarray(v[list(layers)]) for k, v in params.items()}
    in_maps = []
    for c in range(8):
        b, j = c // 4, c % 4
        xc = np.ascontiguousarray(x[b].reshape(NBLK, 128, D)[j::4].reshape(NT * 512, D))
        cosT, sinT = _core_tables(j, NT, S)
        m = dict(sl)
        m.update({"x": xc, "cosT": cosT, "sinT": sinT, "masks": _masks(j), "consts": consts, "linit": linit})
        in_maps.append(m)
    res = run_bass_kernel_spmd(nc, in_maps, core_ids=list(range(8)))
    LAST_RES[0] = res
    out = np.empty((2, S, D), np.float32)
    for c in range(8):
        b, j = c // 4, c % 4
        out[b].reshape(NBLK, 128, D)[j::4] = res.results[c]["out"].reshape(NT * 4, 128, D)
    return out


FUSED = True


def kernel(x, w_in, gate_b, sgu_ln_g, sgu_ln_b, sgu_w, sgu_b, lam, diff_ln_g, w_branch, w_out,
           ffn_w1, ffn_w3, ffn_w2, ln_g, ln_b):
    params = dict(w_in=w_in, gate_b=gate_b, sgu_ln_g=sgu_ln_g, sgu_ln_b=sgu_ln_b, sgu_w=sgu_w, sgu_b=sgu_b,
                  lam=lam, diff_ln_g=diff_ln_g, w_branch=w_branch, w_out=w_out, ffn_w1=ffn_w1, ffn_w3=ffn_w3,
                  ffn_w2=ffn_w2, ln_g=ln_g, ln_b=ln_b)
    params = {k: np.asarray(v, np.float32) for k, v in params.items()}
    x = np.asarray(x, np.float32)
    NT = x.shape[1] // 2048
    depth = params["w_in"].shape[0]
    if FUSED:
        return run_layers(x, params, list(range(depth)), NT)
    for l in range(depth):
        x = run_layers(x, params, [l], NT)
    return x
```

```python
import numpy as np
from contextlib import ExitStack
import concourse.bass as bass
import concourse.mybir as mybir
from concourse.bass_utils import run_bass_kernel_spmd

F32 = mybir.dt.float32
BF16 = mybir.dt.bfloat16
AF = mybir.ActivationFunctionType
ALU = mybir.AluOpType

D = 1024
DFF = 2816
NFC = 22
INW = 7168
DEPTH = 4
SEQ = 16384
BATCH = 2
ALPHA = (2.0 * DEPTH) ** 0.25
EPS = 1e-5
ENGS = ("sync", "scalar", "vector", "gpsimd", "tensor")
SAME_ENGINE_SYNC = True


class Sem:
    LIMIT = 30000

    def __init__(self, P, name):
        self.P, self.name, self.n = P, name, 0
        self.new()

    def new(self):
        self.h = self.P.stack.enter_context(self.P.nc.semaphore(f"{self.name}_{self.n}"))
        self.n += 1
        self.val = 0

    def inc(self, amt, eng):
        if self.val + amt > self.LIMIT:
            self.new()
        self.val += amt
        return (self.h, self.val, eng)


class Arena:
    def __init__(self):
        self.bufs = []


class Buf:
    def __init__(self, P, name, ap=None, arena=None, phase=0):
        self.P, self.name, self.ap = P, name, ap
        self.wset = {}
        self.readers = {}
        self.dsem = None
        self.arena, self.phase = arena, phase
        if arena is not None:
            arena.bufs.append(self)

    def sem(self):
        if self.dsem is None:
            self.dsem = Sem(self.P, "d_" + self.name)
        return self.dsem

    def add_reader(self, tk):
        k = id(tk[0])
        if k not in self.readers or self.readers[k][1] < tk[1]:
            self.readers[k] = tk

    def set_writer(self, tk):
        self.wset = {id(tk[0]): tk}
        self.readers = {}

    def add_writer(self, tk):
        k = id(tk[0])
        if k not in self.wset or self.wset[k][1] < tk[1]:
            self.wset[k] = tk

    def r_waits(self):
        return list(self.wset.values())

    def w_waits(self):
        w = list(self.readers.values()) + list(self.wset.values())
        if self.arena is not None:
            for o in self.arena.bufs:
                if o.phase != self.phase:
                    w += list(o.readers.values()) + list(o.wset.values())
        return w


class Prog:
    def __init__(self, nc):
        self.nc = nc
        self.stack = ExitStack()
        self.ops = {e: [] for e in ENGS}
        self.waited = {e: {} for e in ENGS}
        self.prog = {e: Sem(self, "p_" + e) for e in ENGS}
        self.nbuf = 0

    def _waits(self, eng, tks):
        for tk in tks:
            if tk is None:
                continue
            h, v, teng = tk
            if teng == eng and (eng == "tensor" or not SAME_ENGINE_SYNC):
                continue
            k = id(h)
            if self.waited[eng].get(k, 0) >= v:
                continue
            self.waited[eng][k] = v
            self.ops[eng].append(("w", h, v))

    def op(self, eng, fns, reads=(), writes=(), accum=()):
        if not isinstance(fns, (list, tuple)):
            fns = [fns]
        tks = []
        for b in reads:
            tks += b.r_waits()
        for b in accum:
            tks += b.r_waits()
        for b in writes:
            tks += b.w_waits()
        self._waits(eng, tks)
        tk = self.prog[eng].inc(1, eng)
        for f in fns[:-1]:
            self.ops[eng].append(("o", f, None, 0))
        self.ops[eng].append(("o", fns[-1], tk[0], 1))
        for b in reads:
            b.add_reader(tk)
        for b in list(writes) + list(accum):
            b.set_writer(tk)
        return tk

    def dma(self, eng, out, in_, reads=(), writes=(), nowait_dst=False, sem_of=None, **kw):
        tks = []
        for b in reads:
            tks += b.r_waits()
        for b in writes:
            if not nowait_dst:
                tks += b.w_waits()
        self._waits(eng, tks)
        wb = sem_of if sem_of is not None else writes[0]
        tk = wb.sem().inc(16, "dma")
        self.ops[eng].append(("o", lambda e: e.dma_start(out=out, in_=in_, **kw), tk[0], 16))
        for b in reads:
            b.add_reader(tk)
        for b in writes:
            if nowait_dst:
                b.add_writer(tk)
            else:
                b.set_writer(tk)
        return tk

    def wait(self, eng, tks):
        self._waits(eng, tks)

    def replay(self, eng, e):
        for o in self.ops[eng]:
            if o[0] == "w":
                e.wait_ge(o[1], o[2])
            else:
                ins = o[1](e)
                if o[2] is not None:
                    ins.then_inc(o[2], o[3])


class _Stop(Exception):
    pass


DEBUG_STAGE = None
DEBUG_DUMP = False
LAST_RES = [None]


def build_program(L, NT):
    T = NT * 512
    NB = NT * 4
    nc = bass.Bass("TRN2", target_bir_lowering=False)
    P = Prog(nc)
    st = P.stack

    def dram(name, shape, dtype, kind=None):
        if kind is None:
            return nc.dram_tensor(name, shape, dtype)
        return nc.dram_tensor(name, shape, dtype, kind=kind)

    def dap(h, off, pat):
        return bass.AP(h, off, [list(p) for p in pat])

    EI = "ExternalInput"
    x_in = dram("x", [T, D], F32, EI)
    out_d = dram("out", [T, D], F32, "ExternalOutput")
    w_in_d = dram("w_in", [L, D, INW], F32, EI)
    gate_b_d = dram("gate_b", [L, 2, D], F32, EI)
    sgu_ln_g_d = dram("sgu_ln_g", [L, D], F32, EI)
    sgu_ln_b_d = dram("sgu_ln_b", [L, D], F32, EI)
    sgu_w_d = dram("sgu_w", [L, 8, 128, 128], F32, EI)
    sgu_b_d = dram("sgu_b", [L, 8, 128], F32, EI)
    lam_d = dram("lam", [L, 4, 64], F32, EI)
    diff_g_d = dram("diff_ln_g", [L, 128], F32, EI)
    w_br_d = dram("w_branch", [L, 2, D, D], F32, EI)
    w_out_d = dram("w_out", [L, D, D], F32, EI)
    w1_d = dram("ffn_w1", [L, 2, D, DFF], F32, EI)
    w3_d = dram("ffn_w3", [L, 2, D, DFF], F32, EI)
    w2_d = dram("ffn_w2", [L, 2, DFF, D], F32, EI)
    ln_g_d = dram("ln_g", [L, 3, D], F32, EI)
    ln_b_d = dram("ln_b", [L, 3, D], F32, EI)
    cos_d = dram("cosT", [128, T], F32, EI)
    sin_d = dram("sinT", [128, T], F32, EI)
    mask_d = dram("masks", [128, 4, 128], F32, EI)
    const_d = dram("consts", [128, 3, 128], F32, EI)
    linit_d = dram("linit", [L, 2], F32, EI)

    wib = dram("wib", [L, D, INW], BF16)
    w1b = dram("w1b", [L, 2, D, DFF], BF16)
    w3b = dram("w3b", [L, 2, D, DFF], BF16)
    w2b = dram("w2b", [L, 2, DFF, D], BF16)
    wbrb = dram("wbrb", [L, 2, D, D], BF16)
    wob = dram("wob", [L, D, D], BF16)
    xres = dram("xres", [T, D], F32)
    DK = "ExternalOutput" if DEBUG_DUMP else None
    q_s = dram("q_s", [8, 128, T], BF16, DK)
    kT_l = [dram(f"kT_l{n}", [8 * 128, 512], BF16) for n in range(NT)]
    v_l = [dram(f"v_l{n}", [8 * 128, 512], BF16) for n in range(NT)]
    kT_g = [dram(f"kT_g{n}", [4 * 8 * 128, 512], BF16) for n in range(NT)]
    v_g = [dram(f"v_g{n}", [4 * 8 * 128, 512], BF16) for n in range(NT)]
    mag_s = dram("mag_s", [8, 128, T], F32, DK)
    gb_s = dram("gb_s", [8, 128, T], F32, DK)
    if DEBUG_DUMP:
        dump_hb = dram("dump_hb", [128, 8 * 512], BF16, DK)

    def sb(name, shape, dtype):
        return st.enter_context(nc.sbuf_tensor("s_" + name, shape, dtype)).ap()

    def B(name, ap=None, arena=None, phase=0):
        return Buf(P, name, ap, arena, phase)

    consts_t = sb("consts", [128, 3, 128], F32)
    consts = B("consts", consts_t)
    ident = consts_t[:, 0, :]
    perm = consts_t[:, 1, :]
    ones32 = consts_t[:, 2, :]
    maskb_t = sb("maskb", [128, 4, 128], BF16)
    maskb = B("maskb", maskb_t)
    lnp_t = sb("lnp", [128, 4, 1024], F32)
    lnp = B("lnp", lnp_t)
    wmT_t = sb("wmT", [128, 8, 128], BF16)
    wmT = B("wmT", wmT_t)
    bS_t = sb("bS", [128, 8, 128], F32)
    bS = B("bS", bS_t)
    small_t = sb("small", [128, 64], F32)
    small = B("small", small_t)
    gateb_t = sb("gateb", [128, 16], F32)
    gateb = B("gateb", gateb_t)
    lam_t = sb("lamt", [128, 256], F32)
    lamb = B("lamb", lam_t)
    xa_t = sb("xa", [128, 4, 1024], F32)
    xb_t = sb("xb", [128, 4, 1024], F32)
    xa = B("xa", xa_t)
    xb = B("xb", xb_t)
    xT_t = [sb("xT0", [128, 8, 512], BF16)] * 2
    xT = [B("xT0", xT_t[0])] * 2
    NW = 4
    wr_t = [sb(f"wr{i}", [128, 4096], BF16) for i in range(NW)]
    wr = [B(f"wr{i}", wr_t[i]) for i in range(NW)]
    gT_t = sb("gT", [128, 22, 512], BF16)
    gT = B("gT", gT_t)
    bA_t = sb("bA", [128, 8, 512], BF16)
    bA = B("bA", bA_t)
    bB_t = sb("bB", [128, 4096], BF16)
    bB = B("bB", bB_t)
    vn_t = bB_t[:, :].rearrange("p (a b) -> p a b", b=1024)
    QT_t = bB_t[:, :].rearrange("p (a b) -> p a b", b=512)
    bC_t = sb("bC", [128, 8, 512], BF16)
    bC = B("bC", bC_t)
    s2_t = [sb(f"s2_{i}", [128, 512], F32) for i in range(6)]
    s2 = [B(f"s2_{i}", s2_t[i]) for i in range(6)]
    r2_t = [sb(f"r2_{i}", [128, 1024], F32) for i in range(2)]
    r2 = [B(f"r2_{i}", r2_t[i]) for i in range(2)]
    vtile_t = sb("vtile", [128, 1024], F32)
    vtile = B("vtile", vtile_t)
    sgw_t = vtile_t[:, :].rearrange("p (a b) -> p a b", b=128)
    sgw = vtile
    mask32_t = vtile_t[:, 0:512].rearrange("p (a b) -> p a b", b=128)
    mask32 = vtile
    L2s_t = sb("L2s", [128, 512], F32)
    L2s = B("L2s", L2s_t)
    cosb_t = sb("cosb", [128, 512], F32)
    sinb_t = sb("sinb", [128, 512], F32)
    cosb = B("cosb", cosb_t)
    sinb = B("sinb", sinb_t)
    qks_t = [sb(f"qks{i}", [128, 512], BF16) for i in range(2)]
    qks = [B(f"qks{i}", qks_t[i]) for i in range(2)]
    vsl_t = [sb(f"vsl{i}", [128, 1024], BF16) for i in range(2)]
    vsl = [B(f"vsl{i}", vsl_t[i]) for i in range(2)]
    NKV = 6
    kv_t = [sb(f"kv{i}", [128, 1024], BF16) for i in range(NKV)]
    kv = [B(f"kv{i}", kv_t[i]) for i in range(NKV)]
    NPT = 4
    LS = 384
    pt_t = [sb(f"pt{i}", [128, 2, 512], BF16) for i in range(NPT)]
    pt = [B(f"pt{i}", pt_t[i]) for i in range(NPT)]
    stat_t = sb("stat", [128, 4, 16], F32)
    stats = [B(f"stat{i}", stat_t[:, i, :]) for i in range(4)]
    cb_t = sb("cb", [128, 2, 128], BF16)
    cb = B("cb", cb_t)
    permb = cb_t[:, 0, :]
    onesb = cb_t[:, 1, :]
    hl_t = [sb(f"hl{i}", [128, 512], BF16) for i in range(4)]
    hl = [B(f"hl{i}", hl_t[i]) for i in range(4)]
    lamp_t = sb("lamp", [128, 128], F32)
    lamp = B("lamp", lamp_t)
    linit_t = sb("linit", [128, 2 * L], F32)
    linit = B("linit", linit_t)
    lnp = [B(f"lnp{i}", lnp_t[:, i, :]) for i in range(4)]

    pp_t = [st.enter_context(nc.psum_tensor(f"pp{i}", [128, 2, 512], F32)).ap() for i in range(4)]
    ps_t = [pp_t[i // 2][:, i % 2, :] for i in range(8)]
    ps = [B(f"ps{i}", ps_t[i]) for i in range(8)]
    psrr = [0]

    def next_ps():
        i = psrr[0] % 8
        psrr[0] += 1
        return ps[i]

    d_w = {}

    def dwb(hn, l, sub=0):
        k = (hn, l, sub)
        if k not in d_w:
            d_w[k] = B(f"d_w_{hn}_{l}_{sub}")
        return d_w[k]
    d_xres = [B(f"d_xres{n}") for n in range(NT)]
    d_q = [B(f"d_q{n}") for n in range(NT)]
    d_mag = [B(f"d_mag{n}") for n in range(NT)]
    d_gb = [B(f"d_gb{n}") for n in range(NT)]
    d_kl = [B(f"d_kl{n}") for n in range(NT)]
    d_vl = [B(f"d_vl{n}") for n in range(NT)]
    d_kg = [B(f"d_kg{n}") for n in range(NT)]
    d_vg = [B(f"d_vg{n}") for n in range(NT)]
    d_out = B("d_out")
    ccsem = Sem(P, "cc")
    d_dump = B("d_dump")

    def cast_pieces(l):
        pieces = []

        def cp(src, dst, off, n, buf):
            rows = n // 1024
            r0 = 0
            while r0 < rows:
                r = min(8192, rows - r0)
                pieces.append(lambda r0=r0, r=r: P.dma(
                    "gpsimd", dap(dst, off + r0 * 1024, [(1024, r), (1, 1024)]),
                    dap(src, off + r0 * 1024, [(1024, r), (1, 1024)]), writes=[buf], nowait_dst=True))
                r0 += r
        for slot in range(2):
            o = (l * 2 + slot) * D * DFF
            cp(w1_d, w1b, o, D * DFF, dwb("w1b", l, slot))
            cp(w3_d, w3b, o, D * DFF, dwb("w3b", l, slot))
            cp(w2_d, w2b, o, D * DFF, dwb("w2b", l, slot))
            if slot == 0:
                cp(w_in_d, wib, l * D * INW, D * INW, dwb("wib", l))
                for br in range(2):
                    cp(w_br_d, wbrb, (l * 2 + br) * D * D, D * D, dwb("wbrb", l, br))
                cp(w_out_d, wob, l * D * D, D * D, dwb("wob", l))
        return pieces

    def wbuf_of(e):
        l, hn, off = e[0], e[1], e[2]
        if hn in ("w1b", "w3b", "w2b"):
            return dwb(hn, l, (off - l * 2 * D * DFF) // (D * DFF))
        if hn == "wbrb":
            return dwb(hn, l, (off - l * 2 * D * D) // (D * D))
        return dwb(hn, l)

    def ffn_plan(l, slot):
        base1 = (l * 2 + slot) * D * DFF
        for c in range(6):
            nf = 4 if c < 5 else 2
            yield (l, "w1b", base1 + c * 512, DFF, 8, nf * 128)
            yield (l, "w3b", base1 + c * 512, DFF, 8, nf * 128)
        for c in range(6):
            nf = 4 if c < 5 else 2
            yield (l, "w2b", (l * 2 + slot) * DFF * D + c * 512 * D, D, nf, 1024)

    def layer_plan(l):
        wbase = l * D * INW
        for n in range(NT):
            yield from ffn_plan(l, 0)
            for off in (0, 512, 1024, 1536):
                yield (l, "wib", wbase + off, INW, 8, 512)
            for hf in range(2):
                yield (l, "wib", wbase + 6144 + hf * 512, INW, 8, 512)
            for which in range(2):
                for hf in range(2):
                    yield (l, "wib", wbase + 2048 + which * 1024 + hf * 512, INW, 8, 512)
            yield (l, "wib", wbase + 4096, INW, 8, 512)
            yield (l, "wib", wbase + 4608, INW, 8, 512)
            for hf in range(2):
                yield (l, "wib", wbase + 5120 + hf * 512, INW, 8, 512)
                yield (l, "wbrb", (l * 2 + 0) * D * D + hf * 512, D, 8, 512)

        def m2_plan():
            for hf in range(2):
                yield (l, "wbrb", (l * 2 + 1) * D * D + hf * 512, D, 8, 512)
            yield (l, "wob", l * D * D + 0, D, 8, 512)
            yield (l, "wob", l * D * D + 512, D, 8, 512)
        yield from m2_plan()
        for n in range(1, NT):
            yield from ffn_plan(l, 1)
            yield from m2_plan()
        yield from ffn_plan(l, 1)

    wplan = [e for l in range(L) for e in layer_plan(l)]
    whandles = {"w1b": w1b, "w3b": w3b, "w2b": w2b, "wib": wib, "wbrb": wbrb, "wob": wob}
    wcount = [0]
    wissued = [0]
    WLOOK = NW - 2

    def wissue(k):
        l, hn, off, rowstride, nk, ncols = wplan[k]
        i = k % NW
        view = wr_t[i][:, 0:nk * ncols].rearrange("p (a b) -> p a b", b=ncols)
        P.dma("sync", view, dap(whandles[hn], off, [(rowstride, 128), (128 * rowstride, nk), (1, ncols)]),
              reads=[wbuf_of(wplan[k])], writes=[wr[i]])

    def wload(l, handle, off, rowstride, nk, ncols):
        k = wcount[0]
        wcount[0] += 1
        if DEBUG_STAGE is None:
            e = wplan[k]
            assert e[0] == l and whandles[e[1]] is handle and e[2:] == (off, rowstride, nk, ncols), (k, e, l, off)
            while wissued[0] <= min(k + WLOOK, len(wplan) - 1):
                wissue(wissued[0])
                wissued[0] += 1
        else:
            wplan[k:k + 1] = [(l, [n_ for n_, h_ in whandles.items() if h_ is handle][0], off, rowstride, nk, ncols)]
            wissue(k)
        i = k % NW
        view = wr_t[i][:, 0:nk * ncols].rearrange("p (a b) -> p a b", b=ncols)
        return wr[i], view

    def mm(out_ap, lhsT, rhs, start, stop):
        return lambda e: e.matmul(out_ap, lhsT, rhs, start=start, stop=stop)

    def act(out, in_, func, **kw):
        return lambda e: e.activation(out=out, in_=in_, func=func, **kw)

    def tt(out, in0, in1, op):
        return lambda e: e.tensor_tensor(out=out, in0=in0, in1=in1, op=op)

    def stt(out, in0, scalar, in1, op0, op1):
        return lambda e: e.scalar_tensor_tensor(out=out, in0=in0, scalar=scalar, in1=in1, op0=op0, op1=op1)

    def transposes_to_xT(src_t, src_b, di):
        for dc in range(8):
            pb = next_ps()
            fns = [lambda e, s=s, dc=dc, pb=pb: e.transpose(pb.ap[:, s * 128:(s + 1) * 128],
                                                           src_t[:, s, dc * 128:(dc + 1) * 128], ident)
                   for s in range(4)]
            P.op("tensor", fns, reads=[src_b, consts], writes=[pb])
            kw = dict(writes=[xT[di]]) if dc == 0 else dict(accum=[xT[di]])
            P.op("scalar", act(xT_t[di][:, dc, :], pb.ap, AF.Copy), reads=[pb], **kw)

    lncnt = [0]

    def layernorm(src_b, src_ap, dst_b, dst_ap, gi, bi, eps_ap, dst_first):
        sb_ = stats[lncnt[0] % 4]
        lncnt[0] += 1
        s_ap = sb_.ap
        P.op("vector", [lambda e: e.bn_stats(out=s_ap[:, 0:6], in_=src_ap[:, 0:512]),
                        lambda e: e.bn_stats(out=s_ap[:, 6:12], in_=src_ap[:, 512:1024])],
             reads=[src_b], writes=[sb_])
        P.op("vector", lambda e: e.bn_aggr(out=s_ap[:, 12:14], in_=s_ap[:, 0:12]), reads=[sb_], writes=[sb_])
        P.op("scalar", act(s_ap[:, 14:15], s_ap[:, 13:14], AF.Sqrt, bias=eps_ap, scale=1.0),
             reads=[small, sb_], writes=[sb_])
        P.op("vector", lambda e: e.reciprocal(out=s_ap[:, 14:15], in_=s_ap[:, 14:15]), reads=[sb_], writes=[sb_])
        P.op("vector", stt(s_ap[:, 15:16], s_ap[:, 12:13], -1.0, s_ap[:, 14:15], ALU.mult, ALU.mult),
             reads=[sb_], writes=[sb_])
        P.op("scalar", act(src_ap, src_ap, AF.Identity, scale=s_ap[:, 14:15], bias=s_ap[:, 15:16]),
             reads=[src_b, sb_], writes=[src_b])
        aff = "gpsimd" if lncnt[0] % 2 == 0 else "vector"
        P.op(aff, tt(src_ap, src_ap, lnp_t[:, gi, :], ALU.mult), reads=[src_b, lnp[gi]], writes=[src_b])
        kw = dict(writes=[dst_b]) if dst_first else dict(accum=[dst_b])
        P.op(aff, tt(dst_ap, src_ap, lnp_t[:, bi, :], ALU.add), reads=[src_b, lnp[bi]], **kw)

    def ffn(l, slot, xi, res_b, res_t, dst_b, dst_t, gi, bi):
        base1 = (l * 2 + slot) * D * DFF
        first = True
        for c in range(6):
            nf = 4 if c < 5 else 2
            w1buf, w1v = wload(l, w1b, base1 + c * 512, DFF, 8, nf * 128)
            w3buf, w3v = wload(l, w3b, base1 + c * 512, DFF, 8, nf * 128)
            for j in range(nf):
                fc = c * 4 + j
                p1 = next_ps()
                p3 = next_ps()
                P.op("tensor", [mm(p1.ap, w1v[:, k, j * 128:(j + 1) * 128], xT_t[xi][:, k, :], k == 0, k == 7)
                                for k in range(8)], reads=[w1buf, xT[xi]], writes=[p1])
                P.op("tensor", [mm(p3.ap, w3v[:, k, j * 128:(j + 1) * 128], xT_t[xi][:, k, :], k == 0, k == 7)
                                for k in range(8)], reads=[w3buf, xT[xi]], writes=[p3])
                sbf = s2[fc % 4]
                P.op("scalar", act(sbf.ap, p1.ap, AF.Silu), reads=[p1], writes=[sbf])
                kw = dict(writes=[gT]) if first else dict(accum=[gT])
                first = False
                P.op("vector", tt(gT_t[:, fc, :], sbf.ap, p3.ap, ALU.mult), reads=[sbf, p3], **kw)
        for c in range(6):
            nf = 4 if c < 5 else 2
            w2buf, w2v = wload(l, w2b, (l * 2 + slot) * DFF * D + c * 512 * D, D, nf, 1024)
            for s in range(4):
                for hh in range(2):
                    pb = ps[s * 2 + hh]
                    fns = [mm(pb.ap, gT_t[:, c * 4 + j, s * 128:(s + 1) * 128], w2v[:, j, hh * 512:(hh + 1) * 512],
                              c == 0 and j == 0, c == 5 and j == nf - 1) for j in range(nf)]
                    if c == 0:
                        P.op("tensor", fns, reads=[w2buf, gT], writes=[pb])
                    else:
                        P.op("tensor", fns, reads=[w2buf, gT], accum=[pb])
        for s in range(4):
            rb = r2[s % 2]
            for hh in range(2):
                pb = ps[s * 2 + hh]
                kw = dict(writes=[rb]) if hh == 0 else dict(accum=[rb])
                P.op("vector", stt(rb.ap[:, hh * 512:(hh + 1) * 512], res_t[:, s, hh * 512:(hh + 1) * 512],
                                   2.0 * ALPHA, pb.ap, ALU.mult, ALU.add), reads=[res_b, pb], **kw)
            layernorm(rb, rb.ap, dst_b, dst_t[:, s, :], gi, bi, small_t[:, 1:2], dst_first=(s == 0))

    def load_lnp(parts):
        for i, (h, off) in enumerate(parts):
            P.dma("sync", lnp_t[:, i, :], dap(h, off, [(0, 128), (1, 1024)]), writes=[lnp[i]])

    P.dma("sync", consts_t, const_d.ap(), writes=[consts])
    P.dma("sync", mask32_t, mask_d.ap(), writes=[mask32])
    P.op("vector", lambda e: e.tensor_copy(out=maskb_t, in_=mask32_t), reads=[mask32], writes=[maskb])
    P.op("gpsimd", [lambda e: e.memset(small_t[:, 0:1], EPS), lambda e: e.memset(small_t[:, 1:2], 4 * EPS)],
         writes=[small])
    P.dma("sync", linit_t, dap(linit_d, 0, [(0, 128), (1, 2 * L)]), writes=[linit])
    P.op("vector", lambda e: e.tensor_copy(out=cb_t, in_=consts_t[:, 1:3, :]), reads=[consts], writes=[cb])

    def split_sum(out_b, lhsT_b16, src_ap, src_bs, i0):
        hi, lo = hl[i0], hl[i0 + 1]
        P.op("scalar", act(hi.ap, src_ap, AF.Copy), reads=src_bs, writes=[hi])
        P.op("vector", tt(lo.ap, src_ap, hi.ap, ALU.subtract), reads=list(src_bs) + [hi], writes=[lo])
        P.op("tensor", [mm(out_b.ap, lhsT_b16, hi.ap, True, False), mm(out_b.ap, lhsT_b16, lo.ap, False, True)],
             reads=[cb, hi, lo], writes=[out_b])
    for pc in cast_pieces(0):
        pc()

    cur_n = [0]

    def stage(name, src_t=None, src_b=None):
        if DEBUG_STAGE == name or DEBUG_STAGE == f"{name}@{cur_n[0]}":
            if src_t is not None:
                P.dma("gpsimd", dap(out_d, 0, [(D, 128), (128 * D, 4), (1, D)]), src_t, reads=[src_b], sem_of=src_b, writes=[d_out], nowait_dst=True)
            raise _Stop()

    def emit_layer(l):
        src_h = x_in if l == 0 else xres
        load_lnp([(ln_g_d, (l * 3 + 0) * D), (ln_b_d, (l * 3 + 0) * D), (sgu_ln_g_d, l * D), (sgu_ln_b_d, l * D)])
        with nc.allow_non_contiguous_dma(reason="small param loads"):
            P.dma("sync", sgw_t, dap(sgu_w_d, l * 8 * 128 * 128, [(128, 128), (128 * 128, 8), (1, 128)]), writes=[sgw])
            P.dma("sync", bS_t, dap(sgu_b_d, l * 8 * 128, [(0, 128), (128, 8), (1, 128)]), writes=[bS])
            P.dma("sync", gateb_t, dap(gate_b_d, l * 2 * D, [(1, 128), (128, 16)]), writes=[gateb], allow_slow_non_contiguous=True)
            P.dma("sync", lam_t, dap(lam_d, l * 256, [(0, 128), (1, 256)]), writes=[lamb])
            P.dma("sync", small_t[:, 2:3], dap(diff_g_d, l * 128, [(1, 128), (1, 1)]), writes=[small])
        for g in range(8):
            pb = next_ps()
            P.op("tensor", lambda e, g=g, pb=pb: e.transpose(pb.ap[:, 0:128], sgw_t[:, g, :], ident),
                 reads=[sgw, consts], writes=[pb])
            sbf = s2[g % 4]
            P.op("scalar", act(sbf.ap[:, 0:128], pb.ap[:, 0:128], AF.Copy), reads=[pb], writes=[sbf])
            kw = dict(writes=[wmT]) if g == 0 else dict(accum=[wmT])
            P.op("gpsimd", lambda e, g=g, sbf=sbf: e.affine_select(
                out=wmT_t[:, g, :], in_=sbf.ap[:, 0:128], pattern=[[1, 128]], base=0, channel_multiplier=-1,
                compare_op=ALU.is_ge, fill=0.0), reads=[sbf], **kw)
        P.op("vector", [tt(lamp_t[:, 0:64], lam_t[:, 0:64], lam_t[:, 64:128], ALU.mult),
                        tt(lamp_t[:, 64:128], lam_t[:, 128:192], lam_t[:, 192:256], ALU.mult)],
             reads=[lamb], writes=[lamp])
        P.op("scalar", [act(lamp_t[:, 0:64], lamp_t[:, 0:64], AF.Identity, accum_out=small_t[:, 5:6]),
                        act(lamp_t[:, 64:128], lamp_t[:, 64:128], AF.Identity, accum_out=small_t[:, 6:7])],
             reads=[lamp], writes=[small, lamp])
        P.op("scalar", act(small_t[:, 7:9], small_t[:, 5:7], AF.Exp), reads=[small], writes=[small])
        P.op("vector", tt(small_t[:, 9:10], small_t[:, 7:8], small_t[:, 8:9], ALU.subtract), reads=[small], writes=[small])
        P.op("vector", stt(small_t[:, 4:5], small_t[:, 9:10], -1.0, linit_t[:, 2 * l:2 * l + 1], ALU.mult, ALU.subtract),
             reads=[small, linit], writes=[small])
        P.op("vector", tt(small_t[:, 3:4], small_t[:, 2:3], linit_t[:, 2 * l + 1:2 * l + 2], ALU.mult),
             reads=[small, linit], writes=[small])
        neglam = small_t[:, 4:5]
        gcoef = small_t[:, 3:4]

        cc_pending = []
        nxt_casts = cast_pieces(l + 1) if l + 1 < L else []
        for n in range(NT):
            cur_n[0] = n
            tsl = slice(n * 512, (n + 1) * 512)
            P.dma("sync", xa_t, dap(src_h, n * 512 * D, [(D, 128), (128 * D, 4), (1, D)]),
                  reads=[d_xres[n]] if l > 0 else [], writes=[xa])
            P.dma("sync", cosb_t, cos_d.ap()[:, tsl], writes=[cosb])
            P.dma("sync", sinb_t, sin_d.ap()[:, tsl], writes=[sinb])
            stage('A', xa_t, xa)
            transposes_to_xT(xa_t, xa, 0)
            ffn(l, 0, 0, xa, xa_t, xb, xb_t, 0, 1)
            stage('B', xb_t, xb)
            P.dma("gpsimd", dap(xres, n * 512 * D, [(D, 128), (128 * D, 4), (1, D)]), xb_t, reads=[xb], sem_of=xb, writes=[d_xres[n]])
            transposes_to_xT(xb_t, xb, 1)
            X1 = xT_t[1]
            X1b = xT[1]
            wbase = l * D * INW
            for c2 in range(2):
                wb_, wv = wload(l, wib, wbase + 0 + c2 * 512, INW, 8, 512)
                for j in range(4):
                    c = c2 * 4 + j
                    pb = next_ps()
                    P.op("tensor", [mm(pb.ap, wv[:, k, j * 128:(j + 1) * 128], X1[:, k, :], k == 0, k == 7) for k in range(8)],
                         reads=[wb_, X1b], writes=[pb])
                    kw = dict(writes=[bA]) if c == 0 else dict(accum=[bA])
                    P.op("scalar", act(bA_t[:, c, :], pb.ap, AF.Gelu), reads=[pb], **kw)
            stage('C1', xb_t, xb)
            wv0b, wv0 = wload(l, wib, wbase + 1024, INW, 8, 512)
            wv1b, wv1 = wload(l, wib, wbase + 1536, INW, 8, 512)
            for s in range(4):
                for hh, (wb_, wv) in enumerate([(wv0b, wv0), (wv1b, wv1)]):
                    pb = next_ps()
                    P.op("tensor", [mm(pb.ap, X1[:, k, s * 128:(s + 1) * 128], wv[:, k, :], k == 0, k == 7) for k in range(8)],
                         reads=[wb_, X1b], writes=[pb])
                    kw = dict(writes=[vtile]) if hh == 0 else dict(accum=[vtile])
                    P.op("scalar", act(vtile_t[:, hh * 512:(hh + 1) * 512], pb.ap, AF.Gelu), reads=[pb], **kw)
                layernorm(vtile, vtile_t, bB, vn_t[:, s, :], 2, 3, small_t[:, 0:1], dst_first=(s == 0))
            stage('C4', xb_t, xb)
            for hf in range(2):
                wgb, wg = wload(l, wib, wbase + 6144 + hf * 512, INW, 8, 512)
                for j in range(4):
                    c = hf * 4 + j
                    pg = next_ps()
                    P.op("tensor", [mm(pg.ap, wg[:, k, j * 128:(j + 1) * 128], X1[:, k, :], k == 0, k == 7) for k in range(8)],
                         reads=[wgb, X1b], writes=[pg])
                    gs = s2[(2 * c + 1) % 6]
                    P.op("scalar", act(gs.ap, pg.ap, AF.Sigmoid, bias=gateb_t[:, 8 + c:9 + c], scale=1.0),
                         reads=[pg, gateb], writes=[gs])
                    P.dma("gpsimd", dap(gb_s, c * 128 * T + n * 512, [(T, 128), (1, 512)]), gs.ap, reads=[gs], sem_of=gs,
                          writes=[d_gb[n]], nowait_dst=(c > 0))
            stage('C5', xb_t, xb)
            for which in range(2):
                for hf in range(2):
                    wqb, wq = wload(l, wib, wbase + 2048 + which * 1024 + hf * 512, INW, 8, 512)
                    for j in range(4):
                        h = hf * 4 + j
                        pq = next_ps()
                        P.op("tensor", [mm(pq.ap, wq[:, k, j * 128:(j + 1) * 128], X1[:, k, :], k == 0, k == 7) for k in range(8)],
                             reads=[wqb, X1b], writes=[pq])
                        psw = next_ps()
                        split_sum(psw, permb, pq.ap, [pq], 0)
                        t1 = s2[(2 * h) % 6]
                        t2 = s2[(2 * h + 1) % 6]
                        P.op("vector", tt(t1.ap, pq.ap, cosb_t, ALU.mult), reads=[pq, cosb], writes=[t1])
                        P.op("vector", tt(t2.ap, psw.ap, sinb_t, ALU.mult), reads=[psw, sinb], writes=[t2])
                        qo = qks[h % 2]
                        P.op("gpsimd", tt(qo.ap, t1.ap, t2.ap, ALU.add), reads=[t1, t2], writes=[qo])
                        if which == 0:
                            P.dma("sync", dap(q_s, h * 128 * T + n * 512, [(T, 128), (1, 512)]), qo.ap, reads=[qo], sem_of=qo,
                                  writes=[d_q[n]], nowait_dst=(h > 0))
                        else:
                            P.dma("sync", dap(kT_l[n], h * 128 * 512, [(512, 128), (1, 512)]), qo.ap, reads=[qo], sem_of=qo,
                                  writes=[d_kl[n]], nowait_dst=(h > 0))
            stage('C6', xb_t, xb)
            wv0b, wv0 = wload(l, wib, wbase + 4096, INW, 8, 512)
            wv1b, wv1 = wload(l, wib, wbase + 4608, INW, 8, 512)
            for s in range(4):
                vo = vsl[s % 2]
                for hh, (wb_, wv) in enumerate([(wv0b, wv0), (wv1b, wv1)]):
                    pb = next_ps()
                    P.op("tensor", [mm(pb.ap, X1[:, k, s * 128:(s + 1) * 128], wv[:, k, :], k == 0, k == 7) for k in range(8)],
                         reads=[wb_, X1b], writes=[pb])
                    kw = dict(writes=[vo]) if hh == 0 else dict(accum=[vo])
                    P.op("scalar", act(vo.ap[:, hh * 512:(hh + 1) * 512], pb.ap, AF.Copy), reads=[pb], **kw)
                P.dma("sync", dap(v_l[n], s * 128, [(512, 128), (128 * 512, 8), (1, 128)]),
                      vo.ap[:, :].rearrange("p (a b) -> p a b", b=128), reads=[vo], sem_of=vo, writes=[d_vl[n]],
                      nowait_dst=(s > 0))
            for pc in nxt_casts[n::NT]:
                pc()
            for (src, dst, sbuf_, dbuf_) in ((kT_l[n], kT_g[n], d_kl[n], d_kg[n]), (v_l[n], v_g[n], d_vl[n], d_vg[n])):
                P.wait("gpsimd", sbuf_.r_waits() + dbuf_.w_waits())
                tk = ccsem.inc(1, "cc")
                P.ops["gpsimd"].append(("o", (lambda e, src=src, dst=dst: e.collective_compute(
                    "AllGather", ALU.bypass, replica_groups=[[0, 1, 2, 3], [4, 5, 6, 7]],
                    ins=[src.ap()], outs=[dst.ap()])), tk[0], 1))
                sbuf_.add_reader(tk)
                cc_pending.append((dbuf_, tk))

            stage('C2', xb_t, xb)
            for g in range(8):
                pb = next_ps()
                P.op("tensor", [mm(pb.ap[:, s * 128:(s + 1) * 128], vn_t[:, s, g * 128:(g + 1) * 128], wmT_t[:, g, :], True, True)
                                for s in range(4)], reads=[bB, wmT], writes=[pb])
                sbf = s2[g % 4]
                bsb = bass.AP(bS_t.tensor, g * 128, [[1024, 128], [0, 4], [1, 128]])
                P.op("vector", tt(sbf.ap[:, :].rearrange("p (a b) -> p a b", b=128),
                                  pb.ap[:, :].rearrange("p (a b) -> p a b", b=128), bsb, ALU.add),
                     reads=[pb, bS], writes=[sbf])
                kw = dict(writes=[bC]) if g == 0 else dict(accum=[bC])
                P.op("gpsimd", tt(bC_t[:, g, :], sbf.ap, bA_t[:, g, :], ALU.mult), reads=[sbf, bA], **kw)
            stage('C3', xb_t, xb)
            for hf in range(2):
                wgb, wg = wload(l, wib, wbase + 5120 + hf * 512, INW, 8, 512)
                wab, wa = wload(l, wbrb, (l * 2 + 0) * D * D + hf * 512, D, 8, 512)
                for j in range(4):
                    c = hf * 4 + j
                    pg = next_ps()
                    P.op("tensor", [mm(pg.ap, wg[:, k, j * 128:(j + 1) * 128], X1[:, k, :], k == 0, k == 7) for k in range(8)],
                         reads=[wgb, X1b], writes=[pg])
                    gs = s2[(2 * c) % 6]
                    P.op("scalar", act(gs.ap, pg.ap, AF.Sigmoid, bias=gateb_t[:, c:c + 1], scale=1.0),
                         reads=[pg, gateb], writes=[gs])
                    pm = next_ps()
                    P.op("tensor", [mm(pm.ap, wa[:, k, j * 128:(j + 1) * 128], bC_t[:, k, :], k == 0, k == 7) for k in range(8)],
                         reads=[wab, bC], writes=[pm])
                    P.op("vector", tt(gs.ap, gs.ap, pm.ap, ALU.mult), reads=[gs, pm], writes=[gs])
                    P.dma("gpsimd", dap(mag_s, c * 128 * T + n * 512, [(T, 128), (1, 512)]), gs.ap, reads=[gs], sem_of=gs,
                          writes=[d_mag[n]], nowait_dst=(c > 0))
        cur_n[0] = -1
        stage('C', xb_t, xb)
        last_tk = cc_pending[-1][1]
        for dbuf_, tk in cc_pending:
            dbuf_.set_writer(last_tk)
        del cc_pending[:]

        stage('D', xb_t, xb)
        load_lnp([(ln_g_d, (l * 3 + 1) * D), (ln_b_d, (l * 3 + 1) * D), (ln_g_d, (l * 3 + 2) * D), (ln_b_d, (l * 3 + 2) * D)])
        O1, O2, L1, L2 = ps[0], ps[1], ps[2], ps[3]
        kvglob = [0]
        ptglob = [0]
        def p2_xa(n):
            P.dma("sync", xa_t, dap(xres, n * 512 * D, [(D, 128), (128 * D, 4), (1, D)]), reads=[d_xres[n]], writes=[xa])

        def p2_attn(n):
            cur_n[0] = n
            P.dma("sync", QT_t, dap(q_s, n * 512, [(T, 128), (128 * T, 8), (1, 512)]), reads=[d_q[n]], writes=[bB])
            def head_epilogue(h):
                e0, e1, e2, e3 = s2[0], s2[1], s2[2], s2[3]
                split_sum(L1, onesb, L1.ap, [L1], 0)
                hi2, lo2 = hl[2], hl[3]
                P.op("scalar", [act(hi2.ap[:, 0:LS], L2s_t[:, 0:LS], AF.Copy), act(hi2.ap[:, LS:512], L2.ap[:, LS:512], AF.Copy)],
                     reads=[L2s, L2], writes=[hi2])
                P.op("vector", [tt(lo2.ap[:, 0:LS], L2s_t[:, 0:LS], hi2.ap[:, 0:LS], ALU.subtract),
                                tt(lo2.ap[:, LS:512], L2.ap[:, LS:512], hi2.ap[:, LS:512], ALU.subtract)],
                     reads=[L2s, L2, hi2], writes=[lo2])
                P.op("tensor", [mm(L2.ap, onesb, hi2.ap, True, False), mm(L2.ap, onesb, lo2.ap, False, True)],
                     reads=[cb, hi2, lo2], writes=[L2])
                P.op("vector", [lambda e: e.reciprocal(out=e0.ap, in_=L1.ap), lambda e: e.reciprocal(out=e1.ap, in_=L2.ap)],
                     reads=[L1, L2], writes=[e0, e1])
                P.op("vector", [tt(e2.ap, O1.ap, e0.ap, ALU.mult), tt(e3.ap, O2.ap, e1.ap, ALU.mult)],
                     reads=[O1, O2, e0, e1], writes=[e2, e3])
                P.op("vector", stt(e2.ap, e3.ap, neglam, e2.ap, ALU.mult, ALU.add), reads=[e2, e3, small], writes=[e2])
                P.op("scalar", act(e3.ap, e2.ap, AF.Square), reads=[e2], writes=[e3])
                split_sum(L1, onesb, e3.ap, [e3], 0)
                P.op("scalar", act(e3.ap, L1.ap, AF.Sqrt, bias=small_t[:, 0:1], scale=1.0 / 128.0), reads=[L1, small], writes=[e3])
                P.op("vector", lambda e: e.reciprocal(out=e3.ap, in_=e3.ap), reads=[e3], writes=[e3])
                kw = dict(writes=[bA]) if h == 0 else dict(accum=[bA])
                P.op("vector", stt(bA_t[:, h, :], e2.ap, gcoef, e3.ap, ALU.mult, ALU.mult), reads=[e2, e3, small], **kw)
            chunks = []
            steps = []
            for h in range(8):
                cl = [(r, tl, None) for tl in range(n) for r in range(4)] + [(r, n, r) for r in range(4)]
                for ci, (r, tl, mt) in enumerate(cl):
                    chunks.append((h, r, tl))
                    for a in range(4):
                        steps.append(dict(h=h, c=len(chunks) - 1, a=a, qlo=0 if mt is None else a * 128, mt=mt,
                                          first=(ci == 0 and a == 0), last=(ci == len(cl) - 1 and a == 3)))
            kvbase = kvglob[0]
            kvglob[0] += len(chunks)
            issued = [0]

            def ensure_loaded(upto):
                while issued[0] <= min(upto, len(chunks) - 1):
                    c = issued[0]
                    h, r, tl = chunks[c]
                    i = (kvbase + c) % NKV
                    P.dma("sync", kv_t[i][:, 0:512], dap(kT_g[tl], (r * 1024 + h * 128) * 512, [(512, 128), (1, 512)]),
                          reads=[d_kg[tl]], writes=[kv[i]])
                    P.dma("sync", kv_t[i][:, 512:1024], dap(v_g[tl], (r * 1024 + h * 128) * 512, [(512, 128), (1, 512)]),
                          reads=[d_vg[tl]], writes=[kv[i]], nowait_dst=True)
                    issued[0] += 1

            def emit_qk(si):
                sp = steps[si]
                if sp["a"] == 0:
                    ensure_loaded(sp["c"] + 2)
                i = (kvbase + sp["c"]) % NKV
                a, qlo, h = sp["a"], sp["qlo"], sp["h"]
                SS = pp_t[2 + si % 2]
                P.op("tensor", [mm(SS[:, 0, qlo:512], kv_t[i][0:64, a * 128:(a + 1) * 128], QT_t[0:64, h, qlo:512], True, True),
                                mm(SS[:, 1, qlo:512], kv_t[i][64:128, a * 128:(a + 1) * 128], QT_t[64:128, h, qlo:512], True, True)],
                     reads=[kv[i], bB], writes=[ps[4 + (si % 2) * 2], ps[5 + (si % 2) * 2]])

            def emit_exp(si):
                sp = steps[si]
                i = (kvbase + sp["c"]) % NKV
                a, qlo, h, mt = sp["a"], sp["qlo"], sp["h"], sp["mt"]
                SS = pp_t[2 + si % 2]
                S1, S2 = ps[4 + (si % 2) * 2], ps[5 + (si % 2) * 2]
                pti = (ptglob[0] + si) % NPT
                ptb, ptt = pt[pti], pt_t[pti]
                P.op("scalar", act(ptt[:, :, qlo:512], SS[:, :, qlo:512], AF.Exp, scale=0.125), reads=[S1, S2], writes=[ptb])
                if mt is not None:
                    mk2 = bass.AP(maskb_t.tensor, mt * 128, [[512, 128], [0, 2], [1, 128]])
                    P.op("vector", tt(ptt[:, :, qlo:qlo + 128], ptt[:, :, qlo:qlo + 128], mk2, ALU.mult),
                         reads=[ptb, maskb], writes=[ptb])

            def emit_pv(si):
                sp = steps[si]
                i = (kvbase + sp["c"]) % NKV
                a, qlo, h, mt = sp["a"], sp["qlo"], sp["h"], sp["mt"]
                pti = (ptglob[0] + si) % NPT
                ptb, ptt = pt[pti], pt_t[pti]
                vsl_ = kv_t[i][:, 512 + a * 128:512 + (a + 1) * 128]
                LL = pp_t[1]
                if sp["first"]:
                    P.op("tensor", [mm(O1.ap[:, qlo:512], vsl_, ptt[:, 0, qlo:512], True, False),
                                    mm(O2.ap[:, qlo:512], vsl_, ptt[:, 1, qlo:512], True, False)],
                         reads=[kv[i], ptb], writes=[O1, O2])
                    P.op("vector", [lambda e, ptt=ptt: e.tensor_copy(out=L1.ap, in_=ptt[:, 0, :]),
                                    lambda e, ptt=ptt: e.tensor_copy(out=L2.ap[:, LS:512], in_=ptt[:, 1, LS:512])],
                         reads=[ptb], writes=[L1, L2])
                    P.op("gpsimd", lambda e, ptt=ptt: e.tensor_copy(out=L2s_t[:, 0:LS], in_=ptt[:, 1, 0:LS]), reads=[ptb], writes=[L2s])
                else:
                    P.op("tensor", [mm(O1.ap[:, qlo:512], vsl_, ptt[:, 0, qlo:512], False, False),
                                    mm(O2.ap[:, qlo:512], vsl_, ptt[:, 1, qlo:512], False, False)],
                         reads=[kv[i], ptb], accum=[O1, O2])
                    dlo = max(qlo, LS)
                    P.op("vector", [tt(L1.ap[:, qlo:512], L1.ap[:, qlo:512], ptt[:, 0, qlo:512], ALU.add),
                                    tt(L2.ap[:, dlo:512], L2.ap[:, dlo:512], ptt[:, 1, dlo:512], ALU.add)],
                         reads=[ptb, L1, L2], writes=[L1, L2])
                    if qlo < LS:
                        P.op("gpsimd", tt(L2s_t[:, qlo:LS], L2s_t[:, qlo:LS], ptt[:, 1, qlo:LS], ALU.add),
                             reads=[ptb, L2s], writes=[L2s])
                if sp["last"]:
                    head_epilogue(h)

            emit_qk(0)
            emit_qk(1)
            for si in range(len(steps)):
                emit_exp(si)
                if si + 2 < len(steps):
                    emit_qk(si + 2)
                emit_pv(si)
            ptglob[0] += len(steps)

        def p2_m2(n):
            cur_n[0] = n
            if DEBUG_DUMP and n == 0:
                P.dma("sync", dump_hb.ap(), bA_t[:, :, :].rearrange("p a b -> p (a b)"), reads=[bA], sem_of=bA, writes=[d_dump], nowait_dst=True)
            stage('E', xb_t, xb)
            for hf in range(2):
                wbb, wb2 = wload(l, wbrb, (l * 2 + 1) * D * D + hf * 512, D, 8, 512)
                for j in range(4):
                    c = hf * 4 + j
                    pm = next_ps()
                    P.op("tensor", [mm(pm.ap, wb2[:, k, j * 128:(j + 1) * 128], bA_t[:, k, :], k == 0, k == 7) for k in range(8)],
                         reads=[wbb, bA], writes=[pm])
                    gsl = s2[(2 * c) % 6]
                    msl = s2[(2 * c + 1) % 6]
                    P.dma("sync", gsl.ap, dap(gb_s, c * 128 * T + n * 512, [(T, 128), (1, 512)]), reads=[d_gb[n]], writes=[gsl])
                    P.dma("sync", msl.ap, dap(mag_s, c * 128 * T + n * 512, [(T, 128), (1, 512)]), reads=[d_mag[n]], writes=[msl])
                    P.op("vector", tt(gsl.ap, gsl.ap, pm.ap, ALU.mult), reads=[gsl, pm], writes=[gsl])
                    kw = dict(writes=[bC]) if c == 0 else dict(accum=[bC])
                    P.op("gpsimd", tt(bC_t[:, c, :], gsl.ap, msl.ap, ALU.add), reads=[gsl, msl], **kw)
            wo0b, wo0 = wload(l, wob, l * D * D + 0, D, 8, 512)
            wo1b, wo1 = wload(l, wob, l * D * D + 512, D, 8, 512)
            for s in range(4):
                rb = r2[s % 2]
                for hh, (wb_, wv) in enumerate([(wo0b, wo0), (wo1b, wo1)]):
                    pb = next_ps()
                    P.op("tensor", [mm(pb.ap, bC_t[:, k, s * 128:(s + 1) * 128], wv[:, k, :], k == 0, k == 7) for k in range(8)],
                         reads=[wb_, bC], writes=[pb])
                    kw = dict(writes=[rb]) if hh == 0 else dict(accum=[rb])
                    P.op("vector", stt(rb.ap[:, hh * 512:(hh + 1) * 512], xa_t[:, s, hh * 512:(hh + 1) * 512], ALPHA, pb.ap,
                                       ALU.mult, ALU.add), reads=[xa, pb], **kw)
                layernorm(rb, rb.ap, xb, xb_t[:, s, :], 0, 1, small_t[:, 0:1], dst_first=(s == 0))

        def p2_ffn(n):
            transposes_to_xT(xb_t, xb, 0)
            ffn(l, 1, 0, xb, xb_t, xb, xb_t, 2, 3)
            if l == L - 1:
                P.dma("gpsimd", dap(out_d, n * 512 * D, [(D, 128), (128 * D, 4), (1, D)]), xb_t, reads=[xb],
                      sem_of=xb, writes=[d_out], nowait_dst=True)
            else:
                P.dma("gpsimd", dap(xres, n * 512 * D, [(D, 128), (128 * D, 4), (1, D)]), xb_t, reads=[xb], sem_of=xb, writes=[d_xres[n]])

        p2_xa(0)
        p2_attn(0)
        p2_m2(0)
        for n in range(1, NT):
            p2_xa(n)
            p2_attn(n)
            p2_ffn(n - 1)
            p2_m2(n)
        p2_ffn(NT - 1)

    try:
        for l in range(L):
            emit_layer(l)
    except _Stop:
        pass
    P.wait("gpsimd", d_out.r_waits() + d_dump.r_waits())
    P.op("gpsimd", lambda e: e.memset(small_t[:, 10:11], 0.0), writes=[])

    with nc.Block() as block:
        @block.sync
        def _(e):
            P.replay("sync", e)

        @block.scalar
        def _(e):
            P.replay("scalar", e)

        @block.vector
        def _(e):
            P.replay("vector", e)

        @block.gpsimd
        def _(e):
            P.replay("gpsimd", e)

        @block.tensor
        def _(e):
            P.replay("tensor", e)
    return nc


def _lambda_init(layer):
    import math
    return 0.8 - 0.6 * math.exp(-0.3 * layer)


def _core_tables(j, NT, seq):
    T = NT * 512
    i = np.arange(T)
    pos = ((4 * (i // 128) + j) * 128 + i % 128).astype(np.float32)
    inv_freq = (np.float32(10000.0) ** (-np.arange(0, 64, 2, dtype=np.float32) / np.float32(64))).astype(np.float32)
    ang = pos[None, :] * inv_freq[:, None]
    cos = np.cos(ang).astype(np.float32)
    sin = np.sin(ang).astype(np.float32)
    idx = np.arange(128) % 32
    return np.ascontiguousarray(cos[idx]), np.ascontiguousarray(sin[idx])


def _consts():
    c = np.zeros((128, 3, 128), np.float32)
    c[:, 0, :] = np.eye(128, dtype=np.float32)
    for p in range(128):
        if p % 64 < 32:
            c[p + 32, 1, p] = -1.0
        else:
            c[p - 32, 1, p] = 1.0
    c[:, 2, :] = 1.0
    return c


def _masks(j):
    m = np.zeros((128, 4, 128), np.float32)
    for t in range(4):
        if t < j:
            m[:, t, :] = 1.0
        elif t == j:
            m[:, t, :] = np.triu(np.ones((128, 128), np.float32))
    return m


_PROG_CACHE = {}


def run_layers(x, params, layers, NT):
    L = len(layers)
    key = (L, NT)
    if key not in _PROG_CACHE:
        _PROG_CACHE[key] = build_program(L, NT)
    nc = _PROG_CACHE[key]
    S = 4 * NT * 512
    NBLK = S // 128
    consts = _consts()
    linit = np.array([[_lambda_init(l), 1.0 - _lambda_init(l)] for l in layers], np.float32)
    sl = {k: np.ascontiguousarray(v[list(layers)]) for k, v in params.items()}
    in_maps = []
    for c in range(8):
        b, j = c // 4, c % 4
        xc = np.ascontiguousarray(x[b].reshape(NBLK, 128, D)[j::4].reshape(NT * 512, D))
        cosT, sinT = _core_tables(j, NT, S)
        m = dict(sl)
        m.update({"x": xc, "cosT": cosT, "sinT": sinT, "masks": _masks(j), "consts": consts, "linit": linit})
        in_maps.append(m)
    res = run_bass_kernel_spmd(nc, in_maps, core_ids=list(range(8)))
    LAST_RES[0] = res
    out = np.empty((2, S, D), np.float32)
    for c in range(8):
        b, j = c // 4, c % 4
        out[b].reshape(NBLK, 128, D)[j::4] = res.results[c]["out"].reshape(NT * 4, 128, D)
    return out


FUSED = True


def kernel(x, w_in, gate_b, sgu_ln_g, sgu_ln_b, sgu_w, sgu_b, lam, diff_ln_g, w_branch, w_out,
           ffn_w1, ffn_w3, ffn_w2, ln_g, ln_b):
    params = dict(w_in=w_in, gate_b=gate_b, sgu_ln_g=sgu_ln_g, sgu_ln_b=sgu_ln_b, sgu_w=sgu_w, sgu_b=sgu_b,
                  lam=lam, diff_ln_g=diff_ln_g, w_branch=w_branch, w_out=w_out, ffn_w1=ffn_w1, ffn_w3=ffn_w3,
                  ffn_w2=ffn_w2, ln_g=ln_g, ln_b=ln_b)
    params = {k: np.asarray(v, np.float32) for k, v in params.items()}
    x = np.asarray(x, np.float32)
    NT = x.shape[1] // 2048
    depth = params["w_in"].shape[0]
    if FUSED:
        return run_layers(x, params, list(range(depth)), NT)
    for l in range(depth):
        x = run_layers(x, params, [l], NT)
    return x
```
